# Optimizing a Trainium2 kernel written in Bass

```python
import math
import jax, jax.numpy as jnp
from jax import lax
import numpy as np

D_MODEL = 1024
BATCH = 4
SEQ = 8192
DEPTH = 4

CHUNK = 64
N_MIXERS = 4
N_A = len(range(0, DEPTH, N_MIXERS))
N_B = len(range(1, DEPTH, N_MIXERS))
N_C = len(range(2, DEPTH, N_MIXERS))
N_D = len(range(3, DEPTH, N_MIXERS))
EPS = 1e-6
Q_BLOCK = 128

MLA_HEADS = 16
MLA_NOPE = 64
MLA_ROPE = 32
MLA_QK = MLA_NOPE + MLA_ROPE
MLA_V = 64
MLA_Q_RANK = 384
MLA_KV_RANK = 256
ROPE_BASE = 10000.0

ML_HEADS = 4
ML_QK = D_MODEL // 8
ML_V = D_MODEL // 4
GATE_CAP = 15.0

GLA_HEADS = 4
GLA_K = D_MODEL // 8
GLA_V = D_MODEL // 4
GLA_GATE_RANK = 16
GLA_TAU = 16.0

CONV_WIDTH = 31

D_FF = 4 * D_MODEL

kernel_name = 'hybrid_streaming_encoder_block'


def rms_norm(x, g):
    xf = x.astype(jnp.float32)
    y = xf * lax.rsqrt(jnp.mean(xf * xf, axis=-1, keepdims=True) + EPS)
    return (y * g.astype(jnp.float32)).astype(x.dtype)


def layer_norm(x, g, b):
    xf = x.astype(jnp.float32)
    mu = jnp.mean(xf, axis=-1, keepdims=True)
    var = jnp.mean(jnp.square(xf - mu), axis=-1, keepdims=True)
    y = (xf - mu) * lax.rsqrt(var + EPS) * g.astype(jnp.float32) + b.astype(jnp.float32)
    return y.astype(x.dtype)


def apply_rope(x, pos):
    half = x.shape[-1] // 2
    inv_freq = ROPE_BASE ** (-jnp.arange(half, dtype=jnp.float32) / half)
    ang = pos.astype(jnp.float32)[:, None] * inv_freq[None, :]
    cos = jnp.cos(ang)[:, None, :]
    sin = jnp.sin(ang)[:, None, :]
    xf = x.astype(jnp.float32)
    x1, x2 = xf[..., :half], xf[..., half:]
    return jnp.concatenate([x1 * cos - x2 * sin, x2 * cos + x1 * sin], axis=-1).astype(x.dtype)


def _to_chunks(t, heads, dh):
    b, s = t.shape[0], t.shape[1]
    return t.astype(jnp.float32).reshape(b, s // CHUNK, CHUNK, heads, dh).transpose(1, 0, 3, 2, 4)


def _from_chunks(o):
    nc, b, h, l, dv = o.shape
    return o.transpose(1, 0, 3, 2, 4).reshape(b, nc * l, h, dv)


def chunk_causal_attention(q, k, v):
    b, s, h, dq = q.shape
    nqb = s // Q_BLOCK
    scale = dq ** -0.5
    key_chunk = jnp.arange(s) // CHUNK
    qb = q.reshape(b, nqb, Q_BLOCK, h, dq).transpose(1, 0, 2, 3, 4)

    def one_block(args):
        qi, bi = args
        q_chunk = (bi * Q_BLOCK + jnp.arange(Q_BLOCK)) // CHUNK
        sc = jnp.einsum('bqhd,bkhd->bhqk', qi, k, preferred_element_type=jnp.float32) * scale
        mask = key_chunk[None, :] <= q_chunk[:, None]
        p = jax.nn.softmax(jnp.where(mask, sc, -jnp.inf), axis=-1)
        return jnp.einsum('bhqk,bkhd->bqhd', p.astype(v.dtype), v)

    o = lax.map(one_block, (qb, jnp.arange(nqb)))
    return o.transpose(1, 0, 2, 3, 4).reshape(b, s, h, v.shape[-1])


def mla_mixer(h, w_dq, q_norm, w_uq, w_dkv, kv_norm, w_ukv, q_gain, k_gain, w_o):
    b, s, _ = h.shape
    pos = jnp.arange(s)
    cq = rms_norm(h @ w_dq, q_norm)
    q = (cq @ w_uq).reshape(b, s, MLA_HEADS, MLA_QK)
    dkv = h @ w_dkv
    ckv = rms_norm(dkv[..., :MLA_KV_RANK], kv_norm)
    k_rope = jnp.broadcast_to(dkv[..., None, MLA_KV_RANK:], (b, s, MLA_HEADS, MLA_ROPE))
    kv = (ckv @ w_ukv).reshape(b, s, MLA_HEADS, MLA_NOPE + MLA_V)
    k = jnp.concatenate([kv[..., :MLA_NOPE], k_rope], axis=-1)
    v = kv[..., MLA_NOPE:]
    q = rms_norm(q, q_gain)
    k = rms_norm(k, k_gain)
    q = jnp.concatenate([q[..., :MLA_NOPE], apply_rope(q[..., MLA_NOPE:], pos)], axis=-1)
    k = jnp.concatenate([k[..., :MLA_NOPE], apply_rope(k[..., MLA_NOPE:], pos)], axis=-1)
    o = chunk_causal_attention(q, k, v)
    return o.reshape(b, s, MLA_HEADS * MLA_V) @ w_o


def mlstm_mixer(h, w_in, w_if, b_if, head_norm, w_o):
    b, s, _ = h.shape
    hq, hv = ML_HEADS * ML_QK, ML_HEADS * ML_V
    q, k, v, o_pre = jnp.split(h @ w_in, [hq, 2 * hq, 2 * hq + hv], axis=-1)
    gates = (h @ w_if + b_if).astype(jnp.float32)
    gates = GATE_CAP * jnp.tanh(gates / GATE_CAP)
    log_i = _to_chunks(gates[..., :ML_HEADS], ML_HEADS, 1)[..., 0]
    log_f = _to_chunks(jax.nn.log_sigmoid(gates[..., ML_HEADS:]), ML_HEADS, 1)[..., 0]
    q = _to_chunks(q, ML_HEADS, ML_QK)
    k = _to_chunks(k, ML_HEADS, ML_QK) * ML_QK ** -0.5
    v = _to_chunks(v, ML_HEADS, ML_V)
    causal = jnp.tril(jnp.ones((CHUNK, CHUNK), dtype=bool))

    def step(carry, xs):
        c_st, n_st, m_st = carry
        qc, kc, vc, li, lf = xs
        bcum = jnp.cumsum(lf, axis=-1)
        inter = bcum + m_st[..., None]
        dmat = bcum[..., :, None] - bcum[..., None, :] + li[..., None, :]
        dmat = jnp.where(causal, dmat, -jnp.inf)
        m_t = jnp.maximum(inter, jnp.max(dmat, axis=-1))
        w_intra = jnp.exp(dmat - m_t[..., None])
        w_inter = jnp.exp(inter - m_t)
        qk = jnp.einsum('bhtd,bhsd->bhts', qc, kc) * w_intra
        num = jnp.einsum('bhts,bhsv->bhtv', qk, vc) + w_inter[..., None] * jnp.einsum('bhtd,bhvd->bhtv', qc, c_st)
        den = jnp.sum(qk, axis=-1) + w_inter * jnp.einsum('bhtd,bhd->bht', qc, n_st)
        hc = num / jnp.maximum(jnp.abs(den), jnp.exp(-m_t))[..., None]
        b_last = bcum[..., -1]
        decay_s = b_last[..., None] - bcum + li
        m_new = jnp.maximum(b_last + m_st, jnp.max(decay_s, axis=-1))
        ws = jnp.exp(decay_s - m_new[..., None])
        wc = jnp.exp(b_last + m_st - m_new)
        c_new = wc[..., None, None] * c_st + jnp.einsum('bhs,bhsv,bhsd->bhvd', ws, vc, kc)
        n_new = wc[..., None] * n_st + jnp.einsum('bhs,bhsd->bhd', ws, kc)
        return (c_new, n_new, m_new), hc

    init = (jnp.zeros((b, ML_HEADS, ML_V, ML_QK), jnp.float32),
            jnp.zeros((b, ML_HEADS, ML_QK), jnp.float32),
            jnp.zeros((b, ML_HEADS), jnp.float32))
    _, hs = lax.scan(step, init, (q, k, v, log_i, log_f))
    hs = rms_norm(_from_chunks(hs), head_norm.reshape(ML_HEADS, ML_V)).reshape(b, s, hv)
    out = jax.nn.sigmoid(o_pre.astype(jnp.float32)) * hs
    return out.astype(h.dtype) @ w_o


def gla_mixer(h, w_in, w_a1, w_a2, b_a, head_norm, w_o):
    b, s, _ = h.shape
    hk, hv = GLA_HEADS * GLA_K, GLA_HEADS * GLA_V
    q, k, v, r = jnp.split(h @ w_in, [hk, 2 * hk, 2 * hk + hv], axis=-1)
    log_a = jax.nn.log_sigmoid(((h @ w_a1) @ w_a2 + b_a).astype(jnp.float32)) / GLA_TAU
    q = _to_chunks(q, GLA_HEADS, GLA_K) * GLA_K ** -0.5
    k = _to_chunks(k, GLA_HEADS, GLA_K)
    v = _to_chunks(v, GLA_HEADS, GLA_V)
    log_a = _to_chunks(log_a, GLA_HEADS, GLA_K)
    causal = jnp.tril(jnp.ones((CHUNK, CHUNK), dtype=bool))

    def step(state, xs):
        qc, kc, vc, la = xs
        bcum = jnp.cumsum(la, axis=-2)
        diff = jnp.where(causal[:, :, None], bcum[..., :, None, :] - bcum[..., None, :, :], -jnp.inf)
        att = jnp.einsum('bhtk,bhtsk,bhsk->bhts', qc, jnp.exp(diff), kc)
        out = jnp.einsum('bhts,bhsv->bhtv', att, vc) + jnp.einsum('bhtk,bhkv->bhtv', qc * jnp.exp(bcum), state)
        b_last = bcum[..., -1:, :]
        new_state = (jnp.exp(b_last[..., 0, :])[..., None] * state
                     + jnp.einsum('bhsk,bhsv->bhkv', kc * jnp.exp(b_last - bcum), vc))
        return new_state, out

    init = jnp.zeros((b, GLA_HEADS, GLA_K, GLA_V), jnp.float32)
    _, o = lax.scan(step, init, (q, k, v, log_a))
    o = rms_norm(_from_chunks(o), head_norm.reshape(GLA_HEADS, GLA_V)).reshape(b, s, hv)
    o = o * jax.nn.silu(r.astype(jnp.float32))
    return o.astype(h.dtype) @ w_o


def conv_mixer(h, w_pw1, b_pw1, w_dw, b_dw, ln_g, ln_b, w_pw2, b_pw2):
    a, g = jnp.split(h @ w_pw1 + b_pw1, 2, axis=-1)
    u = a * jax.nn.sigmoid(g)
    u = lax.conv_general_dilated(u, w_dw[:, None, :].astype(u.dtype), window_strides=(1,),
                                 padding=[(CONV_WIDTH - 1, 0)],
                                 dimension_numbers=('NWC', 'WIO', 'NWC'),
                                 feature_group_count=D_MODEL) + b_dw
    u = jax.nn.silu(layer_norm(u, ln_g, ln_b))
    return u @ w_pw2 + b_pw2


def sqrelu_mlp(h, w1, w2):
    return jnp.square(jax.nn.relu(h @ w1)) @ w2


def _w(key, shape, fan_in):
    return jax.random.normal(key, shape, jnp.float32) * (fan_in ** -0.5)


def _gain(key, shape):
    return 1.0 + 0.02 * jax.random.normal(key, shape, jnp.float32)


def _bias(key, shape):
    return 0.02 * jax.random.normal(key, shape, jnp.float32)


def setup_inputs(seed: int = 0) -> dict:
    key = jax.random.key(seed)
    ks = list(jax.random.split(key, 40))
    D = D_MODEL
    x = jax.random.normal(ks[0], (BATCH, SEQ, D), jnp.float32)
    norm_mix = _gain(ks[1], (DEPTH, D))
    norm_ffn = _gain(ks[2], (DEPTH, D))
    mla_w_dq = _w(ks[3], (N_A, D, MLA_Q_RANK), D)
    mla_q_norm = _gain(ks[4], (N_A, MLA_Q_RANK))
    mla_w_uq = _w(ks[5], (N_A, MLA_Q_RANK, MLA_HEADS * MLA_QK), MLA_Q_RANK)
    mla_w_dkv = _w(ks[6], (N_A, D, MLA_KV_RANK + MLA_ROPE), D)
    mla_kv_norm = _gain(ks[7], (N_A, MLA_KV_RANK))
    mla_w_ukv = _w(ks[8], (N_A, MLA_KV_RANK, MLA_HEADS * (MLA_NOPE + MLA_V)), MLA_KV_RANK)
    mla_q_gain = _gain(ks[9], (N_A, MLA_QK))
    mla_k_gain = _gain(ks[10], (N_A, MLA_QK))
    mla_w_o = _w(ks[11], (N_A, MLA_HEADS * MLA_V, D), MLA_HEADS * MLA_V)
    mlstm_w_in = _w(ks[12], (N_B, D, 2 * ML_HEADS * ML_QK + 2 * ML_HEADS * ML_V), D)
    mlstm_w_if = _w(ks[13], (N_B, D, 2 * ML_HEADS), D)
    mlstm_b_if = jnp.concatenate([_bias(ks[14], (N_B, ML_HEADS)),
                                  3.0 + 0.5 * jax.random.normal(ks[15], (N_B, ML_HEADS), jnp.float32)], axis=-1)
    mlstm_head_norm = _gain(ks[16], (N_B, ML_HEADS * ML_V))
    mlstm_w_o = _w(ks[17], (N_B, ML_HEADS * ML_V, D), ML_HEADS * ML_V)
    gla_w_in = _w(ks[18], (N_C, D, 2 * GLA_HEADS * GLA_K + 2 * GLA_HEADS * GLA_V), D)
    gla_w_a1 = _w(ks[19], (N_C, D, GLA_GATE_RANK), D)
    gla_w_a2 = _w(ks[20], (N_C, GLA_GATE_RANK, GLA_HEADS * GLA_K), GLA_GATE_RANK)
    gla_b_a = _bias(ks[21], (N_C, GLA_HEADS * GLA_K))
    gla_head_norm = _gain(ks[22], (N_C, GLA_HEADS * GLA_V))
    gla_w_o = _w(ks[23], (N_C, GLA_HEADS * GLA_V, D), GLA_HEADS * GLA_V)
    conv_w_pw1 = _w(ks[24], (N_D, D, 2 * D), D)
    conv_b_pw1 = _bias(ks[25], (N_D, 2 * D))
    conv_w_dw = _w(ks[26], (N_D, CONV_WIDTH, D), CONV_WIDTH)
    conv_b_dw = _bias(ks[27], (N_D, D))
    conv_ln_g = _gain(ks[28], (N_D, D))
    conv_ln_b = _bias(ks[29], (N_D, D))
    conv_w_pw2 = _w(ks[30], (N_D, D, D), D)
    conv_b_pw2 = _bias(ks[31], (N_D, D))
    ffn_w1 = _w(ks[32], (DEPTH, D, D_FF), D)
    ffn_w2 = _w(ks[33], (DEPTH, D_FF, D), D_FF)
    return {'x': x, 'norm_mix': norm_mix, 'norm_ffn': norm_ffn,
            'mla_w_dq': mla_w_dq, 'mla_q_norm': mla_q_norm, 'mla_w_uq': mla_w_uq,
            'mla_w_dkv': mla_w_dkv, 'mla_kv_norm': mla_kv_norm, 'mla_w_ukv': mla_w_ukv,
            'mla_q_gain': mla_q_gain, 'mla_k_gain': mla_k_gain, 'mla_w_o': mla_w_o,
            'mlstm_w_in': mlstm_w_in, 'mlstm_w_if': mlstm_w_if, 'mlstm_b_if': mlstm_b_if,
            'mlstm_head_norm': mlstm_head_norm, 'mlstm_w_o': mlstm_w_o,
            'gla_w_in': gla_w_in, 'gla_w_a1': gla_w_a1, 'gla_w_a2': gla_w_a2, 'gla_b_a': gla_b_a,
            'gla_head_norm': gla_head_norm, 'gla_w_o': gla_w_o,
            'conv_w_pw1': conv_w_pw1, 'conv_b_pw1': conv_b_pw1, 'conv_w_dw': conv_w_dw,
            'conv_b_dw': conv_b_dw, 'conv_ln_g': conv_ln_g, 'conv_ln_b': conv_ln_b,
            'conv_w_pw2': conv_w_pw2, 'conv_b_pw2': conv_b_pw2,
            'ffn_w1': ffn_w1, 'ffn_w2': ffn_w2}


def reference(x, norm_mix, norm_ffn,
              mla_w_dq, mla_q_norm, mla_w_uq, mla_w_dkv, mla_kv_norm, mla_w_ukv,
              mla_q_gain, mla_k_gain, mla_w_o,
              mlstm_w_in, mlstm_w_if, mlstm_b_if, mlstm_head_norm, mlstm_w_o,
              gla_w_in, gla_w_a1, gla_w_a2, gla_b_a, gla_head_norm, gla_w_o,
              conv_w_pw1, conv_b_pw1, conv_w_dw, conv_b_dw, conv_ln_g, conv_ln_b,
              conv_w_pw2, conv_b_pw2,
              ffn_w1, ffn_w2):
    for i in range(DEPTH):
        kind = i % N_MIXERS
        j = i // N_MIXERS
        hn = rms_norm(x, norm_mix[i])
        if kind == 0:
            y = mla_mixer(hn, mla_w_dq[j], mla_q_norm[j], mla_w_uq[j], mla_w_dkv[j], mla_kv_norm[j],
                          mla_w_ukv[j], mla_q_gain[j], mla_k_gain[j], mla_w_o[j])
        elif kind == 1:
            y = mlstm_mixer(hn, mlstm_w_in[j], mlstm_w_if[j], mlstm_b_if[j], mlstm_head_norm[j], mlstm_w_o[j])
        elif kind == 2:
            y = gla_mixer(hn, gla_w_in[j], gla_w_a1[j], gla_w_a2[j], gla_b_a[j], gla_head_norm[j], gla_w_o[j])
        else:
            y = conv_mixer(hn, conv_w_pw1[j], conv_b_pw1[j], conv_w_dw[j], conv_b_dw[j],
                           conv_ln_g[j], conv_ln_b[j], conv_w_pw2[j], conv_b_pw2[j])
        x = x + y.astype(x.dtype)
        x = x + sqrelu_mlp(rms_norm(x, norm_ffn[i]), ffn_w1[i], ffn_w2[i]).astype(x.dtype)
    return x
```

```python
import math
import numpy as np
from contextlib import ExitStack
import concourse.bass as bass
import concourse.mybir as mybir
from concourse.bass_utils import run_bass_kernel_spmd

F32 = mybir.dt.float32
BF16 = mybir.dt.bfloat16
AF = mybir.ActivationFunctionType
ALU = mybir.AluOpType

S = 8192
D = 1024
EPS = 1e-6
COMPUTE = ('pe', 'act', 'dve', 'pool')
ALLENG = ('pe', 'act', 'dve', 'pool', 'sp')
DMA_POOL = 8


class Tile:
    __slots__ = ('ap', 'w', 'r')

    def __init__(self, ap):
        self.ap = ap
        self.w = None
        self.r = []

    def __getitem__(self, k):
        return V(self, self.ap[k])

    @property
    def v(self):
        return V(self, self.ap)


class V:
    __slots__ = ('t', 'ap')

    def __init__(self, t, ap):
        self.t = t
        self.ap = ap

    def __getitem__(self, k):
        return V(self.t, self.ap[k])

    def bc(self, shape):
        return V(self.t, self.ap.to_broadcast(shape))

    def re(self, pat, **kw):
        return V(self.t, self.ap.rearrange(pat, **kw))

    def ub(self, axis, shape):
        return V(self.t, self.ap.unsqueeze(axis).to_broadcast(shape))


def _ap(x):
    return x.ap if isinstance(x, V) else x


def _tl(*xs):
    return [x.t for x in xs if isinstance(x, V)]


class Prog:
    def __init__(self, nc):
        self.nc = nc
        self.ops = {e: [] for e in ALLENG}
        self.ndma = {e: 0 for e in ALLENG}
        self.lastc = {e: None for e in ALLENG}
        self.dmas = {e: [] for e in ALLENG}

    def op(self, eng, fn, reads=(), writes=(), dma=False):
        deps = set()
        for t in reads:
            if t.w is not None:
                deps.add(t.w)
        for t in writes:
            if t.w is not None:
                deps.add(t.w)
            deps.update(t.r)
        me = (eng, len(self.ops[eng]))
        rec = dict(fn=fn, deps=deps, dma=dma, inc=False, val=None, sem=None)
        if dma:
            rec['dj'] = self.ndma[eng]
            self.ndma[eng] += 1
            self.dmas[eng].append(me)
        else:
            self.lastc[eng] = me
        self.ops[eng].append(rec)
        for t in reads:
            t.r.append(me)
        for t in writes:
            t.w = me
            t.r = []
        return me

    def barrier(self):
        deps = set()
        for e in ALLENG:
            if self.lastc[e] is not None:
                deps.add(self.lastc[e])
            deps.update(self.dmas[e][-DMA_POOL:])
        for e in ALLENG:
            self.ops[e].append(dict(fn=None, deps=set(deps), dma=False, inc=False, val=None, sem=None))

    def emit(self, stack):
        nc = self.nc
        ops = self.ops
        for e in ALLENG:
            for o in ops[e]:
                nd = set()
                for (pe_, pi) in o['deps']:
                    p = ops[pe_][pi]
                    if not p['dma']:
                        if pe_ == e and e == 'pe' and not o['dma'] and o['fn'] is not None:
                            continue
                        p['inc'] = True
                    nd.add((pe_, pi))
                o['deps'] = nd
        sems = {e: stack.enter_context(nc.semaphore('s_' + e)) for e in COMPUTE}
        dsems = {}
        for e in ALLENG:
            if self.ndma[e] > 0:
                dsems[e] = [stack.enter_context(nc.semaphore('d_%s_%d' % (e, k))) for k in range(DMA_POOL)]
        for e in ALLENG:
            cnt = 0
            for o in ops[e]:
                if o['dma']:
                    j = o['dj']
                    o['sem'] = dsems[e][j % DMA_POOL]
                    o['val'] = 16 * (j // DMA_POOL + 1)
                elif o['inc']:
                    cnt += 1
                    o['sem'] = sems[e]
                    o['val'] = cnt
        block = stack.enter_context(nc.Block())

        def run(e, engobj):
            waited = {}
            for o in ops[e]:
                need = {}
                for (pe_, pi) in o['deps']:
                    p = ops[pe_][pi]
                    s = p['sem']
                    if waited.get(s.num, 0) >= p['val']:
                        continue
                    if s.num not in need or need[s.num][1] < p['val']:
                        need[s.num] = (s, p['val'])
                if o['dma'] and o['val'] > 16:
                    s = o['sem']
                    v = o['val'] - 16
                    if waited.get(s.num, 0) < v and (s.num not in need or need[s.num][1] < v):
                        need[s.num] = (s, v)
                for key, (s, v) in need.items():
                    engobj.wait_ge(s, v)
                    waited[key] = v
                if o['fn'] is None:
                    continue
                ins = o['fn'](engobj)
                if o['dma']:
                    ins.then_inc(o['sem'], 16)
                elif o['inc']:
                    ins.then_inc(o['sem'], 1)
            n = self.ndma[e]
            for k in range(min(n, DMA_POOL)):
                cntk = (n - 1 - k) // DMA_POOL + 1
                if waited.get(dsems[e][k].num, 0) < 16 * cntk:
                    engobj.wait_ge(dsems[e][k], 16 * cntk)

        @block.tensor
        def _(pe):
            run('pe', pe)

        @block.scalar
        def _(act):
            run('act', act)

        @block.vector
        def _(dve):
            run('dve', dve)

        @block.gpsimd
        def _(pool):
            run('pool', pool)

        @block.sync
        def _(sp):
            run('sp', sp)


LAYER_W = {
    0: ['mla_w_dq', 'mla_w_uq', 'mla_w_dkv', 'mla_w_ukv', 'mla_w_o'],
    1: ['mlstm_w_in', 'mlstm_w_if', 'mlstm_w_o'],
    2: ['gla_w_in', 'gla_w_a1', 'gla_w_o'],
    3: ['conv_w_pw1', 'conv_w_pw2'],
}
WSHAPE = {
    'mla_w_dq': (1024, 384), 'mla_w_uq': (384, 1536), 'mla_w_dkv': (1024, 288), 'mla_w_ukv': (256, 2048),
    'mla_w_o': (1024, 1024), 'mlstm_w_in': (1024, 3072), 'mlstm_w_if': (1024, 8), 'mlstm_w_o': (1024, 1024),
    'gla_w_in': (1024, 3072), 'gla_w_a1': (1024, 16), 'gla_w_o': (1024, 1024),
    'conv_w_pw1': (1024, 2048), 'conv_w_pw2': (1024, 1024),
}
LAYER_V = {
    0: [('mla_q_norm', 384), ('mla_kv_norm', 256)],
    1: [('mlstm_head_norm', 1024)],
    2: [('gla_head_norm', 1024)],
    3: [('conv_b_pw1', 2048), ('conv_b_dw', 1024), ('conv_ln_g', 1024), ('conv_ln_b', 1024), ('conv_b_pw2', 1024)],
}


class KB:
    def __init__(self, layers, skip_ffn=False):
        self.layers = layers
        self.skip_ffn = skip_ffn
        self.nc = bass.Bass("TRN2", target_bir_lowering=False)
        self.p = Prog(self.nc)
        self.gst = ExitStack()
        self.din = {}
        self.uid = 0

    def dram_in(self, name, shape, dt=F32):
        a = self.nc.dram_tensor(name, list(shape), dt, kind="ExternalInput").ap()
        self.din[name] = a
        return a

    def dram_scr(self, name, shape, dt):
        return self.nc.dram_tensor(name, list(shape), dt, kind="Internal").ap()

    def sb(self, st, shape, dt=F32, name=None):
        self.uid += 1
        return st.enter_context(self.nc.sbuf_tensor("%s_%d" % (name or 'sb', self.uid), list(shape), dt))[:]

    def T(self, st, shape, dt=F32, name=None):
        return Tile(self.sb(st, shape, dt, name))

    def mm(self, out, lhsT, rhs, start=True, stop=True):
        o, l, r = _ap(out), _ap(lhsT), _ap(rhs)
        self.p.op('pe', lambda e: e.matmul(o, l, r, start=start, stop=stop), _tl(lhsT, rhs), _tl(out))

    def act(self, out, in_, func, bias=None, scale=None, eng='act'):
        o, i = _ap(out), _ap(in_)
        kw = {}
        if bias is not None:
            kw['bias'] = _ap(bias)
        if scale is not None:
            kw['scale'] = _ap(scale)
        self.p.op('act', lambda e: e.activation(o, i, func, **kw), _tl(in_, bias, scale), _tl(out))

    def tt(self, eng, out, a, b, op):
        o, x, y = _ap(out), _ap(a), _ap(b)
        self.p.op(eng, lambda e: e.tensor_tensor(o, x, y, op), _tl(a, b), _tl(out))

    def stt(self, eng, out, in0, scalar, in1, op0, op1):
        o, x, s, y = _ap(out), _ap(in0), _ap(scalar), _ap(in1)
        self.p.op(eng, lambda e: e.scalar_tensor_tensor(o, x, s, y, op0, op1), _tl(in0, scalar, in1), _tl(out))

    def ts(self, eng, out, in0, s1, s2, op0, op1=None):
        o, x, a, b = _ap(out), _ap(in0), _ap(s1), _ap(s2)
        if op1 is None:
            self.p.op(eng, lambda e: e.tensor_scalar(o, x, a, None, op0), _tl(in0, s1), _tl(out))
        else:
            self.p.op(eng, lambda e: e.tensor_scalar(o, x, a, b, op0, op1), _tl(in0, s1, s2), _tl(out))

    def copy(self, eng, out, in_):
        o, i = _ap(out), _ap(in_)
        if eng == 'act':
            self.p.op('act', lambda e: e.activation(o, i, AF.Copy), _tl(in_), _tl(out))
        else:
            self.p.op(eng, lambda e: e.tensor_copy(o, i), _tl(in_), _tl(out))

    def memset(self, eng, out, val):
        o = _ap(out)
        self.p.op(eng, lambda e: e.memset(o, val), (), _tl(out))

    def recip(self, out, in_):
        o, i = _ap(out), _ap(in_)
        self.p.op('dve', lambda e: e.reciprocal(o, i), _tl(in_), _tl(out))

    def scan(self, out, d0, d1, init, op0, op1):
        o, a, b = _ap(out), _ap(d0), _ap(d1)
        self.p.op('dve', lambda e: e.tensor_tensor_scan(o, a, b, init, op0, op1), _tl(d0, d1), _tl(out))

    def dma(self, q, out, in_, **kw):
        o, i = _ap(out), _ap(in_)
        self.p.op(q, lambda e: e.dma_start(out=o, in_=i, **kw), _tl(in_), _tl(out), dma=True)

    def build(self):
        nc, p = self.nc, self.p
        layers = self.layers
        gst = self.gst
        xin = self.dram_in("xT", (D, S))
        yout = nc.dram_tensor("yT", [D, S], F32, kind="ExternalOutput").ap()
        nmix = self.dram_in("norm_mix", (128, 32))
        nffn = self.dram_in("norm_ffn", (128, 32))
        for L in layers:
            for n in LAYER_W[L]:
                self.dram_in(n, WSHAPE[n])
            for n, ln in LAYER_V[L]:
                self.dram_in(n, (128, ln // 128))
            self.dram_in("ffn_w1_%d" % L, (D, 4096))
            self.dram_in("ffn_w2_%d" % L, (4096, D))
        if 0 in layers:
            self.dram_in("mla_qg", (96, 16))
            self.dram_in("mla_kg", (96, 16))
            self.dram_in("rope_cos", (96, S))
            self.dram_in("rope_sin", (96, S))
            self.dram_in("rope_R", (96, 96))
        if 1 in layers:
            self.dram_in("mlstm_b_if", (4, 16))
            self.dram_in("sel4", (4, 512))
            self.dram_in("lmat", (128, 128))
            self.dram_in("ident", (128, 128))
            self.dram_in("mneg", (128, 128))
        if 2 in layers:
            self.dram_in("gla_w_a2b", (17, 512))
            self.dram_in("triN", (128, 128))
            self.dram_in("triU", (128, 128))
        if 1 in layers or 2 in layers:
            self.dram_in("bcmask", (128, 128))
        if 3 in layers:
            self.dram_in("conv_w_dw", (128, 8 * 31))
            self.dram_in("identc", (128, 128))

        self.ps = []
        self.ps2 = []
        for i in range(4):
            pa_ = gst.enter_context(nc.psum_tensor("psp%d" % i, [128, 1024], F32))[:]
            self.ps2.append(Tile(pa_))
            self.ps.append(Tile(pa_[:, 0:512]))
            self.ps.append(Tile(pa_[:, 512:1024]))
        self.ones_bf = self.T(gst, [128, 128], BF16, 'ones')
        self.memset('pool', self.ones_bf.v, 1.0)
        self.epsT = self.T(gst, [128, 1], F32, 'eps')
        self.memset('pool', self.epsT.v, EPS)
        self.gmix = self.T(gst, [128, 32], F32, 'gmix')
        self.gffn = self.T(gst, [128, 32], F32, 'gffn')
        self.dma('sp', self.gmix.v, nmix)
        self.dma('sp', self.gffn.v, nffn)

        self.wb = {}
        cast_list = []
        ffn_list = []
        for L in layers:
            for n in LAYER_W[L]:
                shp = WSHAPE[n]
                dst = self.dram_scr(n + "_bf", shp, BF16)
                self.wb[n] = dst
                cast_list.append((self.din[n], dst, shp))
            d1 = self.dram_scr("ffn_w1_%d_bf" % L, (8, 128, 8 * 512), BF16)
            d2 = self.dram_scr("ffn_w2_%d_bf" % L, (4, 128, 32 * 256), BF16)
            self.wb["ffn_w1_%d" % L] = d1
            self.wb["ffn_w2_%d" % L] = d2
            ffn_list.append((self.din["ffn_w1_%d" % L], d1, self.din["ffn_w2_%d" % L], d2))
        self.cast_phase(cast_list, ffn_list)
        p.barrier()

        xm = self.dram_scr("x_mid", (D, S), F32)
        xs = [self.dram_scr("x_s0", (D, S), F32), self.dram_scr("x_s1", (D, S), F32)]
        cur = xin
        for li, L in enumerate(layers):
            nxt = yout if li == len(layers) - 1 else xs[li % 2]
            if self.skip_ffn:
                xm = nxt
            if L == 0:
                self.mla_phase(cur, xm)
            elif L == 1:
                self.mlstm_phase(cur, xm)
            elif L == 2:
                self.gla_phase(cur, xm)
            else:
                self.conv_phase(cur, xm)
            p.barrier()
            if not self.skip_ffn:
                self.ffn_phase(L, xm, nxt)
                p.barrier()
            cur = nxt
        p.emit(gst)
        gst.close()
        return nc

    def cast_phase(self, items, ffn_items):
        CH = 8192
        with ExitStack() as st:
            src_t = [self.T(st, [128, CH], F32, 'cs') for _ in range(2)]
            dst_t = [self.T(st, [128, CH], BF16, 'cd') for _ in range(2)]
            i = 0
            engs = ['dve', 'pool', 'act']
            for (src, dst, shp) in items:
                K, N = shp
                M = K * N // 128
                sv = src.rearrange("(p a) n -> p (a n)", p=128)
                dv = dst.rearrange("(p a) n -> p (a n)", p=128)
                for c0 in range(0, M, CH):
                    w = min(CH, M - c0)
                    a, b = src_t[i % 2], dst_t[i % 2]
                    self.dma('sp', a[:, 0:w], sv[:, c0:c0 + w])
                    self.copy(engs[i % 3], b[:, 0:w], a[:, 0:w])
                    self.dma('pool', dv[:, c0:c0 + w], b[:, 0:w])
                    i += 1
            for (s1, d1, s2, d2) in ffn_items:
                s1v = s1.rearrange("(c p) f -> p c f", p=128)
                for fg in range(8):
                    a, b = src_t[i % 2], dst_t[i % 2]
                    self.dma('sp', a[:, 0:4096].re("p (c f) -> p c f", c=8), s1v[:, :, fg * 512:(fg + 1) * 512])
                    self.copy(engs[i % 3], b[:, 0:4096], a[:, 0:4096])
                    self.dma('pool', d1[fg], b[:, 0:4096])
                    i += 1
                s2v = s2.rearrange("(c p) d -> p c d", p=128)
                for dg in range(4):
                    a, b = src_t[i % 2], dst_t[i % 2]
                    av = a.v.re("p (c d) -> p c d", c=32)
                    for q in range(4):
                        self.dma('sp', av[:, q * 8:(q + 1) * 8, :], s2v[:, q * 8:(q + 1) * 8, dg * 256:(dg + 1) * 256])
                    self.copy(engs[i % 3], b.v, a.v)
                    self.dma('pool', d2[dg], b.v)
                    i += 1

    def rmsnorm(self, x_chunks, gcols, hn_chunks, sq_chunks, ps, std, rstd, n_feat, width, eng_alt=('dve',)):
        n = len(x_chunks)
        for c in range(n):
            self.act(sq_chunks[c], x_chunks[c], AF.Square)
        for c in range(n):
            self.mm(ps, self.ones_bf.v, sq_chunks[c], start=(c == 0), stop=(c == n - 1))
        self.act(std, ps, AF.Ln, bias=self.epsT.v, scale=1.0 / n_feat)
        self.act(rstd, std, AF.Exp, scale=-0.5)
        for c in range(n):
            self.stt(eng_alt[c % len(eng_alt)], hn_chunks[c], x_chunks[c], gcols[c], rstd, ALU.mult, ALU.mult)

    def ffn_phase(self, L, xin, xout):
        TT = 1024
        w1 = self.wb["ffn_w1_%d" % L]
        w2 = self.wb["ffn_w2_%d" % L]
        xv = xin.rearrange("(c p) t -> p c t", p=128)
        ps = self.ps
        with ExitStack() as st:
            xt = self.T(st, [128, 8, TT], F32, 'fx')
            hn = [[self.T(st, [128, 512], BF16, 'fhn') for _ in range(2)] for _ in range(8)]
            sq = [self.T(st, [128, 512], BF16, 'fsq') for _ in range(8)]
            std = self.T(st, [128, 512], F32, 'fstd')
            rstd = self.T(st, [128, 512], F32, 'frstd')
            a = [[self.T(st, [128, 512], BF16, 'fa') for _ in range(2)] for _ in range(32)]
            w1t = [self.T(st, [128, 8, 512], BF16, 'fw1') for _ in range(2)]
            w2t = [self.T(st, [128, 32, 256], BF16, 'fw2') for _ in range(2)]
            rl = [self.T(st, [128, 512], F32, 'frl') for _ in range(4)]
            xres = [self.T(st, [128, TT], F32, 'fxr') for _ in range(2)]
            ot = [self.T(st, [128, TT], F32, 'fo') for _ in range(2)]
            nw1 = 0
            nw2 = 0
            nr = 0
            for sti in range(S // TT):
                t0 = sti * TT
                self.dma('sp', xt.v, xv[:, :, t0:t0 + TT])
                for half in range(2):
                    xs_ = [xt[:, c, half * 512:(half + 1) * 512] for c in range(8)]
                    self.rmsnorm(xs_, [self.gffn[:, L * 8 + c:L * 8 + c + 1] for c in range(8)],
                                 [hn[c][half].v for c in range(8)], [sq[c].v for c in range(8)],
                                 ps[7].v, std.v, rstd.v, D, 512)
                for fg in range(8):
                    wt = w1t[nw1 % 2]
                    nw1 += 1
                    self.dma('sp', wt.v.re("p c f -> p (c f)"), w1[fg])
                    for fi in range(4):
                        f = fg * 4 + fi
                        pb = (f % 2) * 2
                        for k in range(8):
                            for half in range(2):
                                self.mm(ps[pb + half].v, wt[:, k, fi * 128:(fi + 1) * 128], hn[k][half].v,
                                        start=(k == 0), stop=(k == 7))
                        for half in range(2):
                            r = rl[nr % 4]
                            nr += 1
                            self.act(r.v, ps[pb + half].v, AF.Relu)
                            self.tt('pool' if half else 'dve', a[f][half].v, r.v, r.v, ALU.mult)
                for dg in range(4):
                    wt = w2t[nw2 % 2]
                    nw2 += 1
                    self.dma('sp', wt.v.re("p c d -> p (c d)"), w2[dg])
                    for dd in range(2):
                        d = dg * 2 + dd
                        xr = xres[d % 2]
                        o = ot[d % 2]
                        self.dma('sp', xr.v, xin[d * 128:(d + 1) * 128, t0:t0 + TT])
                        pb = 4 + (d % 2) * 2
                        for f in range(32):
                            for half in range(2):
                                self.mm(ps[pb + half].v, wt[:, f, dd * 128:(dd + 1) * 128], a[f][half].v,
                                        start=(f == 0), stop=(f == 31))
                        for half in range(2):
                            self.tt('dve', o[:, half * 512:(half + 1) * 512], ps[pb + half].v,
                                    xr[:, half * 512:(half + 1) * 512], ALU.add)
                        self.dma('pool', xout[d * 128:(d + 1) * 128, t0:t0 + TT], o.v)

    def load_w(self, st, name, kc, n, cols=None):
        t = self.T(st, [128, kc, n], BF16, 'w')
        src = self.wb[name].rearrange("(c p) n -> p c n", p=128)
        if cols is not None:
            src = src[:, :, cols[0]:cols[1]]
        self.dma('sp', t.v, src)
        return t

    def load_x_norm(self, L, xin, t0, xt, hn, sq, std, rstd, psb):
        xv = xin.rearrange("(c p) t -> p c t", p=128)
        self.dma('sp', xt.v, xv[:, :, t0:t0 + 512])
        self.rmsnorm([xt[:, c, :] for c in range(8)], [self.gmix[:, L * 8 + c:L * 8 + c + 1] for c in range(8)],
                     [hn[c].v for c in range(8)], [sq[c].v for c in range(8)], psb.v, std.v, rstd.v, D, 512)

    def out_proj(self, wo, og, xt, ot, xout, t0, bias=None, banks=(6, 7)):
        ps = self.ps
        for oc in range(8):
            pb = ps[banks[oc % 2]]
            for j in range(8):
                self.mm(pb.v, wo[:, j, oc * 128:(oc + 1) * 128], og[j], start=(j == 0), stop=(j == 7))
            o = ot[oc % 2]
            if bias is None:
                self.tt('dve', o.v, pb.v, xt[:, oc, :], ALU.add)
            else:
                self.stt('dve', o.v, pb.v, bias[:, oc:oc + 1], xt[:, oc, :], ALU.add, ALU.add)
            self.dma('pool', xout[oc * 128:(oc + 1) * 128, t0:t0 + 512], o.v)

    def conv_phase(self, xin, xout):
        L = 3
        ps = self.ps
        with ExitStack() as st:
            w1 = self.load_w(st, 'conv_w_pw1', 8, 2048)
            w2 = self.load_w(st, 'conv_w_pw2', 8, 1024)
            b1 = self.T(st, [128, 16], F32)
            bdw = self.T(st, [128, 8], F32)
            lg = self.T(st, [128, 8], F32)
            lb = self.T(st, [128, 8], F32)
            b2 = self.T(st, [128, 8], F32)
            wdw = self.T(st, [128, 8 * 31], F32)
            identf = self.T(st, [128, 128], F32)
            for t_, n in ((b1, 'conv_b_pw1'), (bdw, 'conv_b_dw'), (lg, 'conv_ln_g'), (lb, 'conv_ln_b'),
                          (b2, 'conv_b_pw2'), (wdw, 'conv_w_dw'), (identf, 'identc')):
                self.dma('sp', t_.v, self.din[n])
            Dg = [self.T(st, [128, 31, 128], BF16, 'dg') for _ in range(8)]
            for c in range(8):
                self.tt('dve', Dg[c].v, identf.v.ub(1, [128, 31, 128]),
                        wdw[:, c * 31:(c + 1) * 31].ub(2, [128, 31, 128]), ALU.mult)
            xt = self.T(st, [128, 8, 512], F32, 'cx')
            hn = [self.T(st, [128, 512], BF16) for _ in range(8)]
            sq = [self.T(st, [128, 512], BF16) for _ in range(8)]
            std = self.T(st, [128, 512], F32)
            rstd = self.T(st, [128, 512], F32)
            u = [self.T(st, [128, 542], BF16, 'cu') for _ in range(8)]
            vv = [self.T(st, [128, 512], F32, 'cv') for _ in range(8)]
            vb = [self.T(st, [128, 512], BF16) for _ in range(8)]
            sig = [self.T(st, [128, 512], F32) for _ in range(2)]
            mean = self.T(st, [128, 512], F32)
            m2 = self.T(st, [128, 512], F32)
            var = self.T(st, [128, 512], F32)
            z = [self.T(st, [128, 512], BF16) for _ in range(8)]
            ot = [self.T(st, [128, 512], F32) for _ in range(2)]
            for c in range(8):
                self.memset('pool', u[c][:, 0:30], 0.0)
            for ti in range(S // 512):
                t0 = ti * 512
                self.load_x_norm(L, xin, t0, xt, hn, sq, std, rstd, ps[5])
                for oc in range(8):
                    pa, pg = ps[(oc % 2) * 2], ps[(oc % 2) * 2 + 1]
                    for k in range(8):
                        self.mm(pa.v, w1[:, k, oc * 128:(oc + 1) * 128], hn[k].v, start=(k == 0), stop=(k == 7))
                    for k in range(8):
                        self.mm(pg.v, w1[:, k, 1024 + oc * 128:1024 + (oc + 1) * 128], hn[k].v,
                                start=(k == 0), stop=(k == 7))
                    sg = sig[oc % 2]
                    self.act(sg.v, pg.v, AF.Sigmoid, bias=b1[:, 8 + oc:9 + oc])
                    self.stt('dve', u[oc][:, 30:542], pa.v, b1[:, oc:oc + 1], sg.v, ALU.add, ALU.mult)
                for c in range(8):
                    pc = ps[c % 4]
                    for k in range(31):
                        self.mm(pc.v, Dg[c][:, k, :], u[c][:, k:k + 512], start=(k == 0), stop=(k == 30))
                    self.act(vv[c].v, pc.v, AF.Identity, bias=bdw[:, c:c + 1])
                    self.copy('pool', u[c][:, 0:30], u[c][:, 512:542])
                for c in range(8):
                    self.copy('act', vb[c].v, vv[c].v)
                    self.act(sq[c].v, vv[c].v, AF.Square)
                for c in range(8):
                    self.mm(ps[4].v, self.ones_bf.v, vb[c].v, start=(c == 0), stop=(c == 7))
                for c in range(8):
                    self.mm(ps[5].v, self.ones_bf.v, sq[c].v, start=(c == 0), stop=(c == 7))
                self.act(mean.v, ps[4].v, AF.Copy, scale=1.0 / D)
                self.tt('dve', m2.v, mean.v, mean.v, ALU.mult)
                self.stt('dve', var.v, ps[5].v, 1.0 / D, m2.v, ALU.mult, ALU.subtract)
                self.act(std.v, var.v, AF.Ln, bias=self.epsT.v)
                self.act(rstd.v, std.v, AF.Exp, scale=-0.5)
                for c in range(8):
                    eng = 'dve' if c % 2 == 0 else 'pool'
                    self.tt(eng, vv[c].v, vv[c].v, mean.v, ALU.subtract)
                for c in range(8):
                    eng = 'dve' if c % 2 == 0 else 'pool'
                    self.tt(eng, vv[c].v, vv[c].v, rstd.v, ALU.mult)
                for c in range(8):
                    self.act(z[c].v, vv[c].v, AF.Silu, bias=lb[:, c:c + 1], scale=lg[:, c:c + 1])
                self.out_proj(w2, [z[c].v for c in range(8)], xt, ot, xout, t0, bias=b2)

    nbn = 4
    nbo = 0

    def nb(self):
        self._nb = (getattr(self, '_nb', -1) + 1) % self.nbn
        return self.ps[self.nbo + self._nb]

    def head_finalize(self, st_tiles, O, win, hn, gain, gate_func, og, n_in_head):
        sq, std, rstd, rs, tmpo = st_tiles
        ps = self.ps
        for h in range(4):
            for vc in range(2):
                self.act(sq[h * 2 + vc].v, O[h][:, vc, :], AF.Square)
            for vc in range(2):
                self.mm(ps[5].v, self.ones_bf.v, sq[h * 2 + vc].v, start=(vc == 0), stop=(vc == 1))
            self.act(std.v, ps[5].v, AF.Ln, bias=self.epsT.v, scale=1.0 / 256)
            self.act(rstd.v, std.v, AF.Exp, scale=-0.5)
            for vc in range(2):
                j = h * 2 + vc
                pr = self.nb()
                for k in range(8):
                    self.mm(pr.v, win[:, k, 2048 + j * 128:2048 + (j + 1) * 128], hn[k].v, start=(k == 0), stop=(k == 7))
                self.act(rs[j % 2].v, pr.v, gate_func)
                self.stt('dve', tmpo[j % 2].v, O[h][:, vc, :], gain[:, j:j + 1], rstd.v, ALU.mult, ALU.mult)
                self.tt('pool', og[j].v, tmpo[j % 2].v, rs[j % 2].v, ALU.mult)

    def gla_phase(self, xin, xout):
        L = 2
        ps = self.ps
        with ExitStack() as st:
            win = self.load_w(st, 'gla_w_in', 8, 3072)
            wo = self.load_w(st, 'gla_w_o', 8, 1024)
            wa1 = self.load_w(st, 'gla_w_a1', 8, 16)
            wa2f = self.T(st, [17, 512], F32)
            wa2 = self.T(st, [17, 512], BF16)
            triN = self.T(st, [128, 128], F32)
            triU = self.T(st, [128, 128], F32)
            mask = self.T(st, [128, 128], F32)
            hgn = self.T(st, [128, 8], F32)
            onesc = self.T(st, [128, 1], F32)
            self.memset('pool', onesc.v, 1.0)
            for t_, n in ((wa2f, 'gla_w_a2b'), (triN, 'triN'), (triU, 'triU'), (mask, 'bcmask'), (hgn, 'gla_head_norm')):
                self.dma('sp', t_.v, self.din[n])
            self.copy('dve', wa2.v, wa2f.v)
            xt = self.T(st, [128, 8, 512], F32, 'gx')
            hn = [self.T(st, [128, 512], BF16) for _ in range(8)]
            sq = [self.T(st, [128, 512], BF16) for _ in range(8)]
            std = self.T(st, [128, 512], F32)
            rstd = self.T(st, [128, 512], F32)
            g1a = self.T(st, [17, 512], BF16)
            self.memset('pool', g1a.v, 1.0)
            lsp = [self.T(st, [128, 512], F32) for _ in range(4)]
            ez = self.T(st, [128, 512], F32)
            ep = [self.T(st, [128, 512], F32) for _ in range(2)]
            em = [self.T(st, [128, 512], F32) for _ in range(2)]
            eb = [self.T(st, [128, 8], F32) for _ in range(4)]
            qt = [self.T(st, [128, 512], BF16) for _ in range(4)]
            kt = [self.T(st, [128, 512], BF16) for _ in range(4)]
            vtm = [self.T(st, [128, 1024], BF16) for _ in range(4)]
            kd = [self.T(st, [128, 512], BF16) for _ in range(4)]
            erev = [self.T(st, [128, 512], F32) for _ in range(2)]
            attm = [self.T(st, [128, 4, 128], BF16) for _ in range(2)]
            Sst = [self.T(st, [128, 256], F32) for _ in range(4)]
            Sb = [self.T(st, [128, 256], BF16) for _ in range(4)]
            for h in range(4):
                self.memset('pool', Sst[h].v, 0.0)
                self.memset('pool', Sb[h].v, 0.0)
            O = [self.T(st, [128, 2, 512], F32) for _ in range(4)]
            rs = [self.T(st, [128, 512], F32) for _ in range(2)]
            tmpo = [self.T(st, [128, 512], F32) for _ in range(2)]
            og = [self.T(st, [128, 512], BF16) for _ in range(8)]
            ot = [self.T(st, [128, 512], F32) for _ in range(2)]
            poh = [Tile(ps[6 + i // 2].ap[:, (i % 2) * 256:(i % 2) * 256 + 256]) for i in range(4)]
            sc = 128 ** -0.5
            for ti in range(S // 512):
                t0 = ti * 512
                self.load_x_norm(L, xin, t0, xt, hn, sq, std, rstd, ps[5])
                pb = self.nb()
                for k in range(8):
                    self.mm(pb[0:16, :], wa1[:, k, :], hn[k].v, start=(k == 0), stop=(k == 7))
                self.copy('act', g1a[0:16, :], pb[0:16, :])
                for b in range(4):
                    pz = self.nb()
                    self.mm(pz.v, g1a[0:17, b * 128:(b + 1) * 128], wa2.v)
                    self.act(ez.v, pz.v, AF.Exp, scale=-1.0)
                    self.act(lsp[b].v, ez.v, AF.Ln, bias=onesc.v)
                for h in range(4):
                    pc = self.nb()
                    for b in range(4):
                        self.mm(pc[:, b * 128:(b + 1) * 128], lsp[b][:, h * 128:(h + 1) * 128], triN.v)
                    e_p, e_m = ep[h % 2], em[h % 2]
                    self.act(e_p.v, pc.v, AF.Exp)
                    self.act(e_m.v, pc.v, AF.Exp, scale=-1.0)
                    self.copy('pool', eb[h].v, e_p.v.re("p (c s) -> p c s", s=64)[:, :, 63])
                    pq = self.nb()
                    for k in range(8):
                        self.mm(pq.v, win[:, k, h * 128:(h + 1) * 128], hn[k].v, start=(k == 0), stop=(k == 7))
                    self.stt('dve', qt[h].v, pq.v, sc, e_p.v, ALU.mult, ALU.mult)
                    pk = self.nb()
                    for k in range(8):
                        self.mm(pk.v, win[:, k, 512 + h * 128:512 + (h + 1) * 128], hn[k].v, start=(k == 0), stop=(k == 7))
                    self.tt('dve', kt[h].v, pk.v, e_m.v, ALU.mult)
                for b in range(4):
                    bc = slice(b * 128, (b + 1) * 128)
                    for half in range(2):
                        pv = self.nb()
                        for k in range(8):
                            self.mm(pv.v, hn[k][:, bc], win[:, k, 1024 + half * 512:1024 + (half + 1) * 512],
                                    start=(k == 0), stop=(k == 7))
                        self.copy('act' if half else 'dve', vtm[b][:, half * 512:(half + 1) * 512], pv.v)
                    pk2 = self.nb()
                    for k in range(8):
                        self.mm(pk2.v, hn[k][:, bc], win[:, k, 512:1024], start=(k == 0), stop=(k == 7))
                    pr = self.nb()
                    self.mm(pr.v, triU.v, lsp[b].v)
                    er = erev[b % 2]
                    self.act(er.v, pr.v, AF.Exp)
                    self.tt('dve', kd[b].v, pk2.v, er.v, ALU.mult)
                for b in range(4):
                    bc = slice(b * 128, (b + 1) * 128)
                    pa = ps[4]
                    am = attm[b % 2]
                    for h in range(4):
                        self.mm(pa[:, h * 128:(h + 1) * 128], kt[h][:, bc], qt[h][:, bc])
                    self.tt('dve', am.v, pa.v.re("p (h t) -> p h t", h=4), mask.v.ub(1, [128, 4, 128]), ALU.mult)
                    for h in range(4):
                        po = poh[h]
                        for vc in range(2):
                            self.mm(po[:, vc * 128:(vc + 1) * 128], vtm[b][:, h * 256 + vc * 128:h * 256 + (vc + 1) * 128],
                                    am[:, h, :], start=(vc == 0 and h % 2 == 0), stop=False)
                    for X in range(2):
                        rows = slice(X * 64, (X + 1) * 64)
                        cols = slice(b * 128 + X * 64, b * 128 + (X + 1) * 64)
                        cl = b * 2 + X
                        for h in range(4):
                            po = poh[h]
                            for vc in range(2):
                                self.mm(po[:, vc * 128 + X * 64:vc * 128 + (X + 1) * 64], Sb[h][:, vc * 128:(vc + 1) * 128],
                                        qt[h][:, cols], start=False, stop=True)
                        pus = []
                        for h in range(4):
                            pu = self.nb()
                            pus.append(pu)
                            self.mm(pu[:, 0:256], kd[b][rows, h * 128:(h + 1) * 128], vtm[b][rows, h * 256:(h + 1) * 256])
                        for h in range(4):
                            self.stt('dve', Sst[h].v, Sst[h].v, eb[h][:, cl:cl + 1], pus[h][:, 0:256], ALU.mult, ALU.add)
                        for h in range(4):
                            self.copy('act', Sb[h].v, Sst[h].v)
                    for h in range(4):
                        self.copy('act', O[h][:, :, bc], poh[h].v.re("p (v t) -> p v t", v=2))
                self.head_finalize((sq, std, rstd, rs, tmpo), O, win, hn, hgn, AF.Silu, og, 256)
                self.out_proj(wo, [og[j].v for j in range(8)], xt, ot, xout, t0, banks=(4, 5))


    def mlstm_phase(self, xin, xout):
        L = 1
        ps = self.ps
        nc = self.nc
        gi_s = self.dram_scr("ml_gi", (4, S), F32)
        gf_s = self.dram_scr("ml_gf", (4, S), F32)
        em_s = self.dram_scr("ml_em", (4, S), F32)
        wa_s = self.dram_scr("ml_wa", (4, S), F32)
        wc_s = self.dram_scr("ml_wc", (4, 128), F32)
        with ExitStack() as st:
            wif = self.load_w(st, 'mlstm_w_if', 8, 8)
            bif = self.T(st, [4, 16], F32)
            bi15 = self.T(st, [4, 16], F32)
            onesc = self.T(st, [128, 1], F32)
            self.memset('pool', onesc.v, 1.0)
            self.dma('sp', bif.v, self.din['mlstm_b_if'])
            self.ts('dve', bi15.v, bif.v, 1.0 / 15.0, None, ALU.mult)
            xts = [self.T(st, [128, 8, 512], F32) for _ in range(2)]
            hn = [self.T(st, [128, 512], BF16) for _ in range(8)]
            sq = [self.T(st, [128, 512], BF16) for _ in range(8)]
            std = self.T(st, [128, 512], F32)
            rstd = self.T(st, [128, 512], F32)
            t1 = [self.T(st, [4, 512], F32) for _ in range(2)]
            t2 = [self.T(st, [4, 512], F32) for _ in range(2)]
            t3 = [self.T(st, [4, 512], F32) for _ in range(2)]
            li = [self.T(st, [4, 512], F32) for _ in range(2)]
            lf = [self.T(st, [4, 512], F32) for _ in range(2)]
            for ti in range(S // 512):
                t0 = ti * 512
                xt = xts[ti % 2]
                self.load_x_norm(L, xin, t0, xt, hn, sq, std, rstd, ps[5])
                pgi, pgf = self.nb(), self.nb()
                for k in range(8):
                    self.mm(pgi[0:4, :], wif[:, k, 0:4], hn[k].v, start=(k == 0), stop=(k == 7))
                for k in range(8):
                    self.mm(pgf[0:4, :], wif[:, k, 4:8], hn[k].v, start=(k == 0), stop=(k == 7))
                a1, a2, a3, l_i, l_f = t1[ti % 2], t2[ti % 2], t3[ti % 2], li[ti % 2], lf[ti % 2]
                self.act(a1.v, pgi[0:4, :], AF.Tanh, bias=bi15[:, 0:1], scale=1.0 / 15.0)
                self.ts('dve', l_i.v, a1.v, 15.0, None, ALU.mult)
                self.act(a2.v, pgf[0:4, :], AF.Tanh, bias=bi15[:, 1:2], scale=1.0 / 15.0)
                self.act(a3.v, a2.v, AF.Exp, scale=-15.0)
                self.act(a2.v, a3.v, AF.Ln, bias=onesc[0:4, :])
                self.ts('dve', l_f.v, a2.v, -1.0, None, ALU.mult)
                self.dma('sp', gi_s[:, t0:t0 + 512], l_i.v)
                self.dma('sp', gf_s[:, t0:t0 + 512], l_f.v)
        self.p.barrier()
        with ExitStack() as st:
            def t_(shape, dt=F32):
                return self.T(st, shape, dt)
            Li, Lf, onesr, Floc, Fg, a_, Aloc, Ap, tmpA, wa, emx, Ab = [t_([128, 256]) for _ in range(12)]
            lmat, ident, mneg, rb, rowv = [t_([128, 128]) for _ in range(5)]
            Gs, Apre = t_([128, 1]), t_([128, 1])
            Aend, Astart, wc = t_([128, 4]), t_([128, 4]), t_([128, 4])
            self.dma('sp', Li.v, gi_s.rearrange("h (s t) -> (h s) t", t=256))
            self.dma('sp', Lf.v, gf_s.rearrange("h (s t) -> (h s) t", t=256))
            self.dma('sp', lmat.v, self.din['lmat'])
            self.dma('sp', ident.v, self.din['ident'])
            self.dma('sp', mneg.v, self.din['mneg'])
            self.memset('pool', onesr.v, 1.0)
            self.scan(Floc.v, onesr.v, Lf.v, 0.0, ALU.mult, ALU.add)
            self.copy('dve', rb.v, Floc[:, 255:256].bc([128, 128]))
            pg = self.nb()
            self.mm(pg[:, 0:128], lmat.v, rb.v)
            self.copy('act', Gs.v, pg[:, 0:1])
            self.ts('dve', Fg.v, Floc.v, Gs[:, 0:1], None, ALU.add)
            self.tt('dve', a_.v, Li.v, Fg.v, ALU.subtract)
            self.ts('dve', Aloc.v, a_.v, 0.0, None, ALU.max)
            src, dst = Aloc, Ab
            d = 1
            while d < 256:
                self.tt('dve', dst[:, d:256], src[:, d:256], src[:, 0:256 - d], ALU.max)
                self.copy('dve', dst[:, 0:d], src[:, 0:d])
                src, dst = dst, src
                d *= 2
            Aloc = src
            self.copy('dve', rb.v, Aloc[:, 255:256].bc([128, 128]))
            pr = self.nb()
            self.mm(pr[:, 0:128], rb.v, ident.v)
            self.tt('dve', rowv.v, pr[:, 0:128], mneg.v, ALU.add)
            self.p.op('dve', (lambda o_, i_: (lambda e: e.tensor_reduce(o_, i_, mybir.AxisListType.X, ALU.max)))(Apre.ap, rowv.ap),
                      [rowv], [Apre])
            self.ts('dve', Apre.v, Apre.v, 0.0, None, ALU.max)
            self.ts('dve', Ap.v, Aloc.v, Apre[:, 0:1], None, ALU.max)
            self.copy('dve', Aend.v, Ap.v.re("p (j t) -> p j t", t=64)[:, :, 63])
            self.copy('dve', Astart[:, 0:1], Apre.v)
            self.copy('dve', Astart[:, 1:4], Aend[:, 0:3])
            self.tt('dve', wc.v, Astart.v, Aend.v, ALU.subtract)
            self.act(wc.v, wc.v, AF.Exp)
            self.dma('sp', wc_s.rearrange("h (s j) -> (h s) j", j=4), wc.v)
            self.tt('dve', tmpA.v.re("p (j t) -> p j t", t=64), a_.v.re("p (j t) -> p j t", t=64),
                    Aend.v.ub(2, [128, 4, 64]), ALU.subtract)
            self.act(wa.v, tmpA.v, AF.Exp)
            self.dma('pool', wa_s.rearrange("h (s t) -> (h s) t", t=256), wa.v)
            self.tt('dve', tmpA.v.re("p (j t) -> p j t", t=64), Fg.v.re("p (j t) -> p j t", t=64),
                    Aend.v.ub(2, [128, 4, 64]), ALU.add)
            self.act(emx.v, tmpA.v, AF.Exp)
            self.dma('pool', em_s.rearrange("h (s t) -> (h s) t", t=256), emx.v)
        self.p.barrier()
        with ExitStack() as st:
            win = self.load_w(st, 'mlstm_w_in', 8, 3072)
            wo = self.load_w(st, 'mlstm_w_o', 8, 1024)
            mask = self.T(st, [128, 128], F32)
            hgn = self.T(st, [128, 8], F32)
            sel4 = self.T(st, [4, 512], F32)
            ident = self.T(st, [128, 128], F32)
            for t_, n in ((mask, 'bcmask'), (hgn, 'mlstm_head_norm'), (sel4, 'sel4'), (ident, 'ident')):
                self.dma('sp', t_.v, self.din[n])
            xt = self.T(st, [128, 8, 512], F32)
            hn = [self.T(st, [128, 512], BF16) for _ in range(8)]
            sq = [self.T(st, [128, 512], BF16) for _ in range(8)]
            std = self.T(st, [128, 512], F32)
            rstd = self.T(st, [128, 512], F32)
            emT = self.T(st, [4, 512], F32)
            waT = self.T(st, [4, 512], F32)
            wcT = self.T(st, [4, 128], F32)
            watm = self.T(st, [128, 16], F32)
            wcb = self.T(st, [128, 4, 128], F32)
            embc = [self.T(st, [128, 512], F32) for _ in range(2)]
            qs = [self.T(st, [128, 512], BF16) for _ in range(4)]
            kt = [self.T(st, [128, 512], BF16) for _ in range(4)]
            vaug = [self.T(st, [128, 4, 384], BF16) for _ in range(4)]
            for b in range(4):
                self.memset('pool', vaug[b].v, 1.0)
            kw = [self.T(st, [128, 4, 128], BF16) for _ in range(4)]
            qkw = [self.T(st, [128, 4, 128], BF16) for _ in range(2)]
            Sst = [self.T(st, [128, 384], F32) for _ in range(4)]
            Sb = [self.T(st, [128, 384], BF16) for _ in range(4)]
            for h in range(4):
                self.memset('pool', Sst[h].v, 0.0)
            dn = [self.T(st, [128, 128], F32) for _ in range(2)]
            rdn = [self.T(st, [128, 128], F32) for _ in range(2)]
            O = [self.T(st, [128, 2, 512], F32) for _ in range(4)]
            rs = [self.T(st, [128, 512], F32) for _ in range(2)]
            tmpo = [self.T(st, [128, 512], F32) for _ in range(2)]
            og = [self.T(st, [128, 512], BF16) for _ in range(8)]
            ot = [self.T(st, [128, 512], F32) for _ in range(2)]
            sc = 128 ** -0.5
            self.dma('sp', wcT.v, wc_s)
            for h in range(4):
                pwc = self.nb()
                self.mm(pwc[:, 0:128], sel4[0:4, h * 128:(h + 1) * 128], wcT.v)
                self.copy('act', wcb[:, h, :], pwc[:, 0:128])
            for ti in range(S // 512):
                t0 = ti * 512
                self.load_x_norm(L, xin, t0, xt, hn, sq, std, rstd, ps[5])
                self.dma('sp', emT.v, em_s[:, t0:t0 + 512])
                self.dma('sp', waT.v, wa_s[:, t0:t0 + 512])
                pw = self.nb()
                for b in range(4):
                    self.mm(pw[:, b * 128:(b + 1) * 128], waT[0:4, b * 128:(b + 1) * 128], ident[0:4, 0:128])
                self.ts('dve', watm.v.re("p (b h) -> p b h", h=4), pw.v.re("p (b n) -> p b n", n=128)[:, :, 0:4], sc, None, ALU.mult)
                for h in range(4):
                    pe_ = self.nb()
                    self.mm(pe_.v, sel4[0:4, h * 128:(h + 1) * 128], emT.v)
                    eb_ = embc[h % 2]
                    self.copy('act', eb_.v, pe_.v)
                    pq = self.nb()
                    for k in range(8):
                        self.mm(pq.v, win[:, k, h * 128:(h + 1) * 128], hn[k].v, start=(k == 0), stop=(k == 7))
                    self.tt('dve', qs[h].v, pq.v, eb_.v, ALU.mult)
                    pk = self.nb()
                    for k in range(8):
                        self.mm(pk.v, win[:, k, 512 + h * 128:512 + (h + 1) * 128], hn[k].v, start=(k == 0), stop=(k == 7))
                    self.copy('act', kt[h].v, pk.v)
                for b in range(4):
                    bc = slice(b * 128, (b + 1) * 128)
                    for half in range(2):
                        pv = self.nb()
                        for k in range(8):
                            self.mm(pv.v, hn[k][:, bc], win[:, k, 1024 + half * 512:1024 + (half + 1) * 512],
                                    start=(k == 0), stop=(k == 7))
                        self.copy('act' if half else 'dve', vaug[b][:, 2 * half:2 * half + 2, 0:256],
                                  pv.v.re("p (h v) -> p h v", h=2))
                    pk2 = self.nb()
                    for k in range(8):
                        self.mm(pk2.v, hn[k][:, bc], win[:, k, 512:1024], start=(k == 0), stop=(k == 7))
                    self.tt('dve', kw[b].v, pk2.v.re("p (h d) -> p h d", h=4),
                            watm[:, b * 4:(b + 1) * 4].ub(2, [128, 4, 128]), ALU.mult)
                for b in range(4):
                    bc = slice(b * 128, (b + 1) * 128)
                    pa = self.nb()
                    qk_ = qkw[b % 2]
                    for h in range(4):
                        self.mm(pa[:, h * 128:(h + 1) * 128], kt[h][:, bc], qs[h][:, bc])
                    for h in range(4):
                        self.stt('dve', qk_[:, h, :], pa[:, h * 128:(h + 1) * 128], watm[:, b * 4 + h:b * 4 + h + 1],
                                 mask.v, ALU.mult, ALU.mult)
                    for h in range(4):
                        po = ps[4 + h]
                        for j in range(3):
                            self.mm(po[:, j * 128:(j + 1) * 128], vaug[b][:, h, j * 128:(j + 1) * 128], qk_[:, h, :],
                                    start=(j == 0), stop=False)
                    for X in range(2):
                        rows = slice(X * 64, (X + 1) * 64)
                        cols = slice(b * 128 + X * 64, b * 128 + (X + 1) * 64)
                        cl = b * 2 + X
                        for h in range(4):
                            self.act(Sb[h].v, Sst[h].v, AF.Copy, scale=wcb[:, h, ti * 8 + cl:ti * 8 + cl + 1])
                        for h in range(4):
                            po = ps[4 + h]
                            for j in range(3):
                                self.mm(po[:, j * 128 + X * 64:j * 128 + (X + 1) * 64], Sb[h][:, j * 128:(j + 1) * 128],
                                        qs[h][:, cols], start=False, stop=True)
                        pus = []
                        for h in range(4):
                            pu = self.nb()
                            pus.append(pu)
                            self.mm(pu[:, 0:384], kw[b][rows, h, :], vaug[b][rows, h, :])
                        for h in range(4):
                            self.stt('dve', Sst[h].v, Sst[h].v, wcb[:, h, ti * 8 + cl:ti * 8 + cl + 1], pus[h][:, 0:384],
                                     ALU.mult, ALU.add)
                    for h in range(4):
                        po = ps[4 + h]
                        d_, r_ = dn[h % 2], rdn[h % 2]
                        self.act(d_.v, po[:, 256:384], AF.Abs)
                        self.ts('dve', d_.v, d_.v, 1.0, None, ALU.max)
                        self.act(r_.v, d_.v, AF.Ln)
                        self.act(r_.v, r_.v, AF.Exp, scale=-1.0)
                        self.tt('dve', O[h][:, :, bc], po[:, 0:256].re("p (v t) -> p v t", v=2),
                                r_.v.ub(1, [128, 2, 128]), ALU.mult)
                self.head_finalize((sq, std, rstd, rs, tmpo), O, win, hn, hgn, AF.Sigmoid, og, 256)
                self.out_proj(wo, [og[j].v for j in range(8)], xt, ot, xout, t0, banks=(4, 5))


    def normrope(self, tl, x_pre, cos_t, sin_t, g, Rg, bias_ap, scale, out):
        sq96, xb, std96, rstd96, t1, t2 = tl
        self.act(sq96.v, x_pre.v, AF.Square)
        pss = self.nb()
        self.mm(pss[0:96, :], self.ones_bf[0:96, 0:96], sq96.v)
        self.act(std96.v, pss[0:96, :], AF.Ln, bias=bias_ap, scale=scale)
        self.act(rstd96.v, std96.v, AF.Exp, scale=-0.5)
        self.copy('act', xb.v, x_pre.v)
        prot = self.nb()
        self.mm(prot[0:96, :], Rg.v, xb.v)
        self.stt('dve', t1.v, x_pre.v, g[:, 0:1], cos_t.v, ALU.mult, ALU.mult)
        self.tt('dve', t2.v, prot[0:96, :], sin_t.v, ALU.mult)
        self.tt('pool', t1.v, t1.v, t2.v, ALU.add)
        self.tt('pool', out, t1.v, rstd96.v, ALU.mult)

    def mla_phase(self, xin, xout):
        L = 0
        ps = self.ps
        kr_s = self.dram_scr("mla_kr", (32, S), F32)
        o_s = self.dram_scr("mla_o", (D, S), BF16)
        cosd, sind = self.din['rope_cos'], self.din['rope_sin']
        with ExitStack() as st0:
            cqn = [self.T(st0, [128, S], BF16, 'cqn') for _ in range(3)]
            ckvn = [self.T(st0, [128, S], BF16, 'ckvn') for _ in range(2)]
            with ExitStack() as st:
                wdq = self.load_w(st, 'mla_w_dq', 8, 384)
                wdkv = self.load_w(st, 'mla_w_dkv', 8, 288)
                qn = self.T(st, [128, 3], F32)
                kvn = self.T(st, [128, 2], F32)
                self.dma('sp', qn.v, self.din['mla_q_norm'])
                self.dma('sp', kvn.v, self.din['mla_kv_norm'])
                xts = [self.T(st, [128, 8, 512], F32) for _ in range(2)]
                hn = [self.T(st, [128, 512], BF16) for _ in range(8)]
                sq = [self.T(st, [128, 512], BF16) for _ in range(8)]
                std = self.T(st, [128, 512], F32)
                rstd = self.T(st, [128, 512], F32)
                std2 = self.T(st, [128, 512], F32)
                rstd2 = self.T(st, [128, 512], F32)
                krt = [self.T(st, [32, 512], F32) for _ in range(2)]
                for ti in range(S // 512):
                    t0 = ti * 512
                    tc_ = slice(t0, t0 + 512)
                    xt = xts[ti % 2]
                    self.load_x_norm(L, xin, t0, xt, hn, sq, std, rstd, ps[7])
                    for j in range(3):
                        for k in range(8):
                            self.mm(ps[j].v, wdq[:, k, j * 128:(j + 1) * 128], hn[k].v, start=(k == 0), stop=(k == 7))
                    self.rmsnorm([ps[j].v for j in range(3)], [qn[:, j:j + 1] for j in range(3)],
                                 [cqn[j][:, tc_] for j in range(3)], [sq[j].v for j in range(3)],
                                 ps[3].v, std2.v, rstd2.v, 384, 512)
                    for j in range(2):
                        for k in range(8):
                            self.mm(ps[4 + j].v, wdkv[:, k, j * 128:(j + 1) * 128], hn[k].v, start=(k == 0), stop=(k == 7))
                    self.rmsnorm([ps[4 + j].v for j in range(2)], [kvn[:, j:j + 1] for j in range(2)],
                                 [ckvn[j][:, tc_] for j in range(2)], [sq[3 + j].v for j in range(2)],
                                 ps[6].v, std2.v, rstd2.v, 256, 512)
                    pk = ps[7]
                    for k in range(8):
                        self.mm(pk[0:32, :], wdkv[:, k, 256:288], hn[k].v, start=(k == 0), stop=(k == 7))
                    kr = krt[ti % 2]
                    self.copy('act', kr.v, pk[0:32, :])
                    self.dma('sp', kr_s[:, tc_], kr.v)
            self.p.barrier()
            with ExitStack() as st:
                wuq = self.load_w(st, 'mla_w_uq', 3, 1536)
                wukv = self.load_w(st, 'mla_w_ukv', 2, 2048)
                qg = self.T(st, [96, 16], F32)
                kg = self.T(st, [96, 16], F32)
                Rf = self.T(st, [96, 96], F32)
                Rq = self.T(st, [96, 96], BF16)
                Rk = self.T(st, [96, 96], BF16)
                epsq = self.T(st, [96, 1], F32)
                self.memset('pool', epsq.v, EPS * 96.0)
                self.dma('sp', qg.v, self.din['mla_qg'])
                self.dma('sp', kg.v, self.din['mla_kg'])
                self.dma('sp', Rf.v, self.din['rope_R'])
                self.ts('dve', Rq.v, Rf.v, qg[:, 0:1], None, ALU.mult)
                self.ts('dve', Rk.v, Rf.v, kg[:, 0:1], None, ALU.mult)
                KTs = [self.T(st, [96, S], BF16, 'KT') for _ in range(2)]
                VAs = [self.T(st, [128, 64, 128], BF16, 'VA') for _ in range(2)]
                for v_ in VAs:
                    self.memset('pool', v_.v, 1.0)
                kp = [self.T(st, [96, 512], F32) for _ in range(2)]
                cs = [self.T(st, [96, 512], F32) for _ in range(2)]
                sn = [self.T(st, [96, 512], F32) for _ in range(2)]
                def mk_tl():
                    return dict(sq=self.T(st, [96, 512], BF16), xb=self.T(st, [96, 512], BF16),
                                rs=self.T(st, [96, 512], F32), t1=self.T(st, [96, 512], F32),
                                t2=self.T(st, [96, 512], F32))
                tlq, tlk = mk_tl(), mk_tl()
                Qf = [self.T(st, [96, 512], BF16) for _ in range(2)]
                qpre = self.T(st, [96, 512], F32)
                Pt = [self.T(st, [128, 1024], BF16) for _ in range(3)]
                rsum = [self.T(st, [64, 512], F32) for _ in range(1)]
                oh = [self.T(st, [64, 512], BF16) for _ in range(2)]
                self.nbn = 2
                self.nbo = 4
                cnt = {'ci': 0, 'npt': 0}

                def nr_stages(tl_, x_pre, c_t, s_t, g, Rg, bias_ap, scale, out):
                    hold = {}

                    def s_a():
                        self.tt('dve', tl_['sq'].v, x_pre.v, x_pre.v, ALU.mult)
                        self.copy('pool', tl_['xb'].v, x_pre.v)

                    def s_b():
                        hold['pss'] = self.nb()
                        self.mm(hold['pss'][0:96, :], self.ones_bf[0:96, 0:96], tl_['sq'].v)

                    def s_c():
                        self.act(tl_['rs'].v, hold['pss'][0:96, :], AF.Ln, bias=bias_ap, scale=scale)
                        self.act(tl_['rs'].v, tl_['rs'].v, AF.Exp, scale=-0.5)

                    def s_d():
                        hold['prot'] = self.nb()
                        self.mm(hold['prot'][0:96, :], Rg.v, tl_['xb'].v)

                    def s_e():
                        self.stt('dve', tl_['t1'].v, x_pre.v, g[:, 0:1], c_t.v, ALU.mult, ALU.mult)
                        self.tt('dve', tl_['t2'].v, hold['prot'][0:96, :], s_t.v, ALU.mult)

                    def s_f():
                        self.tt('pool', tl_['t1'].v, tl_['t1'].v, tl_['t2'].v, ALU.add)
                        self.tt('pool', out, tl_['t1'].v, tl_['rs'].v, ALU.mult)
                    return [s_a, s_b, s_c, s_d, s_e, s_f]

                def kgen_stages(h, ti):
                    KT, VA = KTs[h % 2], VAs[h % 2]
                    t0 = ti * 512
                    tc_ = slice(t0, t0 + 512)
                    ci = cnt['ci']
                    cnt['ci'] += 1
                    kpre, c_t, s_t = kp[ci % 2], cs[ci % 2], sn[ci % 2]
                    hold = {}

                    def k1():
                        hold['pk'] = self.nb()
                        for j in range(2):
                            self.mm(hold['pk'][0:64, :], wukv[:, j, h * 128:h * 128 + 64], ckvn[j][:, tc_],
                                    start=(j == 0), stop=(j == 1))
                        self.dma('sp', kpre[64:96, :], kr_s[:, tc_])
                        self.dma('sp', c_t.v, cosd[:, tc_])
                        self.dma('sp', s_t.v, sind[:, tc_])

                    def k2():
                        self.copy('dve', kpre[0:64, :], hold['pk'][0:64, :])

                    def k9():
                        hold['pv'] = self.nb()
                        for b in range(4):
                            for j in range(2):
                                self.mm(hold['pv'][:, b * 64:(b + 1) * 64], ckvn[j][:, t0 + b * 128:t0 + (b + 1) * 128],
                                        wukv[:, j, h * 128 + 64:h * 128 + 128], start=(j == 0), stop=(j == 1))

                    def k10():
                        self.copy('dve', VA[:, ti * 4:(ti + 1) * 4, 0:64], hold['pv'][:, 0:256].re("p (b v) -> p b v", b=4))
                    return [k1, k2] + nr_stages(tlk, kpre, c_t, s_t, kg, Rk, self.epsT[0:96, :], 1.0 / 96.0, KT[:, tc_]) + [k9, k10]

                def qgen_stages(h, qi):
                    qc_ = slice(qi * 512, qi * 512 + 512)
                    ci = cnt['ci']
                    cnt['ci'] += 1
                    c_t, s_t = cs[ci % 2], sn[ci % 2]
                    qf = Qf[(h * 16 + qi) % 2]
                    hold = {}

                    def q1():
                        hold['pq'] = self.nb()
                        for j in range(3):
                            self.mm(hold['pq'][0:96, :], wuq[:, j, h * 96:(h + 1) * 96], cqn[j][:, qc_], start=(j == 0), stop=(j == 2))
                        self.dma('sp', c_t.v, cosd[:, qc_])
                        self.dma('sp', s_t.v, sind[:, qc_])

                    def q2():
                        self.copy('dve', qpre.v, hold['pq'][0:96, :])
                    return [q1, q2] + nr_stages(tlq, qpre, c_t, s_t, qg, Rq, epsq.v, 1.0, qf.v)

                def attn(h, qi, pending):
                    KT, VA = KTs[h % 2], VAs[h % 2]
                    qc_ = slice(qi * 512, qi * 512 + 512)
                    qf = Qf[(h * 16 + qi) % 2]
                    po = ps[6 + qi % 2]
                    nkb = 4 * qi + 4
                    npairs = nkb // 2
                    pend = []
                    for pp in range(npairs):
                        pst = self.ps2[cnt['npt'] % 2]
                        P = Pt[cnt['npt'] % 3]
                        cnt['npt'] += 1
                        for hf in range(2):
                            kb = 2 * pp + hf
                            self.mm(pst[:, hf * 512:(hf + 1) * 512], KT[:, kb * 128:(kb + 1) * 128], qf.v)
                        self.act(P.v, pst.v, AF.Exp)
                        for hf in range(2):
                            kb = 2 * pp + hf
                            kl = kb - 4 * qi
                            if kl >= 0:
                                if kl > 0:
                                    self.memset('pool', P[:, hf * 512:hf * 512 + 128 * kl], 0.0)
                                self.memset('pool', P[64:128, hf * 512 + 128 * kl:hf * 512 + 128 * kl + 64], 0.0)
                        pend.append((pp, P))
                        if len(pend) > 1:
                            pp_, P_ = pend.pop(0)
                            for hf in range(2):
                                kb_ = 2 * pp_ + hf
                                self.mm(po.v, VA[:, kb_, :], P_[:, hf * 512:(hf + 1) * 512], start=(kb_ == 0), stop=False)
                        left = npairs - pp
                        nst = -(-len(pending) // left)
                        for _ in range(nst):
                            if pending:
                                pending.pop(0)()
                    for (pp_, P_) in pend:
                        for hf in range(2):
                            kb_ = 2 * pp_ + hf
                            self.mm(po.v, VA[:, kb_, :], P_[:, hf * 512:(hf + 1) * 512], start=(kb_ == 0), stop=(kb_ == nkb - 1))
                    while pending:
                        pending.pop(0)()
                    rs_, oh_ = rsum[0], oh[qi % 2]
                    self.act(rs_.v, po[64:128, :], AF.Ln)
                    self.act(rs_.v, rs_.v, AF.Exp, scale=-1.0)
                    self.tt('dve', oh_.v, po[0:64, :], rs_.v, ALU.mult)
                    self.dma('sp', o_s[h * 64:(h + 1) * 64, qc_], oh_.v)

                for ti in range(16):
                    for f_ in kgen_stages(0, ti):
                        f_()
                for f_ in qgen_stages(0, 0):
                    f_()
                for h in range(16):
                    for qi in range(16):
                        pending = []
                        qs_ = ks_ = []
                        if qi + 1 < 16:
                            qs_ = qgen_stages(h, qi + 1)
                        elif h + 1 < 16:
                            qs_ = qgen_stages(h + 1, 0)
                        if h + 1 < 16:
                            ks_ = kgen_stages(h + 1, qi)
                        qs_, ks_ = list(qs_), list(ks_)
                        while qs_ or ks_:
                            if qs_:
                                pending.append(qs_.pop(0))
                            if ks_:
                                pending.append(ks_.pop(0))
                        attn(h, qi, pending)
                self.nbn = 4
                self.nbo = 0
        self.p.barrier()
        with ExitStack() as st:
            wo = self.load_w(st, 'mla_w_o', 8, 1024)
            xts = [self.T(st, [128, 8, 512], F32) for _ in range(2)]
            ots = [self.T(st, [128, 8, 512], BF16) for _ in range(2)]
            ot = [self.T(st, [128, 512], F32) for _ in range(2)]
            xv = xin.rearrange("(c p) t -> p c t", p=128)
            ov = o_s.rearrange("(c p) t -> p c t", p=128)
            for ti in range(S // 512):
                t0 = ti * 512
                xt, o_t = xts[ti % 2], ots[ti % 2]
                self.dma('sp', xt.v, xv[:, :, t0:t0 + 512])
                self.dma('sp', o_t.v, ov[:, :, t0:t0 + 512])
                self.out_proj(wo, [o_t[:, j, :] for j in range(8)], xt, ot, xout, t0)


def _colmajor(v, n):
    return np.ascontiguousarray(np.asarray(v, np.float32).reshape(n // 128, 128).T)


def _host_inputs(inputs, layers):
    m = {}
    m["norm_mix"] = np.ascontiguousarray(
        np.asarray(inputs["norm_mix"], np.float32).reshape(4, 8, 128).transpose(2, 0, 1).reshape(128, 32))
    m["norm_ffn"] = np.ascontiguousarray(
        np.asarray(inputs["norm_ffn"], np.float32).reshape(4, 8, 128).transpose(2, 0, 1).reshape(128, 32))
    for L in layers:
        for n in LAYER_W[L]:
            m[n] = np.ascontiguousarray(np.asarray(inputs[n], np.float32)[0])
        for n, ln in LAYER_V[L]:
            m[n] = _colmajor(inputs[n][0], ln)
        m["ffn_w1_%d" % L] = np.ascontiguousarray(np.asarray(inputs["ffn_w1"], np.float32)[L])
        m["ffn_w2_%d" % L] = np.ascontiguousarray(np.asarray(inputs["ffn_w2"], np.float32)[L])
    idx = np.arange(128)
    same = (idx[:, None] // 64) == (idx[None, :] // 64)
    if 1 in layers or 2 in layers:
        m["bcmask"] = (same & (idx[:, None] <= idx[None, :])).astype(np.float32)
    if 0 in layers:
        m["mla_qg"] = np.ascontiguousarray(np.repeat(np.asarray(inputs["mla_q_gain"], np.float32)[0].reshape(96, 1), 16, axis=1))
        m["mla_kg"] = np.ascontiguousarray(np.repeat(np.asarray(inputs["mla_k_gain"], np.float32)[0].reshape(96, 1), 16, axis=1))
        inv = (10000.0 ** (-np.arange(16, dtype=np.float32) / 16.0)).astype(np.float32)
        ang = np.arange(S, dtype=np.float32)[None, :] * inv[:, None]
        cos = np.ones((96, S), np.float32)
        sin = np.zeros((96, S), np.float32)
        cos[64:80] = np.cos(ang)
        cos[80:96] = np.cos(ang)
        sin[64:80] = np.sin(ang)
        sin[80:96] = np.sin(ang)
        m["rope_cos"] = cos
        m["rope_sin"] = sin
        R = np.zeros((96, 96), np.float32)
        for i in range(16):
            R[80 + i, 64 + i] = -1.0
            R[64 + i, 80 + i] = 1.0
        m["rope_R"] = R
    if 1 in layers:
        b = np.asarray(inputs["mlstm_b_if"], np.float32)[0]
        m["mlstm_b_if"] = np.ascontiguousarray(np.tile(np.stack([b[0:4], b[4:8]], axis=1), (1, 8)))
        sel = np.zeros((4, 512), np.float32)
        for h in range(4):
            sel[h, h * 128:(h + 1) * 128] = 1.0
        m["sel4"] = sel
        m["lmat"] = (((idx[:, None] // 32) == (idx[None, :] // 32)) & (idx[:, None] < idx[None, :])).astype(np.float32)
        m["ident"] = np.eye(128, dtype=np.float32)
        m["mneg"] = np.where(((idx[:, None] // 32) == (idx[None, :] // 32)) & (idx[None, :] < idx[:, None]), 0.0, -1e30).astype(np.float32)
    if 2 in layers:
        m["gla_w_a2b"] = np.ascontiguousarray(np.concatenate(
            [np.asarray(inputs["gla_w_a2"], np.float32)[0], np.asarray(inputs["gla_b_a"], np.float32)[0][None, :]], axis=0))
        m["triN"] = (same & (idx[:, None] <= idx[None, :])).astype(np.float32) * (-1.0 / 16.0)
        m["triU"] = (same & (idx[:, None] > idx[None, :])).astype(np.float32) * (-1.0 / 16.0)
    if 3 in layers:
        w = np.asarray(inputs["conv_w_dw"], np.float32)[0]
        m["conv_w_dw"] = np.ascontiguousarray(w.reshape(31, 8, 128).transpose(2, 1, 0).reshape(128, 8 * 31))
        m["identc"] = np.eye(128, dtype=np.float32)
    return m


_NC_CACHE = {}


def run_layers(layers, xT_list, inputs, skip_ffn=False):
    key = (tuple(layers), skip_ffn)
    if key not in _NC_CACHE:
        _NC_CACHE[key] = KB(list(layers), skip_ffn).build()
    nc = _NC_CACHE[key]
    shared = _host_inputs(inputs, layers)
    in_maps = []
    for xT in xT_list:
        mm = dict(shared)
        mm["xT"] = xT
        in_maps.append(mm)
    res = run_bass_kernel_spmd(nc, in_maps, core_ids=list(range(len(xT_list))))
    return [r["yT"] for r in res.results]


def kernel(**inputs):
    x = np.asarray(inputs["x"], np.float32)
    B = x.shape[0]
    xT = [np.ascontiguousarray(x[b % B].T) for b in range(8)]
    outs = run_layers((0, 1, 2, 3), xT, inputs)
    y = np.stack([np.ascontiguousarray(outs[b].T) for b in range(B)], axis=0)
    return y.astype(np.float32)
```

```python
import math
import numpy as np
from contextlib import ExitStack
import concourse.bass as bass
import concourse.mybir as mybir
from concourse.bass_utils import run_bass_kernel_spmd

F32 = mybir.dt.float32
BF16 = mybir.dt.bfloat16
AF = mybir.ActivationFunctionType
ALU = mybir.AluOpType

S = 8192
D = 1024
EPS = 1e-6
COMPUTE = ('pe', 'act', 'dve', 'pool')
ALLENG = ('pe', 'act', 'dve', 'pool', 'sp')
DMA_POOL = 8


class Tile:
    __slots__ = ('ap', 'w', 'r')

    def __init__(self, ap):
        self.ap = ap
        self.w = None
        self.r = []

    def __getitem__(self, k):
        return V(self, self.ap[k])

    @property
    def v(self):
        return V(self, self.ap)


class V:
    __slots__ = ('t', 'ap')

    def __init__(self, t, ap):
        self.t = t
        self.ap = ap

    def __getitem__(self, k):
        return V(self.t, self.ap[k])

    def bc(self, shape):
        return V(self.t, self.ap.to_broadcast(shape))

    def re(self, pat, **kw):
        return V(self.t, self.ap.rearrange(pat, **kw))

    def ub(self, axis, shape):
        return V(self.t, self.ap.unsqueeze(axis).to_broadcast(shape))


def _ap(x):
    return x.ap if isinstance(x, V) else x


def _tl(*xs):
    return [x.t for x in xs if isinstance(x, V)]


class Prog:
    def __init__(self, nc):
        self.nc = nc
        self.ops = {e: [] for e in ALLENG}
        self.ndma = {e: 0 for e in ALLENG}
        self.lastc = {e: None for e in ALLENG}
        self.dmas = {e: [] for e in ALLENG}

    def op(self, eng, fn, reads=(), writes=(), dma=False):
        deps = set()
        for t in reads:
            if t.w is not None:
                deps.add(t.w)
        for t in writes:
            if t.w is not None:
                deps.add(t.w)
            deps.update(t.r)
        me = (eng, len(self.ops[eng]))
        rec = dict(fn=fn, deps=deps, dma=dma, inc=False, val=None, sem=None)
        if dma:
            rec['dj'] = self.ndma[eng]
            self.ndma[eng] += 1
            self.dmas[eng].append(me)
        else:
            self.lastc[eng] = me
        self.ops[eng].append(rec)
        for t in reads:
            t.r.append(me)
        for t in writes:
            t.w = me
            t.r = []
        return me

    def barrier(self):
        deps = set()
        for e in ALLENG:
            if self.lastc[e] is not None:
                deps.add(self.lastc[e])
            deps.update(self.dmas[e][-DMA_POOL:])
        for e in ALLENG:
            self.ops[e].append(dict(fn=None, deps=set(deps), dma=False, inc=False, val=None, sem=None))

    def emit(self, stack):
        nc = self.nc
        ops = self.ops
        for e in ALLENG:
            for o in ops[e]:
                nd = set()
                for (pe_, pi) in o['deps']:
                    p = ops[pe_][pi]
                    if not p['dma']:
                        if pe_ == e and e == 'pe' and not o['dma'] and o['fn'] is not None:
                            continue
                        p['inc'] = True
                    nd.add((pe_, pi))
                o['deps'] = nd
        sems = {e: stack.enter_context(nc.semaphore('s_' + e)) for e in COMPUTE}
        dsems = {}
        for e in ALLENG:
            if self.ndma[e] > 0:
                dsems[e] = [stack.enter_context(nc.semaphore('d_%s_%d' % (e, k))) for k in range(DMA_POOL)]
        for e in ALLENG:
            cnt = 0
            for o in ops[e]:
                if o['dma']:
                    j = o['dj']
                    o['sem'] = dsems[e][j % DMA_POOL]
                    o['val'] = 16 * (j // DMA_POOL + 1)
                elif o['inc']:
                    cnt += 1
                    o['sem'] = sems[e]
                    o['val'] = cnt
        block = stack.enter_context(nc.Block())

        def run(e, engobj):
            waited = {}
            for o in ops[e]:
                need = {}
                for (pe_, pi) in o['deps']:
                    p = ops[pe_][pi]
                    s = p['sem']
                    if waited.get(s.num, 0) >= p['val']:
                        continue
                    if s.num not in need or need[s.num][1] < p['val']:
                        need[s.num] = (s, p['val'])
                if o['dma'] and o['val'] > 16:
                    s = o['sem']
                    v = o['val'] - 16
                    if waited.get(s.num, 0) < v and (s.num not in need or need[s.num][1] < v):
                        need[s.num] = (s, v)
                for key, (s, v) in need.items():
                    engobj.wait_ge(s, v)
                    waited[key] = v
                if o['fn'] is None:
                    continue
                ins = o['fn'](engobj)
                if o['dma']:
                    ins.then_inc(o['sem'], 16)
                elif o['inc']:
                    ins.then_inc(o['sem'], 1)
            n = self.ndma[e]
            for k in range(min(n, DMA_POOL)):
                cntk = (n - 1 - k) // DMA_POOL + 1
                if waited.get(dsems[e][k].num, 0) < 16 * cntk:
                    engobj.wait_ge(dsems[e][k], 16 * cntk)

        @block.tensor
        def _(pe):
            run('pe', pe)

        @block.scalar
        def _(act):
            run('act', act)

        @block.vector
        def _(dve):
            run('dve', dve)

        @block.gpsimd
        def _(pool):
            run('pool', pool)

        @block.sync
        def _(sp):
            run('sp', sp)


LAYER_W = {
    0: ['mla_w_dq', 'mla_w_uq', 'mla_w_dkv', 'mla_w_ukv', 'mla_w_o'],
    1: ['mlstm_w_in', 'mlstm_w_if', 'mlstm_w_o'],
    2: ['gla_w_in', 'gla_w_a1', 'gla_w_o'],
    3: ['conv_w_pw1', 'conv_w_pw2'],
}
WSHAPE = {
    'mla_w_dq': (1024, 384), 'mla_w_uq': (384, 1536), 'mla_w_dkv': (1024, 288), 'mla_w_ukv': (256, 2048),
    'mla_w_o': (1024, 1024), 'mlstm_w_in': (1024, 3072), 'mlstm_w_if': (1024, 8), 'mlstm_w_o': (1024, 1024),
    'gla_w_in': (1024, 3072), 'gla_w_a1': (1024, 16), 'gla_w_o': (1024, 1024),
    'conv_w_pw1': (1024, 2048), 'conv_w_pw2': (1024, 1024),
}
LAYER_V = {
    0: [('mla_q_norm', 384), ('mla_kv_norm', 256)],
    1: [('mlstm_head_norm', 1024)],
    2: [('gla_head_norm', 1024)],
    3: [('conv_b_pw1', 2048), ('conv_b_dw', 1024), ('conv_ln_g', 1024), ('conv_ln_b', 1024), ('conv_b_pw2', 1024)],
}


class KB:
    def __init__(self, layers, skip_ffn=False):
        self.layers = layers
        self.skip_ffn = skip_ffn
        self.nc = bass.Bass("TRN2", target_bir_lowering=False)
        self.p = Prog(self.nc)
        self.gst = ExitStack()
        self.din = {}
        self.uid = 0

    def dram_in(self, name, shape, dt=F32):
        a = self.nc.dram_tensor(name, list(shape), dt, kind="ExternalInput").ap()
        self.din[name] = a
        return a

    def dram_scr(self, name, shape, dt):
        return self.nc.dram_tensor(name, list(shape), dt, kind="Internal").ap()

    def sb(self, st, shape, dt=F32, name=None):
        self.uid += 1
        return st.enter_context(self.nc.sbuf_tensor("%s_%d" % (name or 'sb', self.uid), list(shape), dt))[:]

    def T(self, st, shape, dt=F32, name=None):
        return Tile(self.sb(st, shape, dt, name))

    def mm(self, out, lhsT, rhs, start=True, stop=True):
        o, l, r = _ap(out), _ap(lhsT), _ap(rhs)
        self.p.op('pe', lambda e: e.matmul(o, l, r, start=start, stop=stop), _tl(lhsT, rhs), _tl(out))

    def act(self, out, in_, func, bias=None, scale=None, eng='act'):
        o, i = _ap(out), _ap(in_)
        kw = {}
        if bias is not None:
            kw['bias'] = _ap(bias)
        if scale is not None:
            kw['scale'] = _ap(scale)
        self.p.op('act', lambda e: e.activation(o, i, func, **kw), _tl(in_, bias, scale), _tl(out))

    def tt(self, eng, out, a, b, op):
        o, x, y = _ap(out), _ap(a), _ap(b)
        self.p.op(eng, lambda e: e.tensor_tensor(o, x, y, op), _tl(a, b), _tl(out))

    def stt(self, eng, out, in0, scalar, in1, op0, op1):
        o, x, s, y = _ap(out), _ap(in0), _ap(scalar), _ap(in1)
        self.p.op(eng, lambda e: e.scalar_tensor_tensor(o, x, s, y, op0, op1), _tl(in0, scalar, in1), _tl(out))

    def ts(self, eng, out, in0, s1, s2, op0, op1=None):
        o, x, a, b = _ap(out), _ap(in0), _ap(s1), _ap(s2)
        if op1 is None:
            self.p.op(eng, lambda e: e.tensor_scalar(o, x, a, None, op0), _tl(in0, s1), _tl(out))
        else:
            self.p.op(eng, lambda e: e.tensor_scalar(o, x, a, b, op0, op1), _tl(in0, s1, s2), _tl(out))

    def copy(self, eng, out, in_):
        o, i = _ap(out), _ap(in_)
        if eng == 'act':
            self.p.op('act', lambda e: e.activation(o, i, AF.Copy), _tl(in_), _tl(out))
        else:
            self.p.op(eng, lambda e: e.tensor_copy(o, i), _tl(in_), _tl(out))

    def memset(self, eng, out, val):
        o = _ap(out)
        self.p.op(eng, lambda e: e.memset(o, val), (), _tl(out))

    def recip(self, out, in_):
        o, i = _ap(out), _ap(in_)
        self.p.op('dve', lambda e: e.reciprocal(o, i), _tl(in_), _tl(out))

    def scan(self, out, d0, d1, init, op0, op1):
        o, a, b = _ap(out), _ap(d0), _ap(d1)
        self.p.op('dve', lambda e: e.tensor_tensor_scan(o, a, b, init, op0, op1), _tl(d0, d1), _tl(out))

    def dma(self, q, out, in_, **kw):
        o, i = _ap(out), _ap(in_)
        self.p.op(q, lambda e: e.dma_start(out=o, in_=i, **kw), _tl(in_), _tl(out), dma=True)

    def build(self):
        nc, p = self.nc, self.p
        layers = self.layers
        gst = self.gst
        xin = self.dram_in("xT", (D, S))
        yout = nc.dram_tensor("yT", [D, S], F32, kind="ExternalOutput").ap()
        nmix = self.dram_in("norm_mix", (128, 32))
        nffn = self.dram_in("norm_ffn", (128, 32))
        for L in layers:
            for n in LAYER_W[L]:
                self.dram_in(n, WSHAPE[n])
            for n, ln in LAYER_V[L]:
                self.dram_in(n, (128, ln // 128))
            self.dram_in("ffn_w1_%d" % L, (D, 4096))
            self.dram_in("ffn_w2_%d" % L, (4096, D))
        if 0 in layers:
            self.dram_in("mla_qg", (96, 16))
            self.dram_in("mla_kg", (96, 16))
            self.dram_in("rope_cos", (96, S))
            self.dram_in("rope_sin", (96, S))
            self.dram_in("rope_R", (96, 96))
        if 1 in layers:
            self.dram_in("mlstm_b_if", (4, 16))
            self.dram_in("sel4", (4, 512))
            self.dram_in("lmat", (128, 128))
            self.dram_in("ident", (128, 128))
            self.dram_in("mneg", (128, 128))
        if 2 in layers:
            self.dram_in("gla_w_a2b", (17, 512))
            self.dram_in("triN", (128, 128))
            self.dram_in("triU", (128, 128))
        if 1 in layers or 2 in layers:
            self.dram_in("bcmask", (128, 128))
        if 3 in layers:
            self.dram_in("conv_w_dw", (128, 8 * 31))
            self.dram_in("identc", (128, 128))

        self.ps = []
        self.ps2 = []
        for i in range(4):
            pa_ = gst.enter_context(nc.psum_tensor("psp%d" % i, [128, 1024], F32))[:]
            self.ps2.append(Tile(pa_))
            self.ps.append(Tile(pa_[:, 0:512]))
            self.ps.append(Tile(pa_[:, 512:1024]))
        self.ones_bf = self.T(gst, [128, 128], BF16, 'ones')
        self.memset('pool', self.ones_bf.v, 1.0)
        self.epsT = self.T(gst, [128, 1], F32, 'eps')
        self.memset('pool', self.epsT.v, EPS)
        self.gmix = self.T(gst, [128, 32], F32, 'gmix')
        self.gffn = self.T(gst, [128, 32], F32, 'gffn')
        self.dma('sp', self.gmix.v, nmix)
        self.dma('sp', self.gffn.v, nffn)

        self.wb = {}
        cast_list = []
        ffn_list = []
        for L in layers:
            for n in LAYER_W[L]:
                shp = WSHAPE[n]
                dst = self.dram_scr(n + "_bf", shp, BF16)
                self.wb[n] = dst
                cast_list.append((self.din[n], dst, shp))
            d1 = self.dram_scr("ffn_w1_%d_bf" % L, (8, 128, 8 * 512), BF16)
            d2 = self.dram_scr("ffn_w2_%d_bf" % L, (4, 128, 32 * 256), BF16)
            self.wb["ffn_w1_%d" % L] = d1
            self.wb["ffn_w2_%d" % L] = d2
            ffn_list.append((self.din["ffn_w1_%d" % L], d1, self.din["ffn_w2_%d" % L], d2))
        self.cast_phase(cast_list, ffn_list)
        p.barrier()

        xm = self.dram_scr("x_mid", (D, S), F32)
        xs = [self.dram_scr("x_s0", (D, S), F32), self.dram_scr("x_s1", (D, S), F32)]
        cur = xin
        for li, L in enumerate(layers):
            nxt = yout if li == len(layers) - 1 else xs[li % 2]
            if self.skip_ffn:
                xm = nxt
            if L == 0:
                self.mla_phase(cur, xm)
            elif L == 1:
                self.mlstm_phase(cur, xm)
            elif L == 2:
                self.gla_phase(cur, xm)
            else:
                self.conv_phase(cur, xm)
            p.barrier()
            if not self.skip_ffn:
                self.ffn_phase(L, xm, nxt)
                p.barrier()
            cur = nxt
        p.emit(gst)
        gst.close()
        return nc

    def cast_phase(self, items, ffn_items):
        CH = 8192
        with ExitStack() as st:
            src_t = [self.T(st, [128, CH], F32, 'cs') for _ in range(2)]
            dst_t = [self.T(st, [128, CH], BF16, 'cd') for _ in range(2)]
            i = 0
            engs = ['dve', 'act', 'dve']
            for (src, dst, shp) in items:
                K, N = shp
                M = K * N // 128
                sv = src.rearrange("(p a) n -> p (a n)", p=128)
                dv = dst.rearrange("(p a) n -> p (a n)", p=128)
                for c0 in range(0, M, CH):
                    w = min(CH, M - c0)
                    a, b = src_t[i % 2], dst_t[i % 2]
                    self.dma('sp', a[:, 0:w], sv[:, c0:c0 + w])
                    self.copy(engs[i % 3], b[:, 0:w], a[:, 0:w])
                    self.dma('pool', dv[:, c0:c0 + w], b[:, 0:w])
                    i += 1
            for (s1, d1, s2, d2) in ffn_items:
                s1v = s1.rearrange("(c p) f -> p c f", p=128)
                for fg in range(8):
                    a, b = src_t[i % 2], dst_t[i % 2]
                    self.dma('sp', a[:, 0:4096].re("p (c f) -> p c f", c=8), s1v[:, :, fg * 512:(fg + 1) * 512])
                    self.copy(engs[i % 3], b[:, 0:4096], a[:, 0:4096])
                    self.dma('pool', d1[fg], b[:, 0:4096])
                    i += 1
                s2v = s2.rearrange("(c p) d -> p c d", p=128)
                for dg in range(4):
                    a, b = src_t[i % 2], dst_t[i % 2]
                    av = a.v.re("p (c d) -> p c d", c=32)
                    for q in range(4):
                        self.dma('sp', av[:, q * 8:(q + 1) * 8, :], s2v[:, q * 8:(q + 1) * 8, dg * 256:(dg + 1) * 256])
                    self.copy(engs[i % 3], b.v, a.v)
                    self.dma('pool', d2[dg], b.v)
                    i += 1

    def rmsnorm(self, x_chunks, gcols, hn_chunks, sq_chunks, ps, std, rstd, n_feat, width, eng_alt=('dve',)):
        n = len(x_chunks)
        for c in range(n):
            self.act(sq_chunks[c], x_chunks[c], AF.Square)
        for c in range(n):
            self.mm(ps, self.ones_bf.v, sq_chunks[c], start=(c == 0), stop=(c == n - 1))
        self.act(std, ps, AF.Ln, bias=self.epsT.v, scale=1.0 / n_feat)
        self.act(rstd, std, AF.Exp, scale=-0.5)
        for c in range(n):
            self.stt(eng_alt[c % len(eng_alt)], hn_chunks[c], x_chunks[c], gcols[c], rstd, ALU.mult, ALU.mult)

    def ffn_phase(self, L, xin, xout):
        TT = 1024
        w1 = self.wb["ffn_w1_%d" % L]
        w2 = self.wb["ffn_w2_%d" % L]
        xv = xin.rearrange("(c p) t -> p c t", p=128)
        ps = self.ps
        with ExitStack() as st:
            xt = self.T(st, [128, 8, TT], F32, 'fx')
            hn = [[self.T(st, [128, 512], BF16, 'fhn') for _ in range(2)] for _ in range(8)]
            sq = [self.T(st, [128, 512], BF16, 'fsq') for _ in range(8)]
            std = self.T(st, [128, 512], F32, 'fstd')
            rstd = self.T(st, [128, 512], F32, 'frstd')
            a = [[self.T(st, [128, 512], BF16, 'fa') for _ in range(2)] for _ in range(32)]
            w1t = [self.T(st, [128, 8, 512], BF16, 'fw1') for _ in range(2)]
            w2t = [self.T(st, [128, 32, 256], BF16, 'fw2') for _ in range(2)]
            rl = [self.T(st, [128, 512], F32, 'frl') for _ in range(4)]
            xres = [self.T(st, [128, TT], F32, 'fxr') for _ in range(2)]
            ot = [self.T(st, [128, TT], F32, 'fo') for _ in range(2)]
            nw1 = 0
            nw2 = 0
            nr = 0
            for sti in range(S // TT):
                t0 = sti * TT
                self.dma('sp', xt.v, xv[:, :, t0:t0 + TT])
                for half in range(2):
                    xs_ = [xt[:, c, half * 512:(half + 1) * 512] for c in range(8)]
                    self.rmsnorm(xs_, [self.gffn[:, L * 8 + c:L * 8 + c + 1] for c in range(8)],
                                 [hn[c][half].v for c in range(8)], [sq[c].v for c in range(8)],
                                 ps[7].v, std.v, rstd.v, D, 512)
                for fg in range(8):
                    wt = w1t[nw1 % 2]
                    nw1 += 1
                    self.dma('sp', wt.v.re("p c f -> p (c f)"), w1[fg])
                    for fi in range(4):
                        f = fg * 4 + fi
                        pb = (f % 2) * 2
                        for k in range(8):
                            for half in range(2):
                                self.mm(ps[pb + half].v, wt[:, k, fi * 128:(fi + 1) * 128], hn[k][half].v,
                                        start=(k == 0), stop=(k == 7))
                        for half in range(2):
                            r = rl[nr % 4]
                            nr += 1
                            self.act(r.v, ps[pb + half].v, AF.Relu)
                            self.tt('pool' if half else 'dve', a[f][half].v, r.v, r.v, ALU.mult)
                for dg in range(4):
                    wt = w2t[nw2 % 2]
                    nw2 += 1
                    self.dma('sp', wt.v.re("p c d -> p (c d)"), w2[dg])
                    for dd in range(2):
                        d = dg * 2 + dd
                        xr = xres[d % 2]
                        o = ot[d % 2]
                        self.dma('sp', xr.v, xin[d * 128:(d + 1) * 128, t0:t0 + TT])
                        pb = 4 + (d % 2) * 2
                        for f in range(32):
                            for half in range(2):
                                self.mm(ps[pb + half].v, wt[:, f, dd * 128:(dd + 1) * 128], a[f][half].v,
                                        start=(f == 0), stop=(f == 31))
                        for half in range(2):
                            self.tt('dve', o[:, half * 512:(half + 1) * 512], ps[pb + half].v,
                                    xr[:, half * 512:(half + 1) * 512], ALU.add)
                        self.dma('pool', xout[d * 128:(d + 1) * 128, t0:t0 + TT], o.v)

    def load_w(self, st, name, kc, n, cols=None):
        t = self.T(st, [128, kc, n], BF16, 'w')
        src = self.wb[name].rearrange("(c p) n -> p c n", p=128)
        if cols is not None:
            src = src[:, :, cols[0]:cols[1]]
        self.dma('sp', t.v, src)
        return t

    def load_x_norm(self, L, xin, t0, xt, hn, sq, std, rstd, psb):
        xv = xin.rearrange("(c p) t -> p c t", p=128)
        self.dma('sp', xt.v, xv[:, :, t0:t0 + 512])
        self.rmsnorm([xt[:, c, :] for c in range(8)], [self.gmix[:, L * 8 + c:L * 8 + c + 1] for c in range(8)],
                     [hn[c].v for c in range(8)], [sq[c].v for c in range(8)], psb.v, std.v, rstd.v, D, 512)

    def out_proj(self, wo, og, xt, ot, xout, t0, bias=None, banks=(6, 7)):
        ps = self.ps
        for oc in range(8):
            pb = ps[banks[oc % 2]]
            for j in range(8):
                self.mm(pb.v, wo[:, j, oc * 128:(oc + 1) * 128], og[j], start=(j == 0), stop=(j == 7))
            o = ot[oc % 2]
            if bias is None:
                self.tt('dve', o.v, pb.v, xt[:, oc, :], ALU.add)
            else:
                self.stt('dve', o.v, pb.v, bias[:, oc:oc + 1], xt[:, oc, :], ALU.add, ALU.add)
            self.dma('pool', xout[oc * 128:(oc + 1) * 128, t0:t0 + 512], o.v)

    def conv_phase(self, xin, xout):
        L = 3
        ps = self.ps
        with ExitStack() as st:
            w1 = self.load_w(st, 'conv_w_pw1', 8, 2048)
            w2 = self.load_w(st, 'conv_w_pw2', 8, 1024)
            b1 = self.T(st, [128, 16], F32)
            bdw = self.T(st, [128, 8], F32)
            lg = self.T(st, [128, 8], F32)
            lb = self.T(st, [128, 8], F32)
            b2 = self.T(st, [128, 8], F32)
            wdw = self.T(st, [128, 8 * 31], F32)
            identf = self.T(st, [128, 128], F32)
            for t_, n in ((b1, 'conv_b_pw1'), (bdw, 'conv_b_dw'), (lg, 'conv_ln_g'), (lb, 'conv_ln_b'),
                          (b2, 'conv_b_pw2'), (wdw, 'conv_w_dw'), (identf, 'identc')):
                self.dma('sp', t_.v, self.din[n])
            Dg = [self.T(st, [128, 31, 128], BF16, 'dg') for _ in range(8)]
            for c in range(8):
                self.tt('dve', Dg[c].v, identf.v.ub(1, [128, 31, 128]),
                        wdw[:, c * 31:(c + 1) * 31].ub(2, [128, 31, 128]), ALU.mult)
            xt = self.T(st, [128, 8, 512], F32, 'cx')
            hn = [self.T(st, [128, 512], BF16) for _ in range(8)]
            sq = [self.T(st, [128, 512], BF16) for _ in range(8)]
            std = self.T(st, [128, 512], F32)
            rstd = self.T(st, [128, 512], F32)
            u = [self.T(st, [128, 542], BF16, 'cu') for _ in range(8)]
            vv = [self.T(st, [128, 512], F32, 'cv') for _ in range(8)]
            vb = [self.T(st, [128, 512], BF16) for _ in range(8)]
            sig = [self.T(st, [128, 512], F32) for _ in range(2)]
            mean = self.T(st, [128, 512], F32)
            m2 = self.T(st, [128, 512], F32)
            var = self.T(st, [128, 512], F32)
            z = [self.T(st, [128, 512], BF16) for _ in range(8)]
            ot = [self.T(st, [128, 512], F32) for _ in range(2)]
            for c in range(8):
                self.memset('pool', u[c][:, 0:30], 0.0)
            for ti in range(S // 512):
                t0 = ti * 512
                self.load_x_norm(L, xin, t0, xt, hn, sq, std, rstd, ps[5])
                for oc in range(8):
                    pa, pg = ps[(oc % 2) * 2], ps[(oc % 2) * 2 + 1]
                    for k in range(8):
                        self.mm(pa.v, w1[:, k, oc * 128:(oc + 1) * 128], hn[k].v, start=(k == 0), stop=(k == 7))
                    for k in range(8):
                        self.mm(pg.v, w1[:, k, 1024 + oc * 128:1024 + (oc + 1) * 128], hn[k].v,
                                start=(k == 0), stop=(k == 7))
                    sg = sig[oc % 2]
                    self.act(sg.v, pg.v, AF.Sigmoid, bias=b1[:, 8 + oc:9 + oc])
                    self.stt('dve', u[oc][:, 30:542], pa.v, b1[:, oc:oc + 1], sg.v, ALU.add, ALU.mult)
                for c in range(8):
                    pc = ps[c % 4]
                    for k in range(31):
                        self.mm(pc.v, Dg[c][:, k, :], u[c][:, k:k + 512], start=(k == 0), stop=(k == 30))
                    self.act(vv[c].v, pc.v, AF.Identity, bias=bdw[:, c:c + 1])
                    self.copy('pool', u[c][:, 0:30], u[c][:, 512:542])
                for c in range(8):
                    self.copy('act', vb[c].v, vv[c].v)
                    self.act(sq[c].v, vv[c].v, AF.Square)
                for c in range(8):
                    self.mm(ps[4].v, self.ones_bf.v, vb[c].v, start=(c == 0), stop=(c == 7))
                for c in range(8):
                    self.mm(ps[5].v, self.ones_bf.v, sq[c].v, start=(c == 0), stop=(c == 7))
                self.act(mean.v, ps[4].v, AF.Copy, scale=1.0 / D)
                self.tt('dve', m2.v, mean.v, mean.v, ALU.mult)
                self.stt('dve', var.v, ps[5].v, 1.0 / D, m2.v, ALU.mult, ALU.subtract)
                self.act(std.v, var.v, AF.Ln, bias=self.epsT.v)
                self.act(rstd.v, std.v, AF.Exp, scale=-0.5)
                for c in range(8):
                    eng = 'dve' if c % 2 == 0 else 'pool'
                    self.tt(eng, vv[c].v, vv[c].v, mean.v, ALU.subtract)
                for c in range(8):
                    eng = 'dve' if c % 2 == 0 else 'pool'
                    self.tt(eng, vv[c].v, vv[c].v, rstd.v, ALU.mult)
                for c in range(8):
                    self.act(z[c].v, vv[c].v, AF.Silu, bias=lb[:, c:c + 1], scale=lg[:, c:c + 1])
                self.out_proj(w2, [z[c].v for c in range(8)], xt, ot, xout, t0, bias=b2)

    nbn = 4
    nbo = 0

    def nb(self):
        self._nb = (getattr(self, '_nb', -1) + 1) % self.nbn
        return self.ps[self.nbo + self._nb]

    def head_finalize(self, st_tiles, O, win, hn, gain, gate_func, og, n_in_head):
        sq, std, rstd, rs, tmpo = st_tiles
        ps = self.ps
        for h in range(4):
            for vc in range(2):
                self.act(sq[h * 2 + vc].v, O[h][:, vc, :], AF.Square)
            for vc in range(2):
                self.mm(ps[5].v, self.ones_bf.v, sq[h * 2 + vc].v, start=(vc == 0), stop=(vc == 1))
            self.act(std.v, ps[5].v, AF.Ln, bias=self.epsT.v, scale=1.0 / 256)
            self.act(rstd.v, std.v, AF.Exp, scale=-0.5)
            for vc in range(2):
                j = h * 2 + vc
                pr = self.nb()
                for k in range(8):
                    self.mm(pr.v, win[:, k, 2048 + j * 128:2048 + (j + 1) * 128], hn[k].v, start=(k == 0), stop=(k == 7))
                self.act(rs[j % 2].v, pr.v, gate_func)
                self.stt('dve', tmpo[j % 2].v, O[h][:, vc, :], gain[:, j:j + 1], rstd.v, ALU.mult, ALU.mult)
                self.tt('pool', og[j].v, tmpo[j % 2].v, rs[j % 2].v, ALU.mult)

    def gla_phase(self, xin, xout):
        L = 2
        ps = self.ps
        with ExitStack() as st:
            win = self.load_w(st, 'gla_w_in', 8, 3072)
            wo = self.load_w(st, 'gla_w_o', 8, 1024)
            wa1 = self.load_w(st, 'gla_w_a1', 8, 16)
            wa2f = self.T(st, [17, 512], F32)
            wa2 = self.T(st, [17, 512], BF16)
            triN = self.T(st, [128, 128], F32)
            triU = self.T(st, [128, 128], F32)
            mask = self.T(st, [128, 128], F32)
            hgn = self.T(st, [128, 8], F32)
            onesc = self.T(st, [128, 1], F32)
            self.memset('pool', onesc.v, 1.0)
            for t_, n in ((wa2f, 'gla_w_a2b'), (triN, 'triN'), (triU, 'triU'), (mask, 'bcmask'), (hgn, 'gla_head_norm')):
                self.dma('sp', t_.v, self.din[n])
            self.copy('dve', wa2.v, wa2f.v)
            xt = self.T(st, [128, 8, 512], F32, 'gx')
            hn = [self.T(st, [128, 512], BF16) for _ in range(8)]
            sq = [self.T(st, [128, 512], BF16) for _ in range(8)]
            std = self.T(st, [128, 512], F32)
            rstd = self.T(st, [128, 512], F32)
            g1a = self.T(st, [17, 512], BF16)
            self.memset('pool', g1a.v, 1.0)
            lsp = [self.T(st, [128, 512], F32) for _ in range(4)]
            ez = self.T(st, [128, 512], F32)
            ep = [self.T(st, [128, 512], F32) for _ in range(2)]
            em = [self.T(st, [128, 512], F32) for _ in range(2)]
            eb = [self.T(st, [128, 8], F32) for _ in range(4)]
            qt = [self.T(st, [128, 512], BF16) for _ in range(4)]
            kt = [self.T(st, [128, 512], BF16) for _ in range(4)]
            vtm = [self.T(st, [128, 1024], BF16) for _ in range(4)]
            kd = [self.T(st, [128, 512], BF16) for _ in range(4)]
            erev = [self.T(st, [128, 512], F32) for _ in range(2)]
            attm = [self.T(st, [128, 4, 128], BF16) for _ in range(2)]
            Sst = [self.T(st, [128, 256], F32) for _ in range(4)]
            Sb = [self.T(st, [128, 256], BF16) for _ in range(4)]
            for h in range(4):
                self.memset('pool', Sst[h].v, 0.0)
                self.memset('pool', Sb[h].v, 0.0)
            O = [self.T(st, [128, 2, 512], F32) for _ in range(4)]
            rs = [self.T(st, [128, 512], F32) for _ in range(2)]
            tmpo = [self.T(st, [128, 512], F32) for _ in range(2)]
            og = [self.T(st, [128, 512], BF16) for _ in range(8)]
            ot = [self.T(st, [128, 512], F32) for _ in range(2)]
            poh = [Tile(ps[6 + i // 2].ap[:, (i % 2) * 256:(i % 2) * 256 + 256]) for i in range(4)]
            sc = 128 ** -0.5
            for ti in range(S // 512):
                t0 = ti * 512
                self.load_x_norm(L, xin, t0, xt, hn, sq, std, rstd, ps[5])
                pb = self.nb()
                for k in range(8):
                    self.mm(pb[0:16, :], wa1[:, k, :], hn[k].v, start=(k == 0), stop=(k == 7))
                self.copy('act', g1a[0:16, :], pb[0:16, :])
                for b in range(4):
                    pz = self.nb()
                    self.mm(pz.v, g1a[0:17, b * 128:(b + 1) * 128], wa2.v)
                    self.act(ez.v, pz.v, AF.Exp, scale=-1.0)
                    self.act(lsp[b].v, ez.v, AF.Ln, bias=onesc.v)
                for h in range(4):
                    pc = self.nb()
                    for b in range(4):
                        self.mm(pc[:, b * 128:(b + 1) * 128], lsp[b][:, h * 128:(h + 1) * 128], triN.v)
                    e_p, e_m = ep[h % 2], em[h % 2]
                    self.act(e_p.v, pc.v, AF.Exp)
                    self.act(e_m.v, pc.v, AF.Exp, scale=-1.0)
                    self.copy('pool', eb[h].v, e_p.v.re("p (c s) -> p c s", s=64)[:, :, 63])
                    pq = self.nb()
                    for k in range(8):
                        self.mm(pq.v, win[:, k, h * 128:(h + 1) * 128], hn[k].v, start=(k == 0), stop=(k == 7))
                    self.stt('dve', qt[h].v, pq.v, sc, e_p.v, ALU.mult, ALU.mult)
                    pk = self.nb()
                    for k in range(8):
                        self.mm(pk.v, win[:, k, 512 + h * 128:512 + (h + 1) * 128], hn[k].v, start=(k == 0), stop=(k == 7))
                    self.tt('dve', kt[h].v, pk.v, e_m.v, ALU.mult)
                for b in range(4):
                    bc = slice(b * 128, (b + 1) * 128)
                    for half in range(2):
                        pv = self.nb()
                        for k in range(8):
                            self.mm(pv.v, hn[k][:, bc], win[:, k, 1024 + half * 512:1024 + (half + 1) * 512],
                                    start=(k == 0), stop=(k == 7))
                        self.copy('act' if half else 'dve', vtm[b][:, half * 512:(half + 1) * 512], pv.v)
                    pk2 = self.nb()
                    for k in range(8):
                        self.mm(pk2.v, hn[k][:, bc], win[:, k, 512:1024], start=(k == 0), stop=(k == 7))
                    pr = self.nb()
                    self.mm(pr.v, triU.v, lsp[b].v)
                    er = erev[b % 2]
                    self.act(er.v, pr.v, AF.Exp)
                    self.tt('dve', kd[b].v, pk2.v, er.v, ALU.mult)
                for b in range(4):
                    bc = slice(b * 128, (b + 1) * 128)
                    pa = ps[4]
                    am = attm[b % 2]
                    for h in range(4):
                        self.mm(pa[:, h * 128:(h + 1) * 128], kt[h][:, bc], qt[h][:, bc])
                    self.tt('dve', am.v, pa.v.re("p (h t) -> p h t", h=4), mask.v.ub(1, [128, 4, 128]), ALU.mult)
                    for h in range(4):
                        po = poh[h]
                        for vc in range(2):
                            self.mm(po[:, vc * 128:(vc + 1) * 128], vtm[b][:, h * 256 + vc * 128:h * 256 + (vc + 1) * 128],
                                    am[:, h, :], start=(vc == 0 and h % 2 == 0), stop=False)
                    for X in range(2):
                        rows = slice(X * 64, (X + 1) * 64)
                        cols = slice(b * 128 + X * 64, b * 128 + (X + 1) * 64)
                        cl = b * 2 + X
                        for h in range(4):
                            po = poh[h]
                            for vc in range(2):
                                self.mm(po[:, vc * 128 + X * 64:vc * 128 + (X + 1) * 64], Sb[h][:, vc * 128:(vc + 1) * 128],
                                        qt[h][:, cols], start=False, stop=True)
                        pus = []
                        for h in range(4):
                            pu = self.nb()
                            pus.append(pu)
                            self.mm(pu[:, 0:256], kd[b][rows, h * 128:(h + 1) * 128], vtm[b][rows, h * 256:(h + 1) * 256])
                        for h in range(4):
                            self.stt('dve', Sst[h].v, Sst[h].v, eb[h][:, cl:cl + 1], pus[h][:, 0:256], ALU.mult, ALU.add)
                        for h in range(4):
                            self.copy('act', Sb[h].v, Sst[h].v)
                    for h in range(4):
                        self.copy('act', O[h][:, :, bc], poh[h].v.re("p (v t) -> p v t", v=2))
                self.head_finalize((sq, std, rstd, rs, tmpo), O, win, hn, hgn, AF.Silu, og, 256)
                self.out_proj(wo, [og[j].v for j in range(8)], xt, ot, xout, t0, banks=(4, 5))


    def mlstm_phase(self, xin, xout):
        L = 1
        ps = self.ps
        nc = self.nc
        gi_s = self.dram_scr("ml_gi", (4, S), F32)
        gf_s = self.dram_scr("ml_gf", (4, S), F32)
        em_s = self.dram_scr("ml_em", (4, S), F32)
        wa_s = self.dram_scr("ml_wa", (4, S), F32)
        wc_s = self.dram_scr("ml_wc", (4, 128), F32)
        with ExitStack() as st:
            wif = self.load_w(st, 'mlstm_w_if', 8, 8)
            bif = self.T(st, [4, 16], F32)
            bi15 = self.T(st, [4, 16], F32)
            onesc = self.T(st, [128, 1], F32)
            self.memset('pool', onesc.v, 1.0)
            self.dma('sp', bif.v, self.din['mlstm_b_if'])
            self.ts('dve', bi15.v, bif.v, 1.0 / 15.0, None, ALU.mult)
            xts = [self.T(st, [128, 8, 512], F32) for _ in range(2)]
            hn = [self.T(st, [128, 512], BF16) for _ in range(8)]
            sq = [self.T(st, [128, 512], BF16) for _ in range(8)]
            std = self.T(st, [128, 512], F32)
            rstd = self.T(st, [128, 512], F32)
            t1 = [self.T(st, [4, 512], F32) for _ in range(2)]
            t2 = [self.T(st, [4, 512], F32) for _ in range(2)]
            t3 = [self.T(st, [4, 512], F32) for _ in range(2)]
            li = [self.T(st, [4, 512], F32) for _ in range(2)]
            lf = [self.T(st, [4, 512], F32) for _ in range(2)]
            for ti in range(S // 512):
                t0 = ti * 512
                xt = xts[ti % 2]
                self.load_x_norm(L, xin, t0, xt, hn, sq, std, rstd, ps[5])
                pgi, pgf = self.nb(), self.nb()
                for k in range(8):
                    self.mm(pgi[0:4, :], wif[:, k, 0:4], hn[k].v, start=(k == 0), stop=(k == 7))
                for k in range(8):
                    self.mm(pgf[0:4, :], wif[:, k, 4:8], hn[k].v, start=(k == 0), stop=(k == 7))
                a1, a2, a3, l_i, l_f = t1[ti % 2], t2[ti % 2], t3[ti % 2], li[ti % 2], lf[ti % 2]
                self.act(a1.v, pgi[0:4, :], AF.Tanh, bias=bi15[:, 0:1], scale=1.0 / 15.0)
                self.ts('dve', l_i.v, a1.v, 15.0, None, ALU.mult)
                self.act(a2.v, pgf[0:4, :], AF.Tanh, bias=bi15[:, 1:2], scale=1.0 / 15.0)
                self.act(a3.v, a2.v, AF.Exp, scale=-15.0)
                self.act(a2.v, a3.v, AF.Ln, bias=onesc[0:4, :])
                self.ts('dve', l_f.v, a2.v, -1.0, None, ALU.mult)
                self.dma('sp', gi_s[:, t0:t0 + 512], l_i.v)
                self.dma('sp', gf_s[:, t0:t0 + 512], l_f.v)
        self.p.barrier()
        with ExitStack() as st:
            def t_(shape, dt=F32):
                return self.T(st, shape, dt)
            Li, Lf, onesr, Floc, Fg, a_, Aloc, Ap, tmpA, wa, emx, Ab = [t_([128, 256]) for _ in range(12)]
            lmat, ident, mneg, rb, rowv = [t_([128, 128]) for _ in range(5)]
            Gs, Apre = t_([128, 1]), t_([128, 1])
            Aend, Astart, wc = t_([128, 4]), t_([128, 4]), t_([128, 4])
            self.dma('sp', Li.v, gi_s.rearrange("h (s t) -> (h s) t", t=256))
            self.dma('sp', Lf.v, gf_s.rearrange("h (s t) -> (h s) t", t=256))
            self.dma('sp', lmat.v, self.din['lmat'])
            self.dma('sp', ident.v, self.din['ident'])
            self.dma('sp', mneg.v, self.din['mneg'])
            self.memset('pool', onesr.v, 1.0)
            self.scan(Floc.v, onesr.v, Lf.v, 0.0, ALU.mult, ALU.add)
            self.copy('dve', rb.v, Floc[:, 255:256].bc([128, 128]))
            pg = self.nb()
            self.mm(pg[:, 0:128], lmat.v, rb.v)
            self.copy('act', Gs.v, pg[:, 0:1])
            self.ts('dve', Fg.v, Floc.v, Gs[:, 0:1], None, ALU.add)
            self.tt('dve', a_.v, Li.v, Fg.v, ALU.subtract)
            self.ts('dve', Aloc.v, a_.v, 0.0, None, ALU.max)
            src, dst = Aloc, Ab
            d = 1
            while d < 256:
                self.tt('dve', dst[:, d:256], src[:, d:256], src[:, 0:256 - d], ALU.max)
                self.copy('dve', dst[:, 0:d], src[:, 0:d])
                src, dst = dst, src
                d *= 2
            Aloc = src
            self.copy('dve', rb.v, Aloc[:, 255:256].bc([128, 128]))
            pr = self.nb()
            self.mm(pr[:, 0:128], rb.v, ident.v)
            self.tt('dve', rowv.v, pr[:, 0:128], mneg.v, ALU.add)
            self.p.op('dve', (lambda o_, i_: (lambda e: e.tensor_reduce(o_, i_, mybir.AxisListType.X, ALU.max)))(Apre.ap, rowv.ap),
                      [rowv], [Apre])
            self.ts('dve', Apre.v, Apre.v, 0.0, None, ALU.max)
            self.ts('dve', Ap.v, Aloc.v, Apre[:, 0:1], None, ALU.max)
            self.copy('dve', Aend.v, Ap.v.re("p (j t) -> p j t", t=64)[:, :, 63])
            self.copy('dve', Astart[:, 0:1], Apre.v)
            self.copy('dve', Astart[:, 1:4], Aend[:, 0:3])
            self.tt('dve', wc.v, Astart.v, Aend.v, ALU.subtract)
            self.act(wc.v, wc.v, AF.Exp)
            self.dma('sp', wc_s.rearrange("h (s j) -> (h s) j", j=4), wc.v)
            self.tt('dve', tmpA.v.re("p (j t) -> p j t", t=64), a_.v.re("p (j t) -> p j t", t=64),
                    Aend.v.ub(2, [128, 4, 64]), ALU.subtract)
            self.act(wa.v, tmpA.v, AF.Exp)
            self.dma('pool', wa_s.rearrange("h (s t) -> (h s) t", t=256), wa.v)
            self.tt('dve', tmpA.v.re("p (j t) -> p j t", t=64), Fg.v.re("p (j t) -> p j t", t=64),
                    Aend.v.ub(2, [128, 4, 64]), ALU.add)
            self.act(emx.v, tmpA.v, AF.Exp)
            self.dma('pool', em_s.rearrange("h (s t) -> (h s) t", t=256), emx.v)
        self.p.barrier()
        with ExitStack() as st:
            win = self.load_w(st, 'mlstm_w_in', 8, 3072)
            wo = self.load_w(st, 'mlstm_w_o', 8, 1024)
            mask = self.T(st, [128, 128], F32)
            hgn = self.T(st, [128, 8], F32)
            sel4 = self.T(st, [4, 512], F32)
            ident = self.T(st, [128, 128], F32)
            for t_, n in ((mask, 'bcmask'), (hgn, 'mlstm_head_norm'), (sel4, 'sel4'), (ident, 'ident')):
                self.dma('sp', t_.v, self.din[n])
            xt = self.T(st, [128, 8, 512], F32)
            hn = [self.T(st, [128, 512], BF16) for _ in range(8)]
            sq = [self.T(st, [128, 512], BF16) for _ in range(8)]
            std = self.T(st, [128, 512], F32)
            rstd = self.T(st, [128, 512], F32)
            emT = self.T(st, [4, 512], F32)
            waT = self.T(st, [4, 512], F32)
            wcT = self.T(st, [4, 128], F32)
            watm = self.T(st, [128, 16], F32)
            wcb = self.T(st, [128, 4, 128], F32)
            embc = [self.T(st, [128, 512], F32) for _ in range(2)]
            qs = [self.T(st, [128, 512], BF16) for _ in range(4)]
            kt = [self.T(st, [128, 512], BF16) for _ in range(4)]
            vaug = [self.T(st, [128, 4, 384], BF16) for _ in range(4)]
            for b in range(4):
                self.memset('pool', vaug[b].v, 1.0)
            kw = [self.T(st, [128, 4, 128], BF16) for _ in range(4)]
            qkw = [self.T(st, [128, 4, 128], BF16) for _ in range(2)]
            Sst = [self.T(st, [128, 384], F32) for _ in range(4)]
            Sb = [self.T(st, [128, 384], BF16) for _ in range(4)]
            for h in range(4):
                self.memset('pool', Sst[h].v, 0.0)
            dn = [self.T(st, [128, 128], F32) for _ in range(2)]
            rdn = [self.T(st, [128, 128], F32) for _ in range(2)]
            O = [self.T(st, [128, 2, 512], F32) for _ in range(4)]
            rs = [self.T(st, [128, 512], F32) for _ in range(2)]
            tmpo = [self.T(st, [128, 512], F32) for _ in range(2)]
            og = [self.T(st, [128, 512], BF16) for _ in range(8)]
            ot = [self.T(st, [128, 512], F32) for _ in range(2)]
            sc = 128 ** -0.5
            self.dma('sp', wcT.v, wc_s)
            for h in range(4):
                pwc = self.nb()
                self.mm(pwc[:, 0:128], sel4[0:4, h * 128:(h + 1) * 128], wcT.v)
                self.copy('act', wcb[:, h, :], pwc[:, 0:128])
            for ti in range(S // 512):
                t0 = ti * 512
                self.load_x_norm(L, xin, t0, xt, hn, sq, std, rstd, ps[5])
                self.dma('sp', emT.v, em_s[:, t0:t0 + 512])
                self.dma('sp', waT.v, wa_s[:, t0:t0 + 512])
                pw = self.nb()
                for b in range(4):
                    self.mm(pw[:, b * 128:(b + 1) * 128], waT[0:4, b * 128:(b + 1) * 128], ident[0:4, 0:128])
                self.ts('dve', watm.v.re("p (b h) -> p b h", h=4), pw.v.re("p (b n) -> p b n", n=128)[:, :, 0:4], sc, None, ALU.mult)
                for h in range(4):
                    pe_ = self.nb()
                    self.mm(pe_.v, sel4[0:4, h * 128:(h + 1) * 128], emT.v)
                    eb_ = embc[h % 2]
                    self.copy('act', eb_.v, pe_.v)
                    pq = self.nb()
                    for k in range(8):
                        self.mm(pq.v, win[:, k, h * 128:(h + 1) * 128], hn[k].v, start=(k == 0), stop=(k == 7))
                    self.tt('dve', qs[h].v, pq.v, eb_.v, ALU.mult)
                    pk = self.nb()
                    for k in range(8):
                        self.mm(pk.v, win[:, k, 512 + h * 128:512 + (h + 1) * 128], hn[k].v, start=(k == 0), stop=(k == 7))
                    self.copy('act', kt[h].v, pk.v)
                for b in range(4):
                    bc = slice(b * 128, (b + 1) * 128)
                    for half in range(2):
                        pv = self.nb()
                        for k in range(8):
                            self.mm(pv.v, hn[k][:, bc], win[:, k, 1024 + half * 512:1024 + (half + 1) * 512],
                                    start=(k == 0), stop=(k == 7))
                        self.copy('act' if half else 'dve', vaug[b][:, 2 * half:2 * half + 2, 0:256],
                                  pv.v.re("p (h v) -> p h v", h=2))
                    pk2 = self.nb()
                    for k in range(8):
                        self.mm(pk2.v, hn[k][:, bc], win[:, k, 512:1024], start=(k == 0), stop=(k == 7))
                    self.tt('dve', kw[b].v, pk2.v.re("p (h d) -> p h d", h=4),
                            watm[:, b * 4:(b + 1) * 4].ub(2, [128, 4, 128]), ALU.mult)
                for b in range(4):
                    bc = slice(b * 128, (b + 1) * 128)
                    pa = self.nb()
                    qk_ = qkw[b % 2]
                    for h in range(4):
                        self.mm(pa[:, h * 128:(h + 1) * 128], kt[h][:, bc], qs[h][:, bc])
                    for h in range(4):
                        self.stt('dve', qk_[:, h, :], pa[:, h * 128:(h + 1) * 128], watm[:, b * 4 + h:b * 4 + h + 1],
                                 mask.v, ALU.mult, ALU.mult)
                    for h in range(4):
                        po = ps[4 + h]
                        for j in range(3):
                            self.mm(po[:, j * 128:(j + 1) * 128], vaug[b][:, h, j * 128:(j + 1) * 128], qk_[:, h, :],
                                    start=(j == 0), stop=False)
                    for X in range(2):
                        rows = slice(X * 64, (X + 1) * 64)
                        cols = slice(b * 128 + X * 64, b * 128 + (X + 1) * 64)
                        cl = b * 2 + X
                        for h in range(4):
                            self.act(Sb[h].v, Sst[h].v, AF.Copy, scale=wcb[:, h, ti * 8 + cl:ti * 8 + cl + 1])
                        for h in range(4):
                            po = ps[4 + h]
                            for j in range(3):
                                self.mm(po[:, j * 128 + X * 64:j * 128 + (X + 1) * 64], Sb[h][:, j * 128:(j + 1) * 128],
                                        qs[h][:, cols], start=False, stop=True)
                        pus = []
                        for h in range(4):
                            pu = self.nb()
                            pus.append(pu)
                            self.mm(pu[:, 0:384], kw[b][rows, h, :], vaug[b][rows, h, :])
                        for h in range(4):
                            self.stt('dve', Sst[h].v, Sst[h].v, wcb[:, h, ti * 8 + cl:ti * 8 + cl + 1], pus[h][:, 0:384],
                                     ALU.mult, ALU.add)
                    for h in range(4):
                        po = ps[4 + h]
                        d_, r_ = dn[h % 2], rdn[h % 2]
                        self.act(d_.v, po[:, 256:384], AF.Abs)
                        self.ts('dve', d_.v, d_.v, 1.0, None, ALU.max)
                        self.act(r_.v, d_.v, AF.Ln)
                        self.act(r_.v, r_.v, AF.Exp, scale=-1.0)
                        self.tt('dve', O[h][:, :, bc], po[:, 0:256].re("p (v t) -> p v t", v=2),
                                r_.v.ub(1, [128, 2, 128]), ALU.mult)
                self.head_finalize((sq, std, rstd, rs, tmpo), O, win, hn, hgn, AF.Sigmoid, og, 256)
                self.out_proj(wo, [og[j].v for j in range(8)], xt, ot, xout, t0, banks=(4, 5))


    def normrope(self, tl, x_pre, cos_t, sin_t, g, Rg, bias_ap, scale, out):
        sq96, xb, std96, rstd96, t1, t2 = tl
        self.act(sq96.v, x_pre.v, AF.Square)
        pss = self.nb()
        self.mm(pss[0:96, :], self.ones_bf[0:96, 0:96], sq96.v)
        self.act(std96.v, pss[0:96, :], AF.Ln, bias=bias_ap, scale=scale)
        self.act(rstd96.v, std96.v, AF.Exp, scale=-0.5)
        self.copy('act', xb.v, x_pre.v)
        prot = self.nb()
        self.mm(prot[0:96, :], Rg.v, xb.v)
        self.stt('dve', t1.v, x_pre.v, g[:, 0:1], cos_t.v, ALU.mult, ALU.mult)
        self.tt('dve', t2.v, prot[0:96, :], sin_t.v, ALU.mult)
        self.tt('pool', t1.v, t1.v, t2.v, ALU.add)
        self.tt('pool', out, t1.v, rstd96.v, ALU.mult)

    def mla_phase(self, xin, xout):
        L = 0
        ps = self.ps
        kr_s = self.dram_scr("mla_kr", (32, S), F32)
        o_s = self.dram_scr("mla_o", (D, S), BF16)
        cosd, sind = self.din['rope_cos'], self.din['rope_sin']
        with ExitStack() as st0:
            cqn = [self.T(st0, [128, S], BF16, 'cqn') for _ in range(3)]
            ckvn = [self.T(st0, [128, S], BF16, 'ckvn') for _ in range(2)]
            with ExitStack() as st:
                wdq = self.load_w(st, 'mla_w_dq', 8, 384)
                wdkv = self.load_w(st, 'mla_w_dkv', 8, 288)
                qn = self.T(st, [128, 3], F32)
                kvn = self.T(st, [128, 2], F32)
                self.dma('sp', qn.v, self.din['mla_q_norm'])
                self.dma('sp', kvn.v, self.din['mla_kv_norm'])
                xts = [self.T(st, [128, 8, 512], F32) for _ in range(2)]
                hn = [self.T(st, [128, 512], BF16) for _ in range(8)]
                sq = [self.T(st, [128, 512], BF16) for _ in range(8)]
                std = self.T(st, [128, 512], F32)
                rstd = self.T(st, [128, 512], F32)
                std2 = self.T(st, [128, 512], F32)
                rstd2 = self.T(st, [128, 512], F32)
                krt = [self.T(st, [32, 512], F32) for _ in range(2)]
                for ti in range(S // 512):
                    t0 = ti * 512
                    tc_ = slice(t0, t0 + 512)
                    xt = xts[ti % 2]
                    self.load_x_norm(L, xin, t0, xt, hn, sq, std, rstd, ps[7])
                    for j in range(3):
                        for k in range(8):
                            self.mm(ps[j].v, wdq[:, k, j * 128:(j + 1) * 128], hn[k].v, start=(k == 0), stop=(k == 7))
                    self.rmsnorm([ps[j].v for j in range(3)], [qn[:, j:j + 1] for j in range(3)],
                                 [cqn[j][:, tc_] for j in range(3)], [sq[j].v for j in range(3)],
                                 ps[3].v, std2.v, rstd2.v, 384, 512)
                    for j in range(2):
                        for k in range(8):
                            self.mm(ps[4 + j].v, wdkv[:, k, j * 128:(j + 1) * 128], hn[k].v, start=(k == 0), stop=(k == 7))
                    self.rmsnorm([ps[4 + j].v for j in range(2)], [kvn[:, j:j + 1] for j in range(2)],
                                 [ckvn[j][:, tc_] for j in range(2)], [sq[3 + j].v for j in range(2)],
                                 ps[6].v, std2.v, rstd2.v, 256, 512)
                    pk = ps[7]
                    for k in range(8):
                        self.mm(pk[0:32, :], wdkv[:, k, 256:288], hn[k].v, start=(k == 0), stop=(k == 7))
                    kr = krt[ti % 2]
                    self.copy('act', kr.v, pk[0:32, :])
                    self.dma('sp', kr_s[:, tc_], kr.v)
            self.p.barrier()
            with ExitStack() as st:
                wuq = self.load_w(st, 'mla_w_uq', 3, 1536)
                wukv = self.load_w(st, 'mla_w_ukv', 2, 2048)
                qg = self.T(st, [96, 16], F32)
                kg = self.T(st, [96, 16], F32)
                Rf = self.T(st, [96, 96], F32)
                Rq = self.T(st, [96, 96], BF16)
                Rk = self.T(st, [96, 96], BF16)
                epsq = self.T(st, [96, 1], F32)
                self.memset('pool', epsq.v, EPS * 96.0)
                self.dma('sp', qg.v, self.din['mla_qg'])
                self.dma('sp', kg.v, self.din['mla_kg'])
                self.dma('sp', Rf.v, self.din['rope_R'])
                self.ts('dve', Rq.v, Rf.v, qg[:, 0:1], None, ALU.mult)
                self.ts('dve', Rk.v, Rf.v, kg[:, 0:1], None, ALU.mult)
                KTs = [self.T(st, [96, S], BF16, 'KT') for _ in range(2)]
                VAs = [self.T(st, [128, 64, 128], BF16, 'VA') for _ in range(2)]
                for v_ in VAs:
                    self.memset('pool', v_.v, 1.0)
                kp = [self.T(st, [96, 512], F32) for _ in range(2)]
                cs = [self.T(st, [96, 512], F32) for _ in range(2)]
                sn = [self.T(st, [96, 512], F32) for _ in range(2)]
                def mk_tl():
                    return dict(sq=self.T(st, [96, 512], BF16), xb=self.T(st, [96, 512], BF16),
                                rs=self.T(st, [96, 512], F32), t1=self.T(st, [96, 512], F32),
                                t2=self.T(st, [96, 512], F32))
                tlq, tlk = mk_tl(), mk_tl()
                Qf = [self.T(st, [96, 512], BF16) for _ in range(2)]
                qpre = self.T(st, [96, 512], F32)
                Pt = [self.T(st, [128, 1024], BF16) for _ in range(3)]
                rsum = [self.T(st, [64, 512], F32) for _ in range(1)]
                oh = [self.T(st, [64, 512], BF16) for _ in range(2)]
                self.nbn = 2
                self.nbo = 4
                cnt = {'ci': 0, 'npt': 0}

                def nr_stages(tl_, x_pre, c_t, s_t, g, Rg, bias_ap, scale, out):
                    hold = {}

                    def s_a():
                        self.tt('dve', tl_['sq'].v, x_pre.v, x_pre.v, ALU.mult)
                        self.copy('dve', tl_['xb'].v, x_pre.v)

                    def s_b():
                        hold['pss'] = self.nb()
                        self.mm(hold['pss'][0:96, :], self.ones_bf[0:96, 0:96], tl_['sq'].v)

                    def s_c():
                        self.act(tl_['rs'].v, hold['pss'][0:96, :], AF.Ln, bias=bias_ap, scale=scale)
                        self.act(tl_['rs'].v, tl_['rs'].v, AF.Exp, scale=-0.5)

                    def s_d():
                        hold['prot'] = self.nb()
                        self.mm(hold['prot'][0:96, :], Rg.v, tl_['xb'].v)

                    def s_e():
                        self.stt('dve', tl_['t1'].v, x_pre.v, g[:, 0:1], c_t.v, ALU.mult, ALU.mult)
                        self.tt('dve', tl_['t2'].v, hold['prot'][0:96, :], s_t.v, ALU.mult)

                    def s_f():
                        self.tt('dve', tl_['t1'].v, tl_['t1'].v, tl_['t2'].v, ALU.add)
                        self.tt('pool', out, tl_['t1'].v, tl_['rs'].v, ALU.mult)
                    return [s_a, s_b, s_c, s_d, s_e, s_f]

                def kgen_stages(h, ti):
                    KT, VA = KTs[h % 2], VAs[h % 2]
                    t0 = ti * 512
                    tc_ = slice(t0, t0 + 512)
                    ci = cnt['ci']
                    cnt['ci'] += 1
                    kpre, c_t, s_t = kp[ci % 2], cs[ci % 2], sn[ci % 2]
                    hold = {}

                    def k1():
                        hold['pk'] = self.nb()
                        for j in range(2):
                            self.mm(hold['pk'][0:64, :], wukv[:, j, h * 128:h * 128 + 64], ckvn[j][:, tc_],
                                    start=(j == 0), stop=(j == 1))
                        self.dma('sp', kpre[64:96, :], kr_s[:, tc_])
                        self.dma('sp', c_t.v, cosd[:, tc_])
                        self.dma('sp', s_t.v, sind[:, tc_])

                    def k2():
                        self.copy('dve', kpre[0:64, :], hold['pk'][0:64, :])

                    def k9():
                        hold['pv'] = self.nb()
                        for b in range(4):
                            for j in range(2):
                                self.mm(hold['pv'][:, b * 64:(b + 1) * 64], ckvn[j][:, t0 + b * 128:t0 + (b + 1) * 128],
                                        wukv[:, j, h * 128 + 64:h * 128 + 128], start=(j == 0), stop=(j == 1))

                    def k10():
                        self.copy('dve', VA[:, ti * 4:(ti + 1) * 4, 0:64], hold['pv'][:, 0:256].re("p (b v) -> p b v", b=4))
                    return [k1, k2] + nr_stages(tlk, kpre, c_t, s_t, kg, Rk, self.epsT[0:96, :], 1.0 / 96.0, KT[:, tc_]) + [k9, k10]

                def qgen_stages(h, qi):
                    qc_ = slice(qi * 512, qi * 512 + 512)
                    ci = cnt['ci']
                    cnt['ci'] += 1
                    c_t, s_t = cs[ci % 2], sn[ci % 2]
                    qf = Qf[(h * 16 + qi) % 2]
                    hold = {}

                    def q1():
                        hold['pq'] = self.nb()
                        for j in range(3):
                            self.mm(hold['pq'][0:96, :], wuq[:, j, h * 96:(h + 1) * 96], cqn[j][:, qc_], start=(j == 0), stop=(j == 2))
                        self.dma('sp', c_t.v, cosd[:, qc_])
                        self.dma('sp', s_t.v, sind[:, qc_])

                    def q2():
                        self.copy('dve', qpre.v, hold['pq'][0:96, :])
                    return [q1, q2] + nr_stages(tlq, qpre, c_t, s_t, qg, Rq, epsq.v, 1.0, qf.v)

                def attn(h, qi, pending):
                    KT, VA = KTs[h % 2], VAs[h % 2]
                    qc_ = slice(qi * 512, qi * 512 + 512)
                    qf = Qf[(h * 16 + qi) % 2]
                    po = ps[6 + qi % 2]
                    nkb = 4 * qi + 4
                    npairs = nkb // 2
                    pend = []
                    for pp in range(npairs):
                        pst = self.ps2[cnt['npt'] % 2]
                        P = Pt[cnt['npt'] % 3]
                        cnt['npt'] += 1
                        for hf in range(2):
                            kb = 2 * pp + hf
                            self.mm(pst[:, hf * 512:(hf + 1) * 512], KT[:, kb * 128:(kb + 1) * 128], qf.v)
                        self.act(P.v, pst.v, AF.Exp)
                        for hf in range(2):
                            kb = 2 * pp + hf
                            kl = kb - 4 * qi
                            if kl >= 0:
                                if kl > 0:
                                    self.memset('pool', P[:, hf * 512:hf * 512 + 128 * kl], 0.0)
                                self.memset('pool', P[64:128, hf * 512 + 128 * kl:hf * 512 + 128 * kl + 64], 0.0)
                        pend.append((pp, P))
                        if len(pend) > 2:
                            pp_, P_ = pend.pop(0)
                            for hf in range(2):
                                kb_ = 2 * pp_ + hf
                                self.mm(po.v, VA[:, kb_, :], P_[:, hf * 512:(hf + 1) * 512], start=(kb_ == 0), stop=False)
                        left = npairs - pp
                        nst = -(-len(pending) // left)
                        for _ in range(nst):
                            if pending:
                                pending.pop(0)()
                    for (pp_, P_) in pend:
                        for hf in range(2):
                            kb_ = 2 * pp_ + hf
                            self.mm(po.v, VA[:, kb_, :], P_[:, hf * 512:(hf + 1) * 512], start=(kb_ == 0), stop=(kb_ == nkb - 1))
                    while pending:
                        pending.pop(0)()
                    rs_, oh_ = rsum[0], oh[qi % 2]
                    self.act(rs_.v, po[64:128, :], AF.Ln)
                    self.act(rs_.v, rs_.v, AF.Exp, scale=-1.0)
                    self.tt('dve', oh_.v, po[0:64, :], rs_.v, ALU.mult)
                    self.dma('sp', o_s[h * 64:(h + 1) * 64, qc_], oh_.v)

                for ti in range(16):
                    for f_ in kgen_stages(0, ti):
                        f_()
                for f_ in qgen_stages(0, 0):
                    f_()
                for h in range(16):
                    for qi in range(16):
                        pending = []
                        qs_ = ks_ = []
                        if qi + 1 < 16:
                            qs_ = qgen_stages(h, qi + 1)
                        elif h + 1 < 16:
                            qs_ = qgen_stages(h + 1, 0)
                        if h + 1 < 16:
                            ks_ = kgen_stages(h + 1, qi)
                        qs_, ks_ = list(qs_), list(ks_)
                        while qs_ or ks_:
                            if qs_:
                                pending.append(qs_.pop(0))
                            if ks_:
                                pending.append(ks_.pop(0))
                        attn(h, qi, pending)
                self.nbn = 4
                self.nbo = 0
        self.p.barrier()
        with ExitStack() as st:
            wo = self.load_w(st, 'mla_w_o', 8, 1024)
            xts = [self.T(st, [128, 8, 512], F32) for _ in range(2)]
            ots = [self.T(st, [128, 8, 512], BF16) for _ in range(2)]
            ot = [self.T(st, [128, 512], F32) for _ in range(2)]
            xv = xin.rearrange("(c p) t -> p c t", p=128)
            ov = o_s.rearrange("(c p) t -> p c t", p=128)
            for ti in range(S // 512):
                t0 = ti * 512
                xt, o_t = xts[ti % 2], ots[ti % 2]
                self.dma('sp', xt.v, xv[:, :, t0:t0 + 512])
                self.dma('sp', o_t.v, ov[:, :, t0:t0 + 512])
                self.out_proj(wo, [o_t[:, j, :] for j in range(8)], xt, ot, xout, t0)


def _colmajor(v, n):
    return np.ascontiguousarray(np.asarray(v, np.float32).reshape(n // 128, 128).T)


def _host_inputs(inputs, layers):
    m = {}
    m["norm_mix"] = np.ascontiguousarray(
        np.asarray(inputs["norm_mix"], np.float32).reshape(4, 8, 128).transpose(2, 0, 1).reshape(128, 32))
    m["norm_ffn"] = np.ascontiguousarray(
        np.asarray(inputs["norm_ffn"], np.float32).reshape(4, 8, 128).transpose(2, 0, 1).reshape(128, 32))
    for L in layers:
        for n in LAYER_W[L]:
            m[n] = np.ascontiguousarray(np.asarray(inputs[n], np.float32)[0])
        for n, ln in LAYER_V[L]:
            m[n] = _colmajor(inputs[n][0], ln)
        m["ffn_w1_%d" % L] = np.ascontiguousarray(np.asarray(inputs["ffn_w1"], np.float32)[L])
        m["ffn_w2_%d" % L] = np.ascontiguousarray(np.asarray(inputs["ffn_w2"], np.float32)[L])
    idx = np.arange(128)
    same = (idx[:, None] // 64) == (idx[None, :] // 64)
    if 1 in layers or 2 in layers:
        m["bcmask"] = (same & (idx[:, None] <= idx[None, :])).astype(np.float32)
    if 0 in layers:
        m["mla_qg"] = np.ascontiguousarray(np.repeat(np.asarray(inputs["mla_q_gain"], np.float32)[0].reshape(96, 1), 16, axis=1))
        m["mla_kg"] = np.ascontiguousarray(np.repeat(np.asarray(inputs["mla_k_gain"], np.float32)[0].reshape(96, 1), 16, axis=1))
        inv = (10000.0 ** (-np.arange(16, dtype=np.float32) / 16.0)).astype(np.float32)
        ang = np.arange(S, dtype=np.float32)[None, :] * inv[:, None]
        cos = np.ones((96, S), np.float32)
        sin = np.zeros((96, S), np.float32)
        cos[64:80] = np.cos(ang)
        cos[80:96] = np.cos(ang)
        sin[64:80] = np.sin(ang)
        sin[80:96] = np.sin(ang)
        m["rope_cos"] = cos
        m["rope_sin"] = sin
        R = np.zeros((96, 96), np.float32)
        for i in range(16):
            R[80 + i, 64 + i] = -1.0
            R[64 + i, 80 + i] = 1.0
        m["rope_R"] = R
    if 1 in layers:
        b = np.asarray(inputs["mlstm_b_if"], np.float32)[0]
        m["mlstm_b_if"] = np.ascontiguousarray(np.tile(np.stack([b[0:4], b[4:8]], axis=1), (1, 8)))
        sel = np.zeros((4, 512), np.float32)
        for h in range(4):
            sel[h, h * 128:(h + 1) * 128] = 1.0
        m["sel4"] = sel
        m["lmat"] = (((idx[:, None] // 32) == (idx[None, :] // 32)) & (idx[:, None] < idx[None, :])).astype(np.float32)
        m["ident"] = np.eye(128, dtype=np.float32)
        m["mneg"] = np.where(((idx[:, None] // 32) == (idx[None, :] // 32)) & (idx[None, :] < idx[:, None]), 0.0, -1e30).astype(np.float32)
    if 2 in layers:
        m["gla_w_a2b"] = np.ascontiguousarray(np.concatenate(
            [np.asarray(inputs["gla_w_a2"], np.float32)[0], np.asarray(inputs["gla_b_a"], np.float32)[0][None, :]], axis=0))
        m["triN"] = (same & (idx[:, None] <= idx[None, :])).astype(np.float32) * (-1.0 / 16.0)
        m["triU"] = (same & (idx[:, None] > idx[None, :])).astype(np.float32) * (-1.0 / 16.0)
    if 3 in layers:
        w = np.asarray(inputs["conv_w_dw"], np.float32)[0]
        m["conv_w_dw"] = np.ascontiguousarray(w.reshape(31, 8, 128).transpose(2, 1, 0).reshape(128, 8 * 31))
        m["identc"] = np.eye(128, dtype=np.float32)
    return m


_NC_CACHE = {}


def run_layers(layers, xT_list, inputs, skip_ffn=False):
    key = (tuple(layers), skip_ffn)
    if key not in _NC_CACHE:
        _NC_CACHE[key] = KB(list(layers), skip_ffn).build()
    nc = _NC_CACHE[key]
    shared = _host_inputs(inputs, layers)
    in_maps = []
    for xT in xT_list:
        mm = dict(shared)
        mm["xT"] = xT
        in_maps.append(mm)
    res = run_bass_kernel_spmd(nc, in_maps, core_ids=list(range(len(xT_list))))
    return [r["yT"] for r in res.results]


def kernel(**inputs):
    x = np.asarray(inputs["x"], np.float32)
    B = x.shape[0]
    xT = [np.ascontiguousarray(x[b % B].T) for b in range(8)]
    outs = run_layers((0, 1, 2, 3), xT, inputs)
    y = np.stack([np.ascontiguousarray(outs[b].T) for b in range(B)], axis=0)
    return y.astype(np.float32)
```

```python
import math
import numpy as np
from contextlib import ExitStack
import concourse.bass as bass
import concourse.mybir as mybir
from concourse.bass_utils import run_bass_kernel_spmd

F32 = mybir.dt.float32
BF16 = mybir.dt.bfloat16
AF = mybir.ActivationFunctionType
ALU = mybir.AluOpType

S = 8192
D = 1024
EPS = 1e-6
COMPUTE = ('pe', 'act', 'dve', 'pool')
ALLENG = ('pe', 'act', 'dve', 'pool', 'sp')
DMA_POOL = 8


class Tile:
    __slots__ = ('ap', 'w', 'r')

    def __init__(self, ap):
        self.ap = ap
        self.w = None
        self.r = []

    def __getitem__(self, k):
        return V(self, self.ap[k])

    @property
    def v(self):
        return V(self, self.ap)


class V:
    __slots__ = ('t', 'ap')

    def __init__(self, t, ap):
        self.t = t
        self.ap = ap

    def __getitem__(self, k):
        return V(self.t, self.ap[k])

    def bc(self, shape):
        return V(self.t, self.ap.to_broadcast(shape))

    def re(self, pat, **kw):
        return V(self.t, self.ap.rearrange(pat, **kw))

    def ub(self, axis, shape):
        return V(self.t, self.ap.unsqueeze(axis).to_broadcast(shape))


def _ap(x):
    return x.ap if isinstance(x, V) else x


def _tl(*xs):
    return [x.t for x in xs if isinstance(x, V)]


class Prog:
    def __init__(self, nc):
        self.nc = nc
        self.ops = {e: [] for e in ALLENG}
        self.ndma = {e: 0 for e in ALLENG}
        self.lastc = {e: None for e in ALLENG}
        self.dmas = {e: [] for e in ALLENG}

    def op(self, eng, fn, reads=(), writes=(), dma=False):
        deps = set()
        for t in reads:
            if t.w is not None:
                deps.add(t.w)
        for t in writes:
            if t.w is not None:
                deps.add(t.w)
            deps.update(t.r)
        me = (eng, len(self.ops[eng]))
        rec = dict(fn=fn, deps=deps, dma=dma, inc=False, val=None, sem=None)
        if dma:
            rec['dj'] = self.ndma[eng]
            self.ndma[eng] += 1
            self.dmas[eng].append(me)
        else:
            self.lastc[eng] = me
        self.ops[eng].append(rec)
        for t in reads:
            t.r.append(me)
        for t in writes:
            t.w = me
            t.r = []
        return me

    def barrier(self):
        deps = set()
        for e in ALLENG:
            if self.lastc[e] is not None:
                deps.add(self.lastc[e])
            deps.update(self.dmas[e][-DMA_POOL:])
        for e in ALLENG:
            self.ops[e].append(dict(fn=None, deps=set(deps), dma=False, inc=False, val=None, sem=None))

    def emit(self, stack):
        nc = self.nc
        ops = self.ops
        for e in ALLENG:
            for o in ops[e]:
                nd = set()
                for (pe_, pi) in o['deps']:
                    p = ops[pe_][pi]
                    if not p['dma']:
                        if pe_ == e and e == 'pe' and not o['dma'] and o['fn'] is not None:
                            continue
                        p['inc'] = True
                    nd.add((pe_, pi))
                o['deps'] = nd
        sems = {e: stack.enter_context(nc.semaphore('s_' + e)) for e in COMPUTE}
        dsems = {}
        for e in ALLENG:
            if self.ndma[e] > 0:
                dsems[e] = [stack.enter_context(nc.semaphore('d_%s_%d' % (e, k))) for k in range(DMA_POOL)]
        for e in ALLENG:
            cnt = 0
            for o in ops[e]:
                if o['dma']:
                    j = o['dj']
                    o['sem'] = dsems[e][j % DMA_POOL]
                    o['val'] = 16 * (j // DMA_POOL + 1)
                elif o['inc']:
                    cnt += 1
                    o['sem'] = sems[e]
                    o['val'] = cnt
        block = stack.enter_context(nc.Block())

        def run(e, engobj):
            waited = {}
            for o in ops[e]:
                need = {}
                for (pe_, pi) in o['deps']:
                    p = ops[pe_][pi]
                    s = p['sem']
                    if waited.get(s.num, 0) >= p['val']:
                        continue
                    if s.num not in need or need[s.num][1] < p['val']:
                        need[s.num] = (s, p['val'])
                if o['dma'] and o['val'] > 16:
                    s = o['sem']
                    v = o['val'] - 16
                    if waited.get(s.num, 0) < v and (s.num not in need or need[s.num][1] < v):
                        need[s.num] = (s, v)
                for key, (s, v) in need.items():
                    engobj.wait_ge(s, v)
                    waited[key] = v
                if o['fn'] is None:
                    continue
                ins = o['fn'](engobj)
                if o['dma']:
                    ins.then_inc(o['sem'], 16)
                elif o['inc']:
                    ins.then_inc(o['sem'], 1)
            n = self.ndma[e]
            for k in range(min(n, DMA_POOL)):
                cntk = (n - 1 - k) // DMA_POOL + 1
                if waited.get(dsems[e][k].num, 0) < 16 * cntk:
                    engobj.wait_ge(dsems[e][k], 16 * cntk)

        @block.tensor
        def _(pe):
            run('pe', pe)

        @block.scalar
        def _(act):
            run('act', act)

        @block.vector
        def _(dve):
            run('dve', dve)

        @block.gpsimd
        def _(pool):
            run('pool', pool)

        @block.sync
        def _(sp):
            run('sp', sp)


LAYER_W = {
    0: ['mla_w_dq', 'mla_w_uq', 'mla_w_dkv', 'mla_w_ukv', 'mla_w_o'],
    1: ['mlstm_w_in', 'mlstm_w_if', 'mlstm_w_o'],
    2: ['gla_w_in', 'gla_w_a1', 'gla_w_o'],
    3: ['conv_w_pw1', 'conv_w_pw2'],
}
WSHAPE = {
    'mla_w_dq': (1024, 384), 'mla_w_uq': (384, 1536), 'mla_w_dkv': (1024, 288), 'mla_w_ukv': (256, 2048),
    'mla_w_o': (1024, 1024), 'mlstm_w_in': (1024, 3072), 'mlstm_w_if': (1024, 8), 'mlstm_w_o': (1024, 1024),
    'gla_w_in': (1024, 3072), 'gla_w_a1': (1024, 16), 'gla_w_o': (1024, 1024),
    'conv_w_pw1': (1024, 2048), 'conv_w_pw2': (1024, 1024),
}
LAYER_V = {
    0: [('mla_q_norm', 384), ('mla_kv_norm', 256)],
    1: [('mlstm_head_norm', 1024)],
    2: [('gla_head_norm', 1024)],
    3: [('conv_b_pw1', 2048), ('conv_b_dw', 1024), ('conv_ln_g', 1024), ('conv_ln_b', 1024), ('conv_b_pw2', 1024)],
}


class KB:
    def __init__(self, layers, skip_ffn=False):
        self.layers = layers
        self.skip_ffn = skip_ffn
        self.nc = bass.Bass("TRN2", target_bir_lowering=False)
        self.p = Prog(self.nc)
        self.gst = ExitStack()
        self.din = {}
        self.uid = 0

    def dram_in(self, name, shape, dt=F32):
        a = self.nc.dram_tensor(name, list(shape), dt, kind="ExternalInput").ap()
        self.din[name] = a
        return a

    def dram_scr(self, name, shape, dt):
        return self.nc.dram_tensor(name, list(shape), dt, kind="Internal").ap()

    def sb(self, st, shape, dt=F32, name=None):
        self.uid += 1
        return st.enter_context(self.nc.sbuf_tensor("%s_%d" % (name or 'sb', self.uid), list(shape), dt))[:]

    def T(self, st, shape, dt=F32, name=None):
        return Tile(self.sb(st, shape, dt, name))

    def mm(self, out, lhsT, rhs, start=True, stop=True):
        o, l, r = _ap(out), _ap(lhsT), _ap(rhs)
        self.p.op('pe', lambda e: e.matmul(o, l, r, start=start, stop=stop), _tl(lhsT, rhs), _tl(out))

    def act(self, out, in_, func, bias=None, scale=None, eng='act'):
        o, i = _ap(out), _ap(in_)
        kw = {}
        if bias is not None:
            kw['bias'] = _ap(bias)
        if scale is not None:
            kw['scale'] = _ap(scale)
        self.p.op('act', lambda e: e.activation(o, i, func, **kw), _tl(in_, bias, scale), _tl(out))

    def tt(self, eng, out, a, b, op):
        o, x, y = _ap(out), _ap(a), _ap(b)
        self.p.op(eng, lambda e: e.tensor_tensor(o, x, y, op), _tl(a, b), _tl(out))

    def stt(self, eng, out, in0, scalar, in1, op0, op1):
        o, x, s, y = _ap(out), _ap(in0), _ap(scalar), _ap(in1)
        self.p.op(eng, lambda e: e.scalar_tensor_tensor(o, x, s, y, op0, op1), _tl(in0, scalar, in1), _tl(out))

    def ts(self, eng, out, in0, s1, s2, op0, op1=None):
        o, x, a, b = _ap(out), _ap(in0), _ap(s1), _ap(s2)
        if op1 is None:
            self.p.op(eng, lambda e: e.tensor_scalar(o, x, a, None, op0), _tl(in0, s1), _tl(out))
        else:
            self.p.op(eng, lambda e: e.tensor_scalar(o, x, a, b, op0, op1), _tl(in0, s1, s2), _tl(out))

    def copy(self, eng, out, in_):
        o, i = _ap(out), _ap(in_)
        if eng == 'act':
            self.p.op('act', lambda e: e.activation(o, i, AF.Copy), _tl(in_), _tl(out))
        else:
            self.p.op(eng, lambda e: e.tensor_copy(o, i), _tl(in_), _tl(out))

    def memset(self, eng, out, val):
        o = _ap(out)
        self.p.op(eng, lambda e: e.memset(o, val), (), _tl(out))

    def recip(self, out, in_):
        o, i = _ap(out), _ap(in_)
        self.p.op('dve', lambda e: e.reciprocal(o, i), _tl(in_), _tl(out))

    def scan(self, out, d0, d1, init, op0, op1):
        o, a, b = _ap(out), _ap(d0), _ap(d1)
        self.p.op('dve', lambda e: e.tensor_tensor_scan(o, a, b, init, op0, op1), _tl(d0, d1), _tl(out))

    def dma(self, q, out, in_, **kw):
        o, i = _ap(out), _ap(in_)
        self.p.op(q, lambda e: e.dma_start(out=o, in_=i, **kw), _tl(in_), _tl(out), dma=True)

    def build(self):
        nc, p = self.nc, self.p
        layers = self.layers
        gst = self.gst
        xin = self.dram_in("xT", (D, S))
        yout = nc.dram_tensor("yT", [D, S], F32, kind="ExternalOutput").ap()
        nmix = self.dram_in("norm_mix", (128, 32))
        nffn = self.dram_in("norm_ffn", (128, 32))
        for L in layers:
            for n in LAYER_W[L]:
                self.dram_in(n, WSHAPE[n])
            for n, ln in LAYER_V[L]:
                self.dram_in(n, (128, ln // 128))
            self.dram_in("ffn_w1_%d" % L, (D, 4096))
            self.dram_in("ffn_w2_%d" % L, (4096, D))
        if 0 in layers:
            self.dram_in("mla_qg", (96, 16))
            self.dram_in("mla_kg", (96, 16))
            self.dram_in("rope_cos", (96, S))
            self.dram_in("rope_sin", (96, S))
            self.dram_in("rope_R", (96, 96))
        if 1 in layers:
            self.dram_in("mlstm_b_if", (4, 16))
            self.dram_in("sel4", (4, 512))
            self.dram_in("lmat", (128, 128))
            self.dram_in("ident", (128, 128))
            self.dram_in("mneg", (128, 128))
        if 2 in layers:
            self.dram_in("gla_w_a2b", (17, 512))
            self.dram_in("triN", (128, 128))
            self.dram_in("triU", (128, 128))
        if 1 in layers or 2 in layers:
            self.dram_in("bcmask", (128, 128))
        if 3 in layers:
            self.dram_in("conv_w_dw", (128, 8 * 31))
            self.dram_in("identc", (128, 128))

        self.ps = []
        self.ps2 = []
        for i in range(4):
            pa_ = gst.enter_context(nc.psum_tensor("psp%d" % i, [128, 1024], F32))[:]
            self.ps2.append(Tile(pa_))
            self.ps.append(Tile(pa_[:, 0:512]))
            self.ps.append(Tile(pa_[:, 512:1024]))
        self.ones_bf = self.T(gst, [128, 128], BF16, 'ones')
        self.memset('pool', self.ones_bf.v, 1.0)
        self.epsT = self.T(gst, [128, 1], F32, 'eps')
        self.memset('pool', self.epsT.v, EPS)
        self.gmix = self.T(gst, [128, 32], F32, 'gmix')
        self.gffn = self.T(gst, [128, 32], F32, 'gffn')
        self.dma('sp', self.gmix.v, nmix)
        self.dma('sp', self.gffn.v, nffn)

        self.wb = {}
        cast_list = []
        ffn_list = []
        for L in layers:
            for n in LAYER_W[L]:
                shp = WSHAPE[n]
                dst = self.dram_scr(n + "_bf", shp, BF16)
                self.wb[n] = dst
                cast_list.append((self.din[n], dst, shp))
            d1 = self.dram_scr("ffn_w1_%d_bf" % L, (8, 128, 8 * 512), BF16)
            d2 = self.dram_scr("ffn_w2_%d_bf" % L, (4, 128, 32 * 256), BF16)
            self.wb["ffn_w1_%d" % L] = d1
            self.wb["ffn_w2_%d" % L] = d2
            ffn_list.append((self.din["ffn_w1_%d" % L], d1, self.din["ffn_w2_%d" % L], d2))
        self.cast_phase(cast_list, ffn_list)
        p.barrier()

        xm = self.dram_scr("x_mid", (D, S), F32)
        xs = [self.dram_scr("x_s0", (D, S), F32), self.dram_scr("x_s1", (D, S), F32)]
        cur = xin
        for li, L in enumerate(layers):
            nxt = yout if li == len(layers) - 1 else xs[li % 2]
            if self.skip_ffn:
                xm = nxt
            if L == 0:
                self.mla_phase(cur, xm)
            elif L == 1:
                self.mlstm_phase(cur, xm)
            elif L == 2:
                self.gla_phase(cur, xm)
            else:
                self.conv_phase(cur, xm)
            p.barrier()
            if not self.skip_ffn:
                self.ffn_phase(L, xm, nxt)
                p.barrier()
            cur = nxt
        p.emit(gst)
        gst.close()
        return nc

    def cast_phase(self, items, ffn_items):
        CH = 8192
        with ExitStack() as st:
            src_t = [self.T(st, [128, CH], F32, 'cs') for _ in range(2)]
            dst_t = [self.T(st, [128, CH], BF16, 'cd') for _ in range(2)]
            i = 0
            engs = ['dve', 'act', 'dve']
            for (src, dst, shp) in items:
                K, N = shp
                M = K * N // 128
                sv = src.rearrange("(p a) n -> p (a n)", p=128)
                dv = dst.rearrange("(p a) n -> p (a n)", p=128)
                for c0 in range(0, M, CH):
                    w = min(CH, M - c0)
                    a, b = src_t[i % 2], dst_t[i % 2]
                    self.dma('sp', a[:, 0:w], sv[:, c0:c0 + w])
                    self.copy(engs[i % 3], b[:, 0:w], a[:, 0:w])
                    self.dma('pool', dv[:, c0:c0 + w], b[:, 0:w])
                    i += 1
            for (s1, d1, s2, d2) in ffn_items:
                s1v = s1.rearrange("(c p) f -> p c f", p=128)
                for fg in range(8):
                    a, b = src_t[i % 2], dst_t[i % 2]
                    self.dma('sp', a[:, 0:4096].re("p (c f) -> p c f", c=8), s1v[:, :, fg * 512:(fg + 1) * 512])
                    self.copy(engs[i % 3], b[:, 0:4096], a[:, 0:4096])
                    self.dma('pool', d1[fg], b[:, 0:4096])
                    i += 1
                s2v = s2.rearrange("(c p) d -> p c d", p=128)
                for dg in range(4):
                    a, b = src_t[i % 2], dst_t[i % 2]
                    av = a.v.re("p (c d) -> p c d", c=32)
                    for q in range(4):
                        self.dma('sp', av[:, q * 8:(q + 1) * 8, :], s2v[:, q * 8:(q + 1) * 8, dg * 256:(dg + 1) * 256])
                    self.copy(engs[i % 3], b.v, a.v)
                    self.dma('pool', d2[dg], b.v)
                    i += 1

    def rmsnorm(self, x_chunks, gcols, hn_chunks, sq_chunks, ps, std, rstd, n_feat, width, eng_alt=('dve',)):
        n = len(x_chunks)
        for c in range(n):
            self.act(sq_chunks[c], x_chunks[c], AF.Square)
        for c in range(n):
            self.mm(ps, self.ones_bf.v, sq_chunks[c], start=(c == 0), stop=(c == n - 1))
        self.act(std, ps, AF.Ln, bias=self.epsT.v, scale=1.0 / n_feat)
        self.act(rstd, std, AF.Exp, scale=-0.5)
        for c in range(n):
            self.stt(eng_alt[c % len(eng_alt)], hn_chunks[c], x_chunks[c], gcols[c], rstd, ALU.mult, ALU.mult)

    def ffn_phase(self, L, xin, xout):
        TT = 1024
        w1 = self.wb["ffn_w1_%d" % L]
        w2 = self.wb["ffn_w2_%d" % L]
        xv = xin.rearrange("(c p) t -> p c t", p=128)
        ps = self.ps
        with ExitStack() as st:
            xt = self.T(st, [128, 8, TT], F32, 'fx')
            hn = [[self.T(st, [128, 512], BF16, 'fhn') for _ in range(2)] for _ in range(8)]
            sq = [self.T(st, [128, 512], BF16, 'fsq') for _ in range(8)]
            std = self.T(st, [128, 512], F32, 'fstd')
            rstd = self.T(st, [128, 512], F32, 'frstd')
            a = [[self.T(st, [128, 512], BF16, 'fa') for _ in range(2)] for _ in range(32)]
            w1t = [self.T(st, [128, 8, 512], BF16, 'fw1') for _ in range(2)]
            w2t = [self.T(st, [128, 32, 256], BF16, 'fw2') for _ in range(2)]
            rl = [self.T(st, [128, 512], F32, 'frl') for _ in range(4)]
            xres = [self.T(st, [128, TT], F32, 'fxr') for _ in range(2)]
            ot = [self.T(st, [128, TT], F32, 'fo') for _ in range(2)]
            nw1 = 0
            nw2 = 0
            nr = 0
            gcols = [self.gffn[:, L * 8 + c:L * 8 + c + 1] for c in range(8)]

            def norm_stages(sti):
                t0 = sti * TT
                stg = [lambda: self.dma('sp', xt.v, xv[:, :, t0:t0 + TT])]
                for half in range(2):
                    xs_ = [xt[:, c, half * 512:(half + 1) * 512] for c in range(8)]

                    def f_sq(xs_=xs_):
                        for c in range(8):
                            self.act(sq[c].v, xs_[c], AF.Square)

                    def f_mm():
                        for c in range(8):
                            self.mm(ps[0].v, self.ones_bf.v, sq[c].v, start=(c == 0), stop=(c == 7))

                    def f_rs():
                        self.act(std.v, ps[0].v, AF.Ln, bias=self.epsT.v, scale=1.0 / D)
                        self.act(rstd.v, std.v, AF.Exp, scale=-0.5)

                    def f_hn(xs_=xs_, half=half):
                        for c in range(8):
                            self.stt('dve', hn[c][half].v, xs_[c], gcols[c], rstd.v, ALU.mult, ALU.mult)
                    stg += [f_sq, f_mm, f_rs, f_hn]
                return stg

            for f_ in norm_stages(0):
                f_()
            nst_ = S // TT
            for sti in range(nst_):
                t0 = sti * TT
                for fg in range(8):
                    wt = w1t[nw1 % 2]
                    nw1 += 1
                    self.dma('sp', wt.v.re("p c f -> p (c f)"), w1[fg])
                    for fi in range(4):
                        f = fg * 4 + fi
                        pb = (f % 2) * 2
                        for k in range(8):
                            for half in range(2):
                                self.mm(ps[pb + half].v, wt[:, k, fi * 128:(fi + 1) * 128], hn[k][half].v,
                                        start=(k == 0), stop=(k == 7))
                        for half in range(2):
                            r = rl[nr % 4]
                            nr += 1
                            self.act(r.v, ps[pb + half].v, AF.Relu)
                            self.tt('pool' if half else 'dve', a[f][half].v, r.v, r.v, ALU.mult)
                pending = norm_stages(sti + 1) if sti + 1 < nst_ else []
                grp = 0
                for dg in range(4):
                    wt = w2t[nw2 % 2]
                    nw2 += 1
                    self.dma('sp', wt.v.re("p c d -> p (c d)"), w2[dg])
                    for dd in range(2):
                        d = dg * 2 + dd
                        xr = xres[d % 2]
                        o = ot[d % 2]
                        self.dma('sp', xr.v, xin[d * 128:(d + 1) * 128, t0:t0 + TT])
                        pb = 4 + (d % 2) * 2
                        for f in range(32):
                            for half in range(2):
                                self.mm(ps[pb + half].v, wt[:, f, dd * 128:(dd + 1) * 128], a[f][half].v,
                                        start=(f == 0), stop=(f == 31))
                        for half in range(2):
                            self.tt('dve', o[:, half * 512:(half + 1) * 512], ps[pb + half].v,
                                    xr[:, half * 512:(half + 1) * 512], ALU.add)
                        self.dma('pool', xout[d * 128:(d + 1) * 128, t0:t0 + TT], o.v)
                        left = 8 - grp
                        grp += 1
                        for _ in range(-(-len(pending) // left)):
                            if pending:
                                pending.pop(0)()
                while pending:
                    pending.pop(0)()

    def load_w(self, st, name, kc, n, cols=None):
        t = self.T(st, [128, kc, n], BF16, 'w')
        src = self.wb[name].rearrange("(c p) n -> p c n", p=128)
        if cols is not None:
            src = src[:, :, cols[0]:cols[1]]
        self.dma('sp', t.v, src)
        return t

    def load_x_norm(self, L, xin, t0, xt, hn, sq, std, rstd, psb):
        self.norm_pre(L, xin, t0, xt, sq)
        self.norm_post(L, xt, hn, sq, std, rstd, psb)

    def norm_pre(self, L, xin, t0, xt, sq):
        xv = xin.rearrange("(c p) t -> p c t", p=128)
        self.dma('sp', xt.v, xv[:, :, t0:t0 + 512])
        for c in range(8):
            self.act(sq[c].v, xt[:, c, :], AF.Square)

    def norm_post(self, L, xt, hn, sq, std, rstd, psb):
        for c in range(8):
            self.mm(psb.v, self.ones_bf.v, sq[c].v, start=(c == 0), stop=(c == 7))
        self.act(std.v, psb.v, AF.Ln, bias=self.epsT.v, scale=1.0 / D)
        self.act(rstd.v, std.v, AF.Exp, scale=-0.5)
        for c in range(8):
            self.stt('dve', hn[c].v, xt[:, c, :], self.gmix[:, L * 8 + c:L * 8 + c + 1], rstd.v, ALU.mult, ALU.mult)

    def out_proj(self, wo, og, xt, ot, xout, t0, bias=None, banks=(6, 7), xres=None):
        ps = self.ps
        for oc in range(8):
            pb = ps[banks[oc % 2]]
            if xres is not None:
                xr = xres[0][oc % 2]
                self.dma('sp', xr.v, xres[1][oc * 128:(oc + 1) * 128, t0:t0 + 512])
                xsrc = xr.v
            else:
                xsrc = xt[:, oc, :]
            for j in range(8):
                self.mm(pb.v, wo[:, j, oc * 128:(oc + 1) * 128], og[j], start=(j == 0), stop=(j == 7))
            o = ot[oc % 2]
            if bias is None:
                self.tt('dve', o.v, pb.v, xsrc, ALU.add)
            else:
                self.stt('dve', o.v, pb.v, bias[:, oc:oc + 1], xsrc, ALU.add, ALU.add)
            self.dma('pool', xout[oc * 128:(oc + 1) * 128, t0:t0 + 512], o.v)

    def conv_phase(self, xin, xout):
        L = 3
        ps = self.ps
        with ExitStack() as st:
            w1 = self.load_w(st, 'conv_w_pw1', 8, 2048)
            w2 = self.load_w(st, 'conv_w_pw2', 8, 1024)
            b1 = self.T(st, [128, 16], F32)
            bdw = self.T(st, [128, 8], F32)
            lg = self.T(st, [128, 8], F32)
            lb = self.T(st, [128, 8], F32)
            b2 = self.T(st, [128, 8], F32)
            wdw = self.T(st, [128, 8 * 31], F32)
            identf = self.T(st, [128, 128], F32)
            for t_, n in ((b1, 'conv_b_pw1'), (bdw, 'conv_b_dw'), (lg, 'conv_ln_g'), (lb, 'conv_ln_b'),
                          (b2, 'conv_b_pw2'), (wdw, 'conv_w_dw'), (identf, 'identc')):
                self.dma('sp', t_.v, self.din[n])
            Dg = [self.T(st, [128, 31, 128], BF16, 'dg') for _ in range(8)]
            for c in range(8):
                self.tt('dve', Dg[c].v, identf.v.ub(1, [128, 31, 128]),
                        wdw[:, c * 31:(c + 1) * 31].ub(2, [128, 31, 128]), ALU.mult)
            xt = self.T(st, [128, 8, 512], F32, 'cx')
            hn = [self.T(st, [128, 512], BF16) for _ in range(8)]
            sq = [self.T(st, [128, 512], BF16) for _ in range(8)]
            std = self.T(st, [128, 512], F32)
            rstd = self.T(st, [128, 512], F32)
            u = [self.T(st, [128, 542], BF16, 'cu') for _ in range(8)]
            vv = [self.T(st, [128, 512], F32, 'cv') for _ in range(8)]
            vb = [self.T(st, [128, 512], BF16) for _ in range(8)]
            sig = [self.T(st, [128, 512], F32) for _ in range(2)]
            mean = self.T(st, [128, 512], F32)
            m2 = self.T(st, [128, 512], F32)
            var = self.T(st, [128, 512], F32)
            z = [self.T(st, [128, 512], BF16) for _ in range(8)]
            ot = [self.T(st, [128, 512], F32) for _ in range(2)]
            for c in range(8):
                self.memset('pool', u[c][:, 0:30], 0.0)
            xrs = [self.T(st, [128, 512], F32) for _ in range(2)]
            self.load_x_norm(L, xin, 0, xt, hn, sq, std, rstd, ps[5])
            for ti in range(S // 512):
                t0 = ti * 512
                for oc in range(8):
                    pa, pg = ps[(oc % 2) * 2], ps[(oc % 2) * 2 + 1]
                    for k in range(8):
                        self.mm(pa.v, w1[:, k, oc * 128:(oc + 1) * 128], hn[k].v, start=(k == 0), stop=(k == 7))
                    for k in range(8):
                        self.mm(pg.v, w1[:, k, 1024 + oc * 128:1024 + (oc + 1) * 128], hn[k].v,
                                start=(k == 0), stop=(k == 7))
                    sg = sig[oc % 2]
                    self.act(sg.v, pg.v, AF.Sigmoid, bias=b1[:, 8 + oc:9 + oc])
                    self.stt('dve', u[oc][:, 30:542], pa.v, b1[:, oc:oc + 1], sg.v, ALU.add, ALU.mult)
                for c in range(8):
                    pc = ps[c % 4]
                    for k in range(31):
                        self.mm(pc.v, Dg[c][:, k, :], u[c][:, k:k + 512], start=(k == 0), stop=(k == 30))
                    self.act(vv[c].v, pc.v, AF.Identity, bias=bdw[:, c:c + 1])
                    self.copy('pool', u[c][:, 0:30], u[c][:, 512:542])
                for c in range(8):
                    self.copy('act', vb[c].v, vv[c].v)
                    self.act(sq[c].v, vv[c].v, AF.Square)
                for c in range(8):
                    self.mm(ps[4].v, self.ones_bf.v, vb[c].v, start=(c == 0), stop=(c == 7))
                for c in range(8):
                    self.mm(ps[5].v, self.ones_bf.v, sq[c].v, start=(c == 0), stop=(c == 7))
                self.act(mean.v, ps[4].v, AF.Copy, scale=1.0 / D)
                self.tt('dve', m2.v, mean.v, mean.v, ALU.mult)
                self.stt('dve', var.v, ps[5].v, 1.0 / D, m2.v, ALU.mult, ALU.subtract)
                self.act(std.v, var.v, AF.Ln, bias=self.epsT.v)
                self.act(rstd.v, std.v, AF.Exp, scale=-0.5)
                for c in range(8):
                    eng = 'dve' if c % 2 == 0 else 'pool'
                    self.tt(eng, vv[c].v, vv[c].v, mean.v, ALU.subtract)
                for c in range(8):
                    eng = 'dve' if c % 2 == 0 else 'pool'
                    self.tt(eng, vv[c].v, vv[c].v, rstd.v, ALU.mult)
                for c in range(8):
                    self.act(z[c].v, vv[c].v, AF.Silu, bias=lb[:, c:c + 1], scale=lg[:, c:c + 1])
                if ti + 1 < S // 512:
                    self.norm_pre(L, xin, t0 + 512, xt, sq)
                self.out_proj(w2, [z[c].v for c in range(8)], xt, ot, xout, t0, bias=b2, xres=(xrs, xin))
                if ti + 1 < S // 512:
                    self.norm_post(L, xt, hn, sq, std, rstd, ps[5])

    nbn = 4
    nbo = 0

    def nb(self):
        self._nb = (getattr(self, '_nb', -1) + 1) % self.nbn
        return self.ps[self.nbo + self._nb]

    def head_finalize(self, st_tiles, O, win, hn, gain, gate_func, og, n_in_head):
        sq, std, rstd, rs, tmpo = st_tiles
        ps = self.ps
        for h in range(4):
            for vc in range(2):
                self.act(sq[h * 2 + vc].v, O[h][:, vc, :], AF.Square)
            for vc in range(2):
                self.mm(ps[5].v, self.ones_bf.v, sq[h * 2 + vc].v, start=(vc == 0), stop=(vc == 1))
            self.act(std.v, ps[5].v, AF.Ln, bias=self.epsT.v, scale=1.0 / 256)
            self.act(rstd.v, std.v, AF.Exp, scale=-0.5)
            for vc in range(2):
                j = h * 2 + vc
                pr = self.nb()
                for k in range(8):
                    self.mm(pr.v, win[:, k, 2048 + j * 128:2048 + (j + 1) * 128], hn[k].v, start=(k == 0), stop=(k == 7))
                self.act(rs[j % 2].v, pr.v, gate_func)
                self.stt('dve', tmpo[j % 2].v, O[h][:, vc, :], gain[:, j:j + 1], rstd.v, ALU.mult, ALU.mult)
                self.tt('pool', og[j].v, tmpo[j % 2].v, rs[j % 2].v, ALU.mult)

    def gla_phase(self, xin, xout):
        L = 2
        ps = self.ps
        with ExitStack() as st:
            win = self.load_w(st, 'gla_w_in', 8, 3072)
            wo = self.load_w(st, 'gla_w_o', 8, 1024)
            wa1 = self.load_w(st, 'gla_w_a1', 8, 16)
            wa2f = self.T(st, [17, 512], F32)
            wa2 = self.T(st, [17, 512], BF16)
            triN = self.T(st, [128, 128], F32)
            triU = self.T(st, [128, 128], F32)
            mask = self.T(st, [128, 128], F32)
            hgn = self.T(st, [128, 8], F32)
            onesc = self.T(st, [128, 1], F32)
            self.memset('pool', onesc.v, 1.0)
            for t_, n in ((wa2f, 'gla_w_a2b'), (triN, 'triN'), (triU, 'triU'), (mask, 'bcmask'), (hgn, 'gla_head_norm')):
                self.dma('sp', t_.v, self.din[n])
            self.copy('dve', wa2.v, wa2f.v)
            xt = self.T(st, [128, 8, 512], F32, 'gx')
            hn = [self.T(st, [128, 512], BF16) for _ in range(8)]
            sq = [self.T(st, [128, 512], BF16) for _ in range(8)]
            std = self.T(st, [128, 512], F32)
            rstd = self.T(st, [128, 512], F32)
            g1a = self.T(st, [17, 512], BF16)
            self.memset('pool', g1a.v, 1.0)
            lsp = [self.T(st, [128, 512], F32) for _ in range(4)]
            ez = self.T(st, [128, 512], F32)
            ep = [self.T(st, [128, 512], F32) for _ in range(2)]
            em = [self.T(st, [128, 512], F32) for _ in range(2)]
            eb = [self.T(st, [128, 8], F32) for _ in range(4)]
            qt = [self.T(st, [128, 512], BF16) for _ in range(4)]
            kt = [self.T(st, [128, 512], BF16) for _ in range(4)]
            vtm = [self.T(st, [128, 1024], BF16) for _ in range(4)]
            kd = [self.T(st, [128, 512], BF16) for _ in range(4)]
            erev = [self.T(st, [128, 512], F32) for _ in range(2)]
            attm = [self.T(st, [128, 4, 128], BF16) for _ in range(2)]
            Sst = [self.T(st, [128, 256], F32) for _ in range(4)]
            Sb = [self.T(st, [128, 256], BF16) for _ in range(4)]
            for h in range(4):
                self.memset('pool', Sst[h].v, 0.0)
                self.memset('pool', Sb[h].v, 0.0)
            O = [self.T(st, [128, 2, 512], F32) for _ in range(4)]
            rs = [self.T(st, [128, 512], F32) for _ in range(2)]
            tmpo = [self.T(st, [128, 512], F32) for _ in range(2)]
            og = [self.T(st, [128, 512], BF16) for _ in range(8)]
            ot = [self.T(st, [128, 512], F32) for _ in range(2)]
            poh = [Tile(ps[6 + i // 2].ap[:, (i % 2) * 256:(i % 2) * 256 + 256]) for i in range(4)]
            sc = 128 ** -0.5
            xrs = [self.T(st, [128, 512], F32) for _ in range(2)]
            self.load_x_norm(L, xin, 0, xt, hn, sq, std, rstd, ps[5])
            for ti in range(S // 512):
                t0 = ti * 512
                pb = self.nb()
                for k in range(8):
                    self.mm(pb[0:16, :], wa1[:, k, :], hn[k].v, start=(k == 0), stop=(k == 7))
                self.copy('act', g1a[0:16, :], pb[0:16, :])
                for b in range(4):
                    pz = self.nb()
                    self.mm(pz.v, g1a[0:17, b * 128:(b + 1) * 128], wa2.v)
                    self.act(ez.v, pz.v, AF.Exp, scale=-1.0)
                    self.act(lsp[b].v, ez.v, AF.Ln, bias=onesc.v)
                for h in range(4):
                    pc = self.nb()
                    for b in range(4):
                        self.mm(pc[:, b * 128:(b + 1) * 128], lsp[b][:, h * 128:(h + 1) * 128], triN.v)
                    e_p, e_m = ep[h % 2], em[h % 2]
                    self.act(e_p.v, pc.v, AF.Exp)
                    self.act(e_m.v, pc.v, AF.Exp, scale=-1.0)
                    self.copy('pool', eb[h].v, e_p.v.re("p (c s) -> p c s", s=64)[:, :, 63])
                    pq = self.nb()
                    for k in range(8):
                        self.mm(pq.v, win[:, k, h * 128:(h + 1) * 128], hn[k].v, start=(k == 0), stop=(k == 7))
                    self.stt('dve', qt[h].v, pq.v, sc, e_p.v, ALU.mult, ALU.mult)
                    pk = self.nb()
                    for k in range(8):
                        self.mm(pk.v, win[:, k, 512 + h * 128:512 + (h + 1) * 128], hn[k].v, start=(k == 0), stop=(k == 7))
                    self.tt('dve', kt[h].v, pk.v, e_m.v, ALU.mult)
                for b in range(4):
                    bc = slice(b * 128, (b + 1) * 128)
                    for half in range(2):
                        pv = self.nb()
                        for k in range(8):
                            self.mm(pv.v, hn[k][:, bc], win[:, k, 1024 + half * 512:1024 + (half + 1) * 512],
                                    start=(k == 0), stop=(k == 7))
                        self.copy('act' if half else 'dve', vtm[b][:, half * 512:(half + 1) * 512], pv.v)
                    pk2 = self.nb()
                    for k in range(8):
                        self.mm(pk2.v, hn[k][:, bc], win[:, k, 512:1024], start=(k == 0), stop=(k == 7))
                    pr = self.nb()
                    self.mm(pr.v, triU.v, lsp[b].v)
                    er = erev[b % 2]
                    self.act(er.v, pr.v, AF.Exp)
                    self.tt('dve', kd[b].v, pk2.v, er.v, ALU.mult)
                for b in range(4):
                    bc = slice(b * 128, (b + 1) * 128)
                    pa = ps[4]
                    am = attm[b % 2]
                    for h in range(4):
                        self.mm(pa[:, h * 128:(h + 1) * 128], kt[h][:, bc], qt[h][:, bc])
                    self.tt('dve', am.v, pa.v.re("p (h t) -> p h t", h=4), mask.v.ub(1, [128, 4, 128]), ALU.mult)
                    for h in range(4):
                        po = poh[h]
                        for vc in range(2):
                            self.mm(po[:, vc * 128:(vc + 1) * 128], vtm[b][:, h * 256 + vc * 128:h * 256 + (vc + 1) * 128],
                                    am[:, h, :], start=(vc == 0 and h % 2 == 0), stop=False)
                    for X in range(2):
                        rows = slice(X * 64, (X + 1) * 64)
                        cols = slice(b * 128 + X * 64, b * 128 + (X + 1) * 64)
                        cl = b * 2 + X
                        for h in range(4):
                            po = poh[h]
                            for vc in range(2):
                                self.mm(po[:, vc * 128 + X * 64:vc * 128 + (X + 1) * 64], Sb[h][:, vc * 128:(vc + 1) * 128],
                                        qt[h][:, cols], start=False, stop=True)
                        pus = []
                        for h in range(4):
                            pu = self.nb()
                            pus.append(pu)
                            self.mm(pu[:, 0:256], kd[b][rows, h * 128:(h + 1) * 128], vtm[b][rows, h * 256:(h + 1) * 256])
                        for h in range(4):
                            self.stt('dve', Sst[h].v, Sst[h].v, eb[h][:, cl:cl + 1], pus[h][:, 0:256], ALU.mult, ALU.add)
                        for h in range(4):
                            self.copy('act', Sb[h].v, Sst[h].v)
                    for h in range(4):
                        self.copy('act', O[h][:, :, bc], poh[h].v.re("p (v t) -> p v t", v=2))
                self.head_finalize((sq, std, rstd, rs, tmpo), O, win, hn, hgn, AF.Silu, og, 256)
                if ti + 1 < S // 512:
                    self.norm_pre(L, xin, t0 + 512, xt, sq)
                self.out_proj(wo, [og[j].v for j in range(8)], xt, ot, xout, t0, banks=(4, 5), xres=(xrs, xin))
                if ti + 1 < S // 512:
                    self.norm_post(L, xt, hn, sq, std, rstd, ps[5])


    def mlstm_phase(self, xin, xout):
        L = 1
        ps = self.ps
        nc = self.nc
        gi_s = self.dram_scr("ml_gi", (4, S), F32)
        gf_s = self.dram_scr("ml_gf", (4, S), F32)
        em_s = self.dram_scr("ml_em", (4, S), F32)
        wa_s = self.dram_scr("ml_wa", (4, S), F32)
        wc_s = self.dram_scr("ml_wc", (4, 128), F32)
        with ExitStack() as st:
            wif = self.load_w(st, 'mlstm_w_if', 8, 8)
            bif = self.T(st, [4, 16], F32)
            bi15 = self.T(st, [4, 16], F32)
            onesc = self.T(st, [128, 1], F32)
            self.memset('pool', onesc.v, 1.0)
            self.dma('sp', bif.v, self.din['mlstm_b_if'])
            self.ts('dve', bi15.v, bif.v, 1.0 / 15.0, None, ALU.mult)
            xts = [self.T(st, [128, 8, 512], F32) for _ in range(2)]
            hn = [self.T(st, [128, 512], BF16) for _ in range(8)]
            sq = [self.T(st, [128, 512], BF16) for _ in range(8)]
            std = self.T(st, [128, 512], F32)
            rstd = self.T(st, [128, 512], F32)
            t1 = [self.T(st, [4, 512], F32) for _ in range(2)]
            t2 = [self.T(st, [4, 512], F32) for _ in range(2)]
            t3 = [self.T(st, [4, 512], F32) for _ in range(2)]
            li = [self.T(st, [4, 512], F32) for _ in range(2)]
            lf = [self.T(st, [4, 512], F32) for _ in range(2)]
            for ti in range(S // 512):
                t0 = ti * 512
                xt = xts[ti % 2]
                self.load_x_norm(L, xin, t0, xt, hn, sq, std, rstd, ps[5])
                pgi, pgf = self.nb(), self.nb()
                for k in range(8):
                    self.mm(pgi[0:4, :], wif[:, k, 0:4], hn[k].v, start=(k == 0), stop=(k == 7))
                for k in range(8):
                    self.mm(pgf[0:4, :], wif[:, k, 4:8], hn[k].v, start=(k == 0), stop=(k == 7))
                a1, a2, a3, l_i, l_f = t1[ti % 2], t2[ti % 2], t3[ti % 2], li[ti % 2], lf[ti % 2]
                self.act(a1.v, pgi[0:4, :], AF.Tanh, bias=bi15[:, 0:1], scale=1.0 / 15.0)
                self.ts('dve', l_i.v, a1.v, 15.0, None, ALU.mult)
                self.act(a2.v, pgf[0:4, :], AF.Tanh, bias=bi15[:, 1:2], scale=1.0 / 15.0)
                self.act(a3.v, a2.v, AF.Exp, scale=-15.0)
                self.act(a2.v, a3.v, AF.Ln, bias=onesc[0:4, :])
                self.ts('dve', l_f.v, a2.v, -1.0, None, ALU.mult)
                self.dma('sp', gi_s[:, t0:t0 + 512], l_i.v)
                self.dma('sp', gf_s[:, t0:t0 + 512], l_f.v)
        self.p.barrier()
        with ExitStack() as st:
            def t_(shape, dt=F32):
                return self.T(st, shape, dt)
            Li, Lf, onesr, Floc, Fg, a_, Aloc, Ap, tmpA, wa, emx, Ab = [t_([128, 256]) for _ in range(12)]
            lmat, ident, mneg, rb, rowv = [t_([128, 128]) for _ in range(5)]
            Gs, Apre = t_([128, 1]), t_([128, 1])
            Aend, Astart, wc = t_([128, 4]), t_([128, 4]), t_([128, 4])
            self.dma('sp', Li.v, gi_s.rearrange("h (s t) -> (h s) t", t=256))
            self.dma('sp', Lf.v, gf_s.rearrange("h (s t) -> (h s) t", t=256))
            self.dma('sp', lmat.v, self.din['lmat'])
            self.dma('sp', ident.v, self.din['ident'])
            self.dma('sp', mneg.v, self.din['mneg'])
            self.memset('pool', onesr.v, 1.0)
            self.scan(Floc.v, onesr.v, Lf.v, 0.0, ALU.mult, ALU.add)
            self.copy('dve', rb.v, Floc[:, 255:256].bc([128, 128]))
            pg = self.nb()
            self.mm(pg[:, 0:128], lmat.v, rb.v)
            self.copy('act', Gs.v, pg[:, 0:1])
            self.ts('dve', Fg.v, Floc.v, Gs[:, 0:1], None, ALU.add)
            self.tt('dve', a_.v, Li.v, Fg.v, ALU.subtract)
            self.ts('dve', Aloc.v, a_.v, 0.0, None, ALU.max)
            src, dst = Aloc, Ab
            d = 1
            while d < 256:
                self.tt('dve', dst[:, d:256], src[:, d:256], src[:, 0:256 - d], ALU.max)
                self.copy('dve', dst[:, 0:d], src[:, 0:d])
                src, dst = dst, src
                d *= 2
            Aloc = src
            self.copy('dve', rb.v, Aloc[:, 255:256].bc([128, 128]))
            pr = self.nb()
            self.mm(pr[:, 0:128], rb.v, ident.v)
            self.tt('dve', rowv.v, pr[:, 0:128], mneg.v, ALU.add)
            self.p.op('dve', (lambda o_, i_: (lambda e: e.tensor_reduce(o_, i_, mybir.AxisListType.X, ALU.max)))(Apre.ap, rowv.ap),
                      [rowv], [Apre])
            self.ts('dve', Apre.v, Apre.v, 0.0, None, ALU.max)
            self.ts('dve', Ap.v, Aloc.v, Apre[:, 0:1], None, ALU.max)
            self.copy('dve', Aend.v, Ap.v.re("p (j t) -> p j t", t=64)[:, :, 63])
            self.copy('dve', Astart[:, 0:1], Apre.v)
            self.copy('dve', Astart[:, 1:4], Aend[:, 0:3])
            self.tt('dve', wc.v, Astart.v, Aend.v, ALU.subtract)
            self.act(wc.v, wc.v, AF.Exp)
            self.dma('sp', wc_s.rearrange("h (s j) -> (h s) j", j=4), wc.v)
            self.tt('dve', tmpA.v.re("p (j t) -> p j t", t=64), a_.v.re("p (j t) -> p j t", t=64),
                    Aend.v.ub(2, [128, 4, 64]), ALU.subtract)
            self.act(wa.v, tmpA.v, AF.Exp)
            self.dma('pool', wa_s.rearrange("h (s t) -> (h s) t", t=256), wa.v)
            self.tt('dve', tmpA.v.re("p (j t) -> p j t", t=64), Fg.v.re("p (j t) -> p j t", t=64),
                    Aend.v.ub(2, [128, 4, 64]), ALU.add)
            self.act(emx.v, tmpA.v, AF.Exp)
            self.dma('pool', em_s.rearrange("h (s t) -> (h s) t", t=256), emx.v)
        self.p.barrier()
        with ExitStack() as st:
            win = self.load_w(st, 'mlstm_w_in', 8, 3072)
            wo = self.load_w(st, 'mlstm_w_o', 8, 1024)
            mask = self.T(st, [128, 128], F32)
            hgn = self.T(st, [128, 8], F32)
            sel4 = self.T(st, [4, 512], F32)
            ident = self.T(st, [128, 128], F32)
            for t_, n in ((mask, 'bcmask'), (hgn, 'mlstm_head_norm'), (sel4, 'sel4'), (ident, 'ident')):
                self.dma('sp', t_.v, self.din[n])
            xt = self.T(st, [128, 8, 512], F32)
            hn = [self.T(st, [128, 512], BF16) for _ in range(8)]
            sq = [self.T(st, [128, 512], BF16) for _ in range(8)]
            std = self.T(st, [128, 512], F32)
            rstd = self.T(st, [128, 512], F32)
            emT = self.T(st, [4, 512], F32)
            waT = self.T(st, [4, 512], F32)
            wcT = self.T(st, [4, 128], F32)
            watm = self.T(st, [128, 16], F32)
            wcb = self.T(st, [128, 4, 128], F32)
            embc = [self.T(st, [128, 512], F32) for _ in range(2)]
            qs = [self.T(st, [128, 512], BF16) for _ in range(4)]
            kt = [self.T(st, [128, 512], BF16) for _ in range(4)]
            vaug = [self.T(st, [128, 4, 384], BF16) for _ in range(4)]
            for b in range(4):
                self.memset('pool', vaug[b].v, 1.0)
            kw = [self.T(st, [128, 4, 128], BF16) for _ in range(4)]
            qkw = [self.T(st, [128, 4, 128], BF16) for _ in range(2)]
            Sst = [self.T(st, [128, 384], F32) for _ in range(4)]
            Sb = [self.T(st, [128, 384], BF16) for _ in range(4)]
            for h in range(4):
                self.memset('pool', Sst[h].v, 0.0)
            dn = [self.T(st, [128, 128], F32) for _ in range(2)]
            rdn = [self.T(st, [128, 128], F32) for _ in range(2)]
            O = [self.T(st, [128, 2, 512], F32) for _ in range(4)]
            rs = [self.T(st, [128, 512], F32) for _ in range(2)]
            tmpo = [self.T(st, [128, 512], F32) for _ in range(2)]
            og = [self.T(st, [128, 512], BF16) for _ in range(8)]
            ot = [self.T(st, [128, 512], F32) for _ in range(2)]
            sc = 128 ** -0.5
            self.dma('sp', wcT.v, wc_s)
            for h in range(4):
                pwc = self.nb()
                self.mm(pwc[:, 0:128], sel4[0:4, h * 128:(h + 1) * 128], wcT.v)
                self.copy('act', wcb[:, h, :], pwc[:, 0:128])
            xrs = [self.T(st, [128, 512], F32) for _ in range(2)]
            self.load_x_norm(L, xin, 0, xt, hn, sq, std, rstd, ps[5])
            for ti in range(S // 512):
                t0 = ti * 512
                self.dma('sp', emT.v, em_s[:, t0:t0 + 512])
                self.dma('sp', waT.v, wa_s[:, t0:t0 + 512])
                pw = self.nb()
                for b in range(4):
                    self.mm(pw[:, b * 128:(b + 1) * 128], waT[0:4, b * 128:(b + 1) * 128], ident[0:4, 0:128])
                self.ts('dve', watm.v.re("p (b h) -> p b h", h=4), pw.v.re("p (b n) -> p b n", n=128)[:, :, 0:4], sc, None, ALU.mult)
                for h in range(4):
                    pe_ = self.nb()
                    self.mm(pe_.v, sel4[0:4, h * 128:(h + 1) * 128], emT.v)
                    eb_ = embc[h % 2]
                    self.copy('act', eb_.v, pe_.v)
                    pq = self.nb()
                    for k in range(8):
                        self.mm(pq.v, win[:, k, h * 128:(h + 1) * 128], hn[k].v, start=(k == 0), stop=(k == 7))
                    self.tt('dve', qs[h].v, pq.v, eb_.v, ALU.mult)
                    pk = self.nb()
                    for k in range(8):
                        self.mm(pk.v, win[:, k, 512 + h * 128:512 + (h + 1) * 128], hn[k].v, start=(k == 0), stop=(k == 7))
                    self.copy('act', kt[h].v, pk.v)
                for b in range(4):
                    bc = slice(b * 128, (b + 1) * 128)
                    for half in range(2):
                        pv = self.nb()
                        for k in range(8):
                            self.mm(pv.v, hn[k][:, bc], win[:, k, 1024 + half * 512:1024 + (half + 1) * 512],
                                    start=(k == 0), stop=(k == 7))
                        self.copy('act' if half else 'dve', vaug[b][:, 2 * half:2 * half + 2, 0:256],
                                  pv.v.re("p (h v) -> p h v", h=2))
                    pk2 = self.nb()
                    for k in range(8):
                        self.mm(pk2.v, hn[k][:, bc], win[:, k, 512:1024], start=(k == 0), stop=(k == 7))
                    self.tt('dve', kw[b].v, pk2.v.re("p (h d) -> p h d", h=4),
                            watm[:, b * 4:(b + 1) * 4].ub(2, [128, 4, 128]), ALU.mult)
                for b in range(4):
                    bc = slice(b * 128, (b + 1) * 128)
                    pa = self.nb()
                    qk_ = qkw[b % 2]
                    for h in range(4):
                        self.mm(pa[:, h * 128:(h + 1) * 128], kt[h][:, bc], qs[h][:, bc])
                    for h in range(4):
                        self.stt('dve', qk_[:, h, :], pa[:, h * 128:(h + 1) * 128], watm[:, b * 4 + h:b * 4 + h + 1],
                                 mask.v, ALU.mult, ALU.mult)
                    for h in range(4):
                        po = ps[4 + h]
                        for j in range(3):
                            self.mm(po[:, j * 128:(j + 1) * 128], vaug[b][:, h, j * 128:(j + 1) * 128], qk_[:, h, :],
                                    start=(j == 0), stop=False)
                    for X in range(2):
                        rows = slice(X * 64, (X + 1) * 64)
                        cols = slice(b * 128 + X * 64, b * 128 + (X + 1) * 64)
                        cl = b * 2 + X
                        for h in range(4):
                            self.act(Sb[h].v, Sst[h].v, AF.Copy, scale=wcb[:, h, ti * 8 + cl:ti * 8 + cl + 1])
                        for h in range(4):
                            po = ps[4 + h]
                            for j in range(3):
                                self.mm(po[:, j * 128 + X * 64:j * 128 + (X + 1) * 64], Sb[h][:, j * 128:(j + 1) * 128],
                                        qs[h][:, cols], start=False, stop=True)
                        pus = []
                        for h in range(4):
                            pu = self.nb()
                            pus.append(pu)
                            self.mm(pu[:, 0:384], kw[b][rows, h, :], vaug[b][rows, h, :])
                        for h in range(4):
                            self.stt('dve', Sst[h].v, Sst[h].v, wcb[:, h, ti * 8 + cl:ti * 8 + cl + 1], pus[h][:, 0:384],
                                     ALU.mult, ALU.add)
                    for h in range(4):
                        po = ps[4 + h]
                        d_, r_ = dn[h % 2], rdn[h % 2]
                        self.act(d_.v, po[:, 256:384], AF.Abs)
                        self.ts('dve', d_.v, d_.v, 1.0, None, ALU.max)
                        self.act(r_.v, d_.v, AF.Ln)
                        self.act(r_.v, r_.v, AF.Exp, scale=-1.0)
                        self.tt('dve', O[h][:, :, bc], po[:, 0:256].re("p (v t) -> p v t", v=2),
                                r_.v.ub(1, [128, 2, 128]), ALU.mult)
                self.head_finalize((sq, std, rstd, rs, tmpo), O, win, hn, hgn, AF.Sigmoid, og, 256)
                if ti + 1 < S // 512:
                    self.norm_pre(L, xin, t0 + 512, xt, sq)
                self.out_proj(wo, [og[j].v for j in range(8)], xt, ot, xout, t0, banks=(4, 5), xres=(xrs, xin))
                if ti + 1 < S // 512:
                    self.norm_post(L, xt, hn, sq, std, rstd, ps[5])


    def normrope(self, tl, x_pre, cos_t, sin_t, g, Rg, bias_ap, scale, out):
        sq96, xb, std96, rstd96, t1, t2 = tl
        self.act(sq96.v, x_pre.v, AF.Square)
        pss = self.nb()
        self.mm(pss[0:96, :], self.ones_bf[0:96, 0:96], sq96.v)
        self.act(std96.v, pss[0:96, :], AF.Ln, bias=bias_ap, scale=scale)
        self.act(rstd96.v, std96.v, AF.Exp, scale=-0.5)
        self.copy('act', xb.v, x_pre.v)
        prot = self.nb()
        self.mm(prot[0:96, :], Rg.v, xb.v)
        self.stt('dve', t1.v, x_pre.v, g[:, 0:1], cos_t.v, ALU.mult, ALU.mult)
        self.tt('dve', t2.v, prot[0:96, :], sin_t.v, ALU.mult)
        self.tt('pool', t1.v, t1.v, t2.v, ALU.add)
        self.tt('pool', out, t1.v, rstd96.v, ALU.mult)

    def mla_phase(self, xin, xout):
        L = 0
        ps = self.ps
        kr_s = self.dram_scr("mla_kr", (32, S), F32)
        o_s = self.dram_scr("mla_o", (D, S), BF16)
        cosd, sind = self.din['rope_cos'], self.din['rope_sin']
        with ExitStack() as st0:
            cqn = [self.T(st0, [128, S], BF16, 'cqn') for _ in range(3)]
            ckvn = [self.T(st0, [128, S], BF16, 'ckvn') for _ in range(2)]
            with ExitStack() as st:
                wdq = self.load_w(st, 'mla_w_dq', 8, 384)
                wdkv = self.load_w(st, 'mla_w_dkv', 8, 288)
                qn = self.T(st, [128, 3], F32)
                kvn = self.T(st, [128, 2], F32)
                self.dma('sp', qn.v, self.din['mla_q_norm'])
                self.dma('sp', kvn.v, self.din['mla_kv_norm'])
                xts = [self.T(st, [128, 8, 512], F32) for _ in range(2)]
                hn = [self.T(st, [128, 512], BF16) for _ in range(8)]
                sq = [self.T(st, [128, 512], BF16) for _ in range(8)]
                std = self.T(st, [128, 512], F32)
                rstd = self.T(st, [128, 512], F32)
                std2 = self.T(st, [128, 512], F32)
                rstd2 = self.T(st, [128, 512], F32)
                krt = [self.T(st, [32, 512], F32) for _ in range(2)]
                for ti in range(S // 512):
                    t0 = ti * 512
                    tc_ = slice(t0, t0 + 512)
                    xt = xts[ti % 2]
                    self.load_x_norm(L, xin, t0, xt, hn, sq, std, rstd, ps[7])
                    for j in range(3):
                        for k in range(8):
                            self.mm(ps[j].v, wdq[:, k, j * 128:(j + 1) * 128], hn[k].v, start=(k == 0), stop=(k == 7))
                    self.rmsnorm([ps[j].v for j in range(3)], [qn[:, j:j + 1] for j in range(3)],
                                 [cqn[j][:, tc_] for j in range(3)], [sq[j].v for j in range(3)],
                                 ps[3].v, std2.v, rstd2.v, 384, 512)
                    for j in range(2):
                        for k in range(8):
                            self.mm(ps[4 + j].v, wdkv[:, k, j * 128:(j + 1) * 128], hn[k].v, start=(k == 0), stop=(k == 7))
                    self.rmsnorm([ps[4 + j].v for j in range(2)], [kvn[:, j:j + 1] for j in range(2)],
                                 [ckvn[j][:, tc_] for j in range(2)], [sq[3 + j].v for j in range(2)],
                                 ps[6].v, std2.v, rstd2.v, 256, 512)
                    pk = ps[7]
                    for k in range(8):
                        self.mm(pk[0:32, :], wdkv[:, k, 256:288], hn[k].v, start=(k == 0), stop=(k == 7))
                    kr = krt[ti % 2]
                    self.copy('act', kr.v, pk[0:32, :])
                    self.dma('sp', kr_s[:, tc_], kr.v)
            self.p.barrier()
            with ExitStack() as st:
                wuq = self.load_w(st, 'mla_w_uq', 3, 1536)
                wukv = self.load_w(st, 'mla_w_ukv', 2, 2048)
                qg = self.T(st, [96, 16], F32)
                kg = self.T(st, [96, 16], F32)
                Rf = self.T(st, [96, 96], F32)
                Rq = self.T(st, [96, 96], BF16)
                Rk = self.T(st, [96, 96], BF16)
                epsq = self.T(st, [96, 1], F32)
                self.memset('pool', epsq.v, EPS * 96.0)
                self.dma('sp', qg.v, self.din['mla_qg'])
                self.dma('sp', kg.v, self.din['mla_kg'])
                self.dma('sp', Rf.v, self.din['rope_R'])
                self.ts('dve', Rq.v, Rf.v, qg[:, 0:1], None, ALU.mult)
                self.ts('dve', Rk.v, Rf.v, kg[:, 0:1], None, ALU.mult)
                KTs = [self.T(st, [96, S], BF16, 'KT') for _ in range(2)]
                VAs = [self.T(st, [128, 64, 128], BF16, 'VA') for _ in range(2)]
                for v_ in VAs:
                    self.memset('pool', v_.v, 1.0)
                kp = [self.T(st, [96, 512], F32) for _ in range(2)]
                cs = [self.T(st, [96, 512], F32) for _ in range(2)]
                sn = [self.T(st, [96, 512], F32) for _ in range(2)]
                def mk_tl():
                    return dict(sq=self.T(st, [96, 512], BF16), xb=self.T(st, [96, 512], BF16),
                                rs=self.T(st, [96, 512], F32), t1=self.T(st, [96, 512], F32),
                                t2=self.T(st, [96, 512], F32))
                tlq, tlk = mk_tl(), mk_tl()
                Qf = [self.T(st, [96, 512], BF16) for _ in range(2)]
                qpre = self.T(st, [96, 512], F32)
                Pt = [self.T(st, [128, 1024], BF16) for _ in range(3)]
                rsum = [self.T(st, [64, 512], F32) for _ in range(1)]
                oh = [self.T(st, [64, 512], BF16) for _ in range(2)]
                self.nbn = 2
                self.nbo = 4
                cnt = {'ci': 0, 'npt': 0}

                def nr_stages(tl_, x_pre, c_t, s_t, g, Rg, bias_ap, scale, out):
                    hold = {}

                    def s_a():
                        self.tt('dve', tl_['sq'].v, x_pre.v, x_pre.v, ALU.mult)
                        self.copy('dve', tl_['xb'].v, x_pre.v)

                    def s_b():
                        hold['pss'] = self.nb()
                        self.mm(hold['pss'][0:96, :], self.ones_bf[0:96, 0:96], tl_['sq'].v)

                    def s_c():
                        self.act(tl_['rs'].v, hold['pss'][0:96, :], AF.Ln, bias=bias_ap, scale=scale)
                        self.act(tl_['rs'].v, tl_['rs'].v, AF.Exp, scale=-0.5)

                    def s_d():
                        hold['prot'] = self.nb()
                        self.mm(hold['prot'][0:96, :], Rg.v, tl_['xb'].v)

                    def s_e():
                        self.stt('dve', tl_['t1'].v, x_pre.v, g[:, 0:1], c_t.v, ALU.mult, ALU.mult)
                        self.tt('dve', tl_['t2'].v, hold['prot'][0:96, :], s_t.v, ALU.mult)

                    def s_f():
                        self.tt('dve', tl_['t1'].v, tl_['t1'].v, tl_['t2'].v, ALU.add)
                        self.tt('pool', out, tl_['t1'].v, tl_['rs'].v, ALU.mult)
                    return [s_a, s_b, s_c, s_d, s_e, s_f]

                def kgen_stages(h, ti):
                    KT, VA = KTs[h % 2], VAs[h % 2]
                    t0 = ti * 512
                    tc_ = slice(t0, t0 + 512)
                    ci = cnt['ci']
                    cnt['ci'] += 1
                    kpre, c_t, s_t = kp[ci % 2], cs[ci % 2], sn[ci % 2]
                    hold = {}

                    def k1():
                        hold['pk'] = self.nb()
                        for j in range(2):
                            self.mm(hold['pk'][0:64, :], wukv[:, j, h * 128:h * 128 + 64], ckvn[j][:, tc_],
                                    start=(j == 0), stop=(j == 1))
                        self.dma('sp', kpre[64:96, :], kr_s[:, tc_])
                        self.dma('sp', c_t.v, cosd[:, tc_])
                        self.dma('sp', s_t.v, sind[:, tc_])

                    def k2():
                        self.copy('dve', kpre[0:64, :], hold['pk'][0:64, :])

                    def k9():
                        hold['pv'] = self.nb()
                        for b in range(4):
                            for j in range(2):
                                self.mm(hold['pv'][:, b * 64:(b + 1) * 64], ckvn[j][:, t0 + b * 128:t0 + (b + 1) * 128],
                                        wukv[:, j, h * 128 + 64:h * 128 + 128], start=(j == 0), stop=(j == 1))

                    def k10():
                        self.copy('dve', VA[:, ti * 4:(ti + 1) * 4, 0:64], hold['pv'][:, 0:256].re("p (b v) -> p b v", b=4))
                    return [k1, k2] + nr_stages(tlk, kpre, c_t, s_t, kg, Rk, self.epsT[0:96, :], 1.0 / 96.0, KT[:, tc_]) + [k9, k10]

                def qgen_stages(h, qi):
                    qc_ = slice(qi * 512, qi * 512 + 512)
                    ci = cnt['ci']
                    cnt['ci'] += 1
                    c_t, s_t = cs[ci % 2], sn[ci % 2]
                    qf = Qf[(h * 16 + qi) % 2]
                    hold = {}

                    def q1():
                        hold['pq'] = self.nb()
                        for j in range(3):
                            self.mm(hold['pq'][0:96, :], wuq[:, j, h * 96:(h + 1) * 96], cqn[j][:, qc_], start=(j == 0), stop=(j == 2))
                        self.dma('sp', c_t.v, cosd[:, qc_])
                        self.dma('sp', s_t.v, sind[:, qc_])

                    def q2():
                        self.copy('dve', qpre.v, hold['pq'][0:96, :])
                    return [q1, q2] + nr_stages(tlq, qpre, c_t, s_t, qg, Rq, epsq.v, 1.0, qf.v)

                def attn(h, qi, pending):
                    KT, VA = KTs[h % 2], VAs[h % 2]
                    qc_ = slice(qi * 512, qi * 512 + 512)
                    qf = Qf[(h * 16 + qi) % 2]
                    po = ps[6 + qi % 2]
                    nkb = 4 * qi + 4
                    npairs = nkb // 2
                    pend = []
                    for pp in range(npairs):
                        pst = self.ps2[cnt['npt'] % 2]
                        P = Pt[cnt['npt'] % 3]
                        cnt['npt'] += 1
                        for hf in range(2):
                            kb = 2 * pp + hf
                            self.mm(pst[:, hf * 512:(hf + 1) * 512], KT[:, kb * 128:(kb + 1) * 128], qf.v)
                        self.act(P.v, pst.v, AF.Exp)
                        for hf in range(2):
                            kb = 2 * pp + hf
                            kl = kb - 4 * qi
                            if kl >= 0:
                                if kl > 0:
                                    self.memset('pool', P[:, hf * 512:hf * 512 + 128 * kl], 0.0)
                                self.memset('pool', P[64:128, hf * 512 + 128 * kl:hf * 512 + 128 * kl + 64], 0.0)
                        pend.append((pp, P))
                        if len(pend) > 2:
                            pp_, P_ = pend.pop(0)
                            for hf in range(2):
                                kb_ = 2 * pp_ + hf
                                self.mm(po.v, VA[:, kb_, :], P_[:, hf * 512:(hf + 1) * 512], start=(kb_ == 0), stop=False)
                        left = npairs - pp
                        nst = -(-len(pending) // left)
                        for _ in range(nst):
                            if pending:
                                pending.pop(0)()
                    for (pp_, P_) in pend:
                        for hf in range(2):
                            kb_ = 2 * pp_ + hf
                            self.mm(po.v, VA[:, kb_, :], P_[:, hf * 512:(hf + 1) * 512], start=(kb_ == 0), stop=(kb_ == nkb - 1))
                    while pending:
                        pending.pop(0)()
                    rs_, oh_ = rsum[0], oh[qi % 2]
                    self.act(rs_.v, po[64:128, :], AF.Ln)
                    self.act(rs_.v, rs_.v, AF.Exp, scale=-1.0)
                    self.tt('dve', oh_.v, po[0:64, :], rs_.v, ALU.mult)
                    self.dma('sp', o_s[h * 64:(h + 1) * 64, qc_], oh_.v)

                for ti in range(16):
                    for f_ in kgen_stages(0, ti):
                        f_()
                for f_ in qgen_stages(0, 0):
                    f_()
                for h in range(16):
                    for qi in range(16):
                        pending = []
                        qs_ = ks_ = []
                        if qi + 1 < 16:
                            qs_ = qgen_stages(h, qi + 1)
                        elif h + 1 < 16:
                            qs_ = qgen_stages(h + 1, 0)
                        if h + 1 < 16:
                            ks_ = kgen_stages(h + 1, qi)
                        qs_, ks_ = list(qs_), list(ks_)
                        while qs_ or ks_:
                            if qs_:
                                pending.append(qs_.pop(0))
                            if ks_:
                                pending.append(ks_.pop(0))
                        attn(h, qi, pending)
                self.nbn = 4
                self.nbo = 0
        self.p.barrier()
        with ExitStack() as st:
            wo = self.load_w(st, 'mla_w_o', 8, 1024)
            xts = [self.T(st, [128, 8, 512], F32) for _ in range(2)]
            ots = [self.T(st, [128, 8, 512], BF16) for _ in range(2)]
            ot = [self.T(st, [128, 512], F32) for _ in range(2)]
            xv = xin.rearrange("(c p) t -> p c t", p=128)
            ov = o_s.rearrange("(c p) t -> p c t", p=128)
            for ti in range(S // 512):
                t0 = ti * 512
                xt, o_t = xts[ti % 2], ots[ti % 2]
                self.dma('sp', xt.v, xv[:, :, t0:t0 + 512])
                self.dma('sp', o_t.v, ov[:, :, t0:t0 + 512])
                self.out_proj(wo, [o_t[:, j, :] for j in range(8)], xt, ot, xout, t0)


def _colmajor(v, n):
    return np.ascontiguousarray(np.asarray(v, np.float32).reshape(n // 128, 128).T)


def _host_inputs(inputs, layers):
    m = {}
    m["norm_mix"] = np.ascontiguousarray(
        np.asarray(inputs["norm_mix"], np.float32).reshape(4, 8, 128).transpose(2, 0, 1).reshape(128, 32))
    m["norm_ffn"] = np.ascontiguousarray(
        np.asarray(inputs["norm_ffn"], np.float32).reshape(4, 8, 128).transpose(2, 0, 1).reshape(128, 32))
    for L in layers:
        for n in LAYER_W[L]:
            m[n] = np.ascontiguousarray(np.asarray(inputs[n], np.float32)[0])
        for n, ln in LAYER_V[L]:
            m[n] = _colmajor(inputs[n][0], ln)
        m["ffn_w1_%d" % L] = np.ascontiguousarray(np.asarray(inputs["ffn_w1"], np.float32)[L])
        m["ffn_w2_%d" % L] = np.ascontiguousarray(np.asarray(inputs["ffn_w2"], np.float32)[L])
    idx = np.arange(128)
    same = (idx[:, None] // 64) == (idx[None, :] // 64)
    if 1 in layers or 2 in layers:
        m["bcmask"] = (same & (idx[:, None] <= idx[None, :])).astype(np.float32)
    if 0 in layers:
        m["mla_qg"] = np.ascontiguousarray(np.repeat(np.asarray(inputs["mla_q_gain"], np.float32)[0].reshape(96, 1), 16, axis=1))
        m["mla_kg"] = np.ascontiguousarray(np.repeat(np.asarray(inputs["mla_k_gain"], np.float32)[0].reshape(96, 1), 16, axis=1))
        inv = (10000.0 ** (-np.arange(16, dtype=np.float32) / 16.0)).astype(np.float32)
        ang = np.arange(S, dtype=np.float32)[None, :] * inv[:, None]
        cos = np.ones((96, S), np.float32)
        sin = np.zeros((96, S), np.float32)
        cos[64:80] = np.cos(ang)
        cos[80:96] = np.cos(ang)
        sin[64:80] = np.sin(ang)
        sin[80:96] = np.sin(ang)
        m["rope_cos"] = cos
        m["rope_sin"] = sin
        R = np.zeros((96, 96), np.float32)
        for i in range(16):
            R[80 + i, 64 + i] = -1.0
            R[64 + i, 80 + i] = 1.0
        m["rope_R"] = R
    if 1 in layers:
        b = np.asarray(inputs["mlstm_b_if"], np.float32)[0]
        m["mlstm_b_if"] = np.ascontiguousarray(np.tile(np.stack([b[0:4], b[4:8]], axis=1), (1, 8)))
        sel = np.zeros((4, 512), np.float32)
        for h in range(4):
            sel[h, h * 128:(h + 1) * 128] = 1.0
        m["sel4"] = sel
        m["lmat"] = (((idx[:, None] // 32) == (idx[None, :] // 32)) & (idx[:, None] < idx[None, :])).astype(np.float32)
        m["ident"] = np.eye(128, dtype=np.float32)
        m["mneg"] = np.where(((idx[:, None] // 32) == (idx[None, :] // 32)) & (idx[None, :] < idx[:, None]), 0.0, -1e30).astype(np.float32)
    if 2 in layers:
        m["gla_w_a2b"] = np.ascontiguousarray(np.concatenate(
            [np.asarray(inputs["gla_w_a2"], np.float32)[0], np.asarray(inputs["gla_b_a"], np.float32)[0][None, :]], axis=0))
        m["triN"] = (same & (idx[:, None] <= idx[None, :])).astype(np.float32) * (-1.0 / 16.0)
        m["triU"] = (same & (idx[:, None] > idx[None, :])).astype(np.float32) * (-1.0 / 16.0)
    if 3 in layers:
        w = np.asarray(inputs["conv_w_dw"], np.float32)[0]
        m["conv_w_dw"] = np.ascontiguousarray(w.reshape(31, 8, 128).transpose(2, 1, 0).reshape(128, 8 * 31))
        m["identc"] = np.eye(128, dtype=np.float32)
    return m


_NC_CACHE = {}


def run_layers(layers, xT_list, inputs, skip_ffn=False):
    key = (tuple(layers), skip_ffn)
    if key not in _NC_CACHE:
        _NC_CACHE[key] = KB(list(layers), skip_ffn).build()
    nc = _NC_CACHE[key]
    shared = _host_inputs(inputs, layers)
    in_maps = []
    for xT in xT_list:
        mm = dict(shared)
        mm["xT"] = xT
        in_maps.append(mm)
    res = run_bass_kernel_spmd(nc, in_maps, core_ids=list(range(len(xT_list))))
    return [r["yT"] for r in res.results]


def kernel(**inputs):
    x = np.asarray(inputs["x"], np.float32)
    B = x.shape[0]
    xT = [np.ascontiguousarray(x[b % B].T) for b in range(8)]
    outs = run_layers((0, 1, 2, 3), xT, inputs)
    y = np.stack([np.ascontiguousarray(outs[b].T) for b in range(B)], axis=0)
    return y.astype(np.float32)
```

```python
import math
import numpy as np
from contextlib import ExitStack
import concourse.bass as bass
import concourse.mybir as mybir
from concourse.bass_utils import run_bass_kernel_spmd

F32 = mybir.dt.float32
BF16 = mybir.dt.bfloat16
AF = mybir.ActivationFunctionType
ALU = mybir.AluOpType

S = 8192
D = 1024
EPS = 1e-6
COMPUTE = ('pe', 'act', 'dve', 'pool')
ALLENG = ('pe', 'act', 'dve', 'pool', 'sp')
DMA_POOL = 8


class Tile:
    __slots__ = ('ap', 'w', 'r')

    def __init__(self, ap):
        self.ap = ap
        self.w = None
        self.r = []

    def __getitem__(self, k):
        return V(self, self.ap[k])

    @property
    def v(self):
        return V(self, self.ap)


class V:
    __slots__ = ('t', 'ap')

    def __init__(self, t, ap):
        self.t = t
        self.ap = ap

    def __getitem__(self, k):
        return V(self.t, self.ap[k])

    def bc(self, shape):
        return V(self.t, self.ap.to_broadcast(shape))

    def re(self, pat, **kw):
        return V(self.t, self.ap.rearrange(pat, **kw))

    def ub(self, axis, shape):
        return V(self.t, self.ap.unsqueeze(axis).to_broadcast(shape))


def _ap(x):
    return x.ap if isinstance(x, V) else x


def _tl(*xs):
    return [x.t for x in xs if isinstance(x, V)]


class Prog:
    def __init__(self, nc):
        self.nc = nc
        self.ops = {e: [] for e in ALLENG}
        self.ndma = {e: 0 for e in ALLENG}
        self.lastc = {e: None for e in ALLENG}
        self.dmas = {e: [] for e in ALLENG}

    def op(self, eng, fn, reads=(), writes=(), dma=False):
        deps = set()
        for t in reads:
            if t.w is not None:
                deps.add(t.w)
        for t in writes:
            if t.w is not None:
                deps.add(t.w)
            deps.update(t.r)
        me = (eng, len(self.ops[eng]))
        rec = dict(fn=fn, deps=deps, dma=dma, inc=False, val=None, sem=None)
        if dma:
            rec['dj'] = self.ndma[eng]
            self.ndma[eng] += 1
            self.dmas[eng].append(me)
        else:
            self.lastc[eng] = me
        self.ops[eng].append(rec)
        for t in reads:
            t.r.append(me)
        for t in writes:
            t.w = me
            t.r = []
        return me

    def barrier(self):
        deps = set()
        for e in ALLENG:
            if self.lastc[e] is not None:
                deps.add(self.lastc[e])
            deps.update(self.dmas[e][-DMA_POOL:])
        for e in ALLENG:
            self.ops[e].append(dict(fn=None, deps=set(deps), dma=False, inc=False, val=None, sem=None))

    def emit(self, stack):
        nc = self.nc
        ops = self.ops
        for e in ALLENG:
            for o in ops[e]:
                nd = set()
                for (pe_, pi) in o['deps']:
                    p = ops[pe_][pi]
                    if not p['dma']:
                        if pe_ == e and e == 'pe' and not o['dma'] and o['fn'] is not None:
                            continue
                        p['inc'] = True
                    nd.add((pe_, pi))
                o['deps'] = nd
        sems = {e: stack.enter_context(nc.semaphore('s_' + e)) for e in COMPUTE}
        dsems = {}
        for e in ALLENG:
            if self.ndma[e] > 0:
                dsems[e] = [stack.enter_context(nc.semaphore('d_%s_%d' % (e, k))) for k in range(DMA_POOL)]
        for e in ALLENG:
            cnt = 0
            for o in ops[e]:
                if o['dma']:
                    j = o['dj']
                    o['sem'] = dsems[e][j % DMA_POOL]
                    o['val'] = 16 * (j // DMA_POOL + 1)
                elif o['inc']:
                    cnt += 1
                    o['sem'] = sems[e]
                    o['val'] = cnt
        block = stack.enter_context(nc.Block())

        def run(e, engobj):
            waited = {}
            for o in ops[e]:
                need = {}
                for (pe_, pi) in o['deps']:
                    p = ops[pe_][pi]
                    s = p['sem']
                    if waited.get(s.num, 0) >= p['val']:
                        continue
                    if s.num not in need or need[s.num][1] < p['val']:
                        need[s.num] = (s, p['val'])
                if o['dma'] and o['val'] > 16:
                    s = o['sem']
                    v = o['val'] - 16
                    if waited.get(s.num, 0) < v and (s.num not in need or need[s.num][1] < v):
                        need[s.num] = (s, v)
                for key, (s, v) in need.items():
                    engobj.wait_ge(s, v)
                    waited[key] = v
                if o['fn'] is None:
                    continue
                ins = o['fn'](engobj)
                if o['dma']:
                    ins.then_inc(o['sem'], 16)
                elif o['inc']:
                    ins.then_inc(o['sem'], 1)
            n = self.ndma[e]
            for k in range(min(n, DMA_POOL)):
                cntk = (n - 1 - k) // DMA_POOL + 1
                if waited.get(dsems[e][k].num, 0) < 16 * cntk:
                    engobj.wait_ge(dsems[e][k], 16 * cntk)

        @block.tensor
        def _(pe):
            run('pe', pe)

        @block.scalar
        def _(act):
            run('act', act)

        @block.vector
        def _(dve):
            run('dve', dve)

        @block.gpsimd
        def _(pool):
            run('pool', pool)

        @block.sync
        def _(sp):
            run('sp', sp)


LAYER_W = {
    0: ['mla_w_dq', 'mla_w_uq', 'mla_w_dkv', 'mla_w_ukv', 'mla_w_o'],
    1: ['mlstm_w_in', 'mlstm_w_if', 'mlstm_w_o'],
    2: ['gla_w_in', 'gla_w_a1', 'gla_w_o'],
    3: ['conv_w_pw1', 'conv_w_pw2'],
}
WSHAPE = {
    'mla_w_dq': (1024, 384), 'mla_w_uq': (384, 1536), 'mla_w_dkv': (1024, 288), 'mla_w_ukv': (256, 2048),
    'mla_w_o': (1024, 1024), 'mlstm_w_in': (1024, 3072), 'mlstm_w_if': (1024, 8), 'mlstm_w_o': (1024, 1024),
    'gla_w_in': (1024, 3072), 'gla_w_a1': (1024, 16), 'gla_w_o': (1024, 1024),
    'conv_w_pw1': (1024, 2048), 'conv_w_pw2': (1024, 1024),
}
LAYER_V = {
    0: [('mla_q_norm', 384), ('mla_kv_norm', 256)],
    1: [('mlstm_head_norm', 1024)],
    2: [('gla_head_norm', 1024)],
    3: [('conv_b_pw1', 2048), ('conv_b_dw', 1024), ('conv_ln_g', 1024), ('conv_ln_b', 1024), ('conv_b_pw2', 1024)],
}


class KB:
    def __init__(self, layers, skip_ffn=False):
        self.layers = layers
        self.skip_ffn = skip_ffn
        self.nc = bass.Bass("TRN2", target_bir_lowering=False)
        self.p = Prog(self.nc)
        self.gst = ExitStack()
        self.din = {}
        self.uid = 0

    def dram_in(self, name, shape, dt=F32):
        a = self.nc.dram_tensor(name, list(shape), dt, kind="ExternalInput").ap()
        self.din[name] = a
        return a

    def dram_scr(self, name, shape, dt):
        return self.nc.dram_tensor(name, list(shape), dt, kind="Internal").ap()

    def sb(self, st, shape, dt=F32, name=None):
        self.uid += 1
        return st.enter_context(self.nc.sbuf_tensor("%s_%d" % (name or 'sb', self.uid), list(shape), dt))[:]

    def T(self, st, shape, dt=F32, name=None):
        return Tile(self.sb(st, shape, dt, name))

    def mm(self, out, lhsT, rhs, start=True, stop=True):
        o, l, r = _ap(out), _ap(lhsT), _ap(rhs)
        self.p.op('pe', lambda e: e.matmul(o, l, r, start=start, stop=stop), _tl(lhsT, rhs), _tl(out))

    def act(self, out, in_, func, bias=None, scale=None, eng='act'):
        o, i = _ap(out), _ap(in_)
        kw = {}
        if bias is not None:
            kw['bias'] = _ap(bias)
        if scale is not None:
            kw['scale'] = _ap(scale)
        self.p.op('act', lambda e: e.activation(o, i, func, **kw), _tl(in_, bias, scale), _tl(out))

    def tt(self, eng, out, a, b, op):
        o, x, y = _ap(out), _ap(a), _ap(b)
        self.p.op(eng, lambda e: e.tensor_tensor(o, x, y, op), _tl(a, b), _tl(out))

    def stt(self, eng, out, in0, scalar, in1, op0, op1):
        o, x, s, y = _ap(out), _ap(in0), _ap(scalar), _ap(in1)
        self.p.op(eng, lambda e: e.scalar_tensor_tensor(o, x, s, y, op0, op1), _tl(in0, scalar, in1), _tl(out))

    def ts(self, eng, out, in0, s1, s2, op0, op1=None):
        o, x, a, b = _ap(out), _ap(in0), _ap(s1), _ap(s2)
        if op1 is None:
            self.p.op(eng, lambda e: e.tensor_scalar(o, x, a, None, op0), _tl(in0, s1), _tl(out))
        else:
            self.p.op(eng, lambda e: e.tensor_scalar(o, x, a, b, op0, op1), _tl(in0, s1, s2), _tl(out))

    def copy(self, eng, out, in_):
        o, i = _ap(out), _ap(in_)
        if eng == 'act':
            self.p.op('act', lambda e: e.activation(o, i, AF.Copy), _tl(in_), _tl(out))
        else:
            self.p.op(eng, lambda e: e.tensor_copy(o, i), _tl(in_), _tl(out))

    def memset(self, eng, out, val):
        o = _ap(out)
        self.p.op(eng, lambda e: e.memset(o, val), (), _tl(out))

    def recip(self, out, in_):
        o, i = _ap(out), _ap(in_)
        self.p.op('dve', lambda e: e.reciprocal(o, i), _tl(in_), _tl(out))

    def scan(self, out, d0, d1, init, op0, op1):
        o, a, b = _ap(out), _ap(d0), _ap(d1)
        self.p.op('dve', lambda e: e.tensor_tensor_scan(o, a, b, init, op0, op1), _tl(d0, d1), _tl(out))

    def dma(self, q, out, in_, **kw):
        o, i = _ap(out), _ap(in_)
        self.p.op(q, lambda e: e.dma_start(out=o, in_=i, **kw), _tl(in_), _tl(out), dma=True)

    def build(self):
        nc, p = self.nc, self.p
        layers = self.layers
        gst = self.gst
        xin = self.dram_in("xT", (D, S))
        yout = nc.dram_tensor("yT", [D, S], F32, kind="ExternalOutput").ap()
        nmix = self.dram_in("norm_mix", (128, 32))
        nffn = self.dram_in("norm_ffn", (128, 32))
        for L in layers:
            for n in LAYER_W[L]:
                self.dram_in(n, WSHAPE[n])
            for n, ln in LAYER_V[L]:
                self.dram_in(n, (128, ln // 128))
            self.dram_in("ffn_w1_%d" % L, (D, 4096))
            self.dram_in("ffn_w2_%d" % L, (4096, D))
        if 0 in layers:
            self.dram_in("mla_qg", (96, 16))
            self.dram_in("mla_kg", (96, 16))
            self.dram_in("rope_cos", (96, S))
            self.dram_in("rope_sin", (96, S))
            self.dram_in("rope_R", (96, 96))
        if 1 in layers:
            self.dram_in("mlstm_b_if", (4, 16))
            self.dram_in("sel4", (4, 512))
            self.dram_in("lmat", (128, 128))
            self.dram_in("ident", (128, 128))
            self.dram_in("mneg", (128, 128))
        if 2 in layers:
            self.dram_in("gla_w_a2b", (17, 512))
            self.dram_in("triN", (128, 128))
            self.dram_in("triU", (128, 128))
        if 1 in layers or 2 in layers:
            self.dram_in("bcmask", (128, 128))
        if 3 in layers:
            self.dram_in("conv_w_dw", (128, 8 * 31))
            self.dram_in("identc", (128, 128))

        self.ps = []
        self.ps2 = []
        for i in range(4):
            pa_ = gst.enter_context(nc.psum_tensor("psp%d" % i, [128, 1024], F32))[:]
            self.ps2.append(Tile(pa_))
            self.ps.append(Tile(pa_[:, 0:512]))
            self.ps.append(Tile(pa_[:, 512:1024]))
        self.ones_bf = self.T(gst, [128, 128], BF16, 'ones')
        self.memset('pool', self.ones_bf.v, 1.0)
        self.epsT = self.T(gst, [128, 1], F32, 'eps')
        self.memset('pool', self.epsT.v, EPS)
        self.gmix = self.T(gst, [128, 32], F32, 'gmix')
        self.gffn = self.T(gst, [128, 32], F32, 'gffn')
        self.dma('sp', self.gmix.v, nmix)
        self.dma('sp', self.gffn.v, nffn)

        self.wb = {}
        cast_list = []
        ffn_list = []
        for L in layers:
            for n in LAYER_W[L]:
                shp = WSHAPE[n]
                dst = self.dram_scr(n + "_bf", shp, BF16)
                self.wb[n] = dst
                cast_list.append((self.din[n], dst, shp))
            d1 = self.dram_scr("ffn_w1_%d_bf" % L, (8, 128, 8 * 512), BF16)
            d2 = self.dram_scr("ffn_w2_%d_bf" % L, (4, 128, 32 * 256), BF16)
            self.wb["ffn_w1_%d" % L] = d1
            self.wb["ffn_w2_%d" % L] = d2
            ffn_list.append((self.din["ffn_w1_%d" % L], d1, self.din["ffn_w2_%d" % L], d2))
        self.cast_phase(cast_list, ffn_list)
        p.barrier()

        xm = self.dram_scr("x_mid", (D, S), F32)
        xs = [self.dram_scr("x_s0", (D, S), F32), self.dram_scr("x_s1", (D, S), F32)]
        cur = xin
        for li, L in enumerate(layers):
            nxt = yout if li == len(layers) - 1 else xs[li % 2]
            if self.skip_ffn:
                xm = nxt
            if L == 0:
                self.mla_phase(cur, xm)
            elif L == 1:
                self.mlstm_phase(cur, xm)
            elif L == 2:
                self.gla_phase(cur, xm)
            else:
                self.conv_phase(cur, xm)
            p.barrier()
            if not self.skip_ffn:
                self.ffn_phase(L, xm, nxt)
                p.barrier()
            cur = nxt
        p.emit(gst)
        gst.close()
        return nc

    def cast_phase(self, items, ffn_items):
        CH = 8192
        with ExitStack() as st:
            src_t = [self.T(st, [128, CH], F32, 'cs') for _ in range(2)]
            dst_t = [self.T(st, [128, CH], BF16, 'cd') for _ in range(2)]
            i = 0
            engs = ['dve', 'act', 'dve']
            for (src, dst, shp) in items:
                K, N = shp
                M = K * N // 128
                sv = src.rearrange("(p a) n -> p (a n)", p=128)
                dv = dst.rearrange("(p a) n -> p (a n)", p=128)
                for c0 in range(0, M, CH):
                    w = min(CH, M - c0)
                    a, b = src_t[i % 2], dst_t[i % 2]
                    self.dma('sp', a[:, 0:w], sv[:, c0:c0 + w])
                    self.copy(engs[i % 3], b[:, 0:w], a[:, 0:w])
                    self.dma('pool', dv[:, c0:c0 + w], b[:, 0:w])
                    i += 1
            for (s1, d1, s2, d2) in ffn_items:
                s1v = s1.rearrange("(c p) f -> p c f", p=128)
                for fg in range(8):
                    a, b = src_t[i % 2], dst_t[i % 2]
                    self.dma('sp', a[:, 0:4096].re("p (c f) -> p c f", c=8), s1v[:, :, fg * 512:(fg + 1) * 512])
                    self.copy(engs[i % 3], b[:, 0:4096], a[:, 0:4096])
                    self.dma('pool', d1[fg], b[:, 0:4096])
                    i += 1
                s2v = s2.rearrange("(c p) d -> p c d", p=128)
                for dg in range(4):
                    a, b = src_t[i % 2], dst_t[i % 2]
                    av = a.v.re("p (c d) -> p c d", c=32)
                    for q in range(4):
                        self.dma('sp', av[:, q * 8:(q + 1) * 8, :], s2v[:, q * 8:(q + 1) * 8, dg * 256:(dg + 1) * 256])
                    self.copy(engs[i % 3], b.v, a.v)
                    self.dma('pool', d2[dg], b.v)
                    i += 1

    def rmsnorm(self, x_chunks, gcols, hn_chunks, sq_chunks, ps, std, rstd, n_feat, width, eng_alt=('dve',)):
        n = len(x_chunks)
        for c in range(n):
            self.act(sq_chunks[c], x_chunks[c], AF.Square)
        for c in range(n):
            self.mm(ps, self.ones_bf.v, sq_chunks[c], start=(c == 0), stop=(c == n - 1))
        self.act(std, ps, AF.Ln, bias=self.epsT.v, scale=1.0 / n_feat)
        self.act(rstd, std, AF.Exp, scale=-0.5)
        for c in range(n):
            self.stt(eng_alt[c % len(eng_alt)], hn_chunks[c], x_chunks[c], gcols[c], rstd, ALU.mult, ALU.mult)

    def ffn_phase(self, L, xin, xout):
        TT = 1024
        w1 = self.wb["ffn_w1_%d" % L]
        w2 = self.wb["ffn_w2_%d" % L]
        xv = xin.rearrange("(c p) t -> p c t", p=128)
        ps = self.ps
        with ExitStack() as st:
            xt = self.T(st, [128, 8, TT], F32, 'fx')
            hn = [[self.T(st, [128, 512], BF16, 'fhn') for _ in range(2)] for _ in range(8)]
            sq = [self.T(st, [128, 512], BF16, 'fsq') for _ in range(8)]
            std = self.T(st, [128, 512], F32, 'fstd')
            rstd = self.T(st, [128, 512], F32, 'frstd')
            a = [[self.T(st, [128, 512], BF16, 'fa') for _ in range(2)] for _ in range(32)]
            w1t = [self.T(st, [128, 8, 512], BF16, 'fw1') for _ in range(2)]
            w2t = [self.T(st, [128, 32, 256], BF16, 'fw2') for _ in range(2)]
            rl = [self.T(st, [128, 512], F32, 'frl') for _ in range(4)]
            xres = [self.T(st, [128, TT], F32, 'fxr') for _ in range(2)]
            ot = [self.T(st, [128, TT], F32, 'fo') for _ in range(2)]
            nw1 = 0
            nw2 = 0
            nr = 0
            gcols = [self.gffn[:, L * 8 + c:L * 8 + c + 1] for c in range(8)]

            def norm_stages(sti):
                t0 = sti * TT
                stg = [lambda: self.dma('sp', xt.v, xv[:, :, t0:t0 + TT])]
                for half in range(2):
                    xs_ = [xt[:, c, half * 512:(half + 1) * 512] for c in range(8)]

                    def f_sq(xs_=xs_):
                        for c in range(8):
                            self.act(sq[c].v, xs_[c], AF.Square)

                    def f_mm():
                        for c in range(8):
                            self.mm(ps[0].v, self.ones_bf.v, sq[c].v, start=(c == 0), stop=(c == 7))

                    def f_rs():
                        self.act(std.v, ps[0].v, AF.Ln, bias=self.epsT.v, scale=1.0 / D)
                        self.act(rstd.v, std.v, AF.Exp, scale=-0.5)

                    def f_hn(xs_=xs_, half=half):
                        for c in range(8):
                            self.stt('dve', hn[c][half].v, xs_[c], gcols[c], rstd.v, ALU.mult, ALU.mult)
                    stg += [f_sq, f_mm, f_rs, f_hn]
                return stg

            for f_ in norm_stages(0):
                f_()
            nst_ = S // TT
            for sti in range(nst_):
                t0 = sti * TT
                for fg in range(8):
                    wt = w1t[nw1 % 2]
                    nw1 += 1
                    self.dma('sp', wt.v.re("p c f -> p (c f)"), w1[fg])
                    for fi in range(4):
                        f = fg * 4 + fi
                        pb = (f % 2) * 2
                        for k in range(8):
                            for half in range(2):
                                self.mm(ps[pb + half].v, wt[:, k, fi * 128:(fi + 1) * 128], hn[k][half].v,
                                        start=(k == 0), stop=(k == 7))
                        for half in range(2):
                            r = rl[nr % 4]
                            nr += 1
                            self.act(r.v, ps[pb + half].v, AF.Relu)
                            self.tt('pool' if half else 'dve', a[f][half].v, r.v, r.v, ALU.mult)
                pending = norm_stages(sti + 1) if sti + 1 < nst_ else []
                grp = 0
                for dg in range(4):
                    wt = w2t[nw2 % 2]
                    nw2 += 1
                    self.dma('sp', wt.v.re("p c d -> p (c d)"), w2[dg])
                    for dd in range(2):
                        d = dg * 2 + dd
                        xr = xres[d % 2]
                        o = ot[d % 2]
                        self.dma('sp', xr.v, xin[d * 128:(d + 1) * 128, t0:t0 + TT])
                        pb = 4 + (d % 2) * 2
                        for f in range(32):
                            for half in range(2):
                                self.mm(ps[pb + half].v, wt[:, f, dd * 128:(dd + 1) * 128], a[f][half].v,
                                        start=(f == 0), stop=(f == 31))
                        for half in range(2):
                            self.tt('dve', o[:, half * 512:(half + 1) * 512], ps[pb + half].v,
                                    xr[:, half * 512:(half + 1) * 512], ALU.add)
                        self.dma('pool', xout[d * 128:(d + 1) * 128, t0:t0 + TT], o.v)
                        left = 8 - grp
                        grp += 1
                        for _ in range(-(-len(pending) // left)):
                            if pending:
                                pending.pop(0)()
                while pending:
                    pending.pop(0)()

    def load_w(self, st, name, kc, n, cols=None):
        t = self.T(st, [128, kc, n], BF16, 'w')
        src = self.wb[name].rearrange("(c p) n -> p c n", p=128)
        if cols is not None:
            src = src[:, :, cols[0]:cols[1]]
        self.dma('sp', t.v, src)
        return t

    def load_x_norm(self, L, xin, t0, xt, hn, sq, std, rstd, psb):
        self.norm_pre(L, xin, t0, xt, sq)
        self.norm_post(L, xt, hn, sq, std, rstd, psb)

    def norm_pre(self, L, xin, t0, xt, sq):
        xv = xin.rearrange("(c p) t -> p c t", p=128)
        self.dma('sp', xt.v, xv[:, :, t0:t0 + 512])
        for c in range(8):
            self.act(sq[c].v, xt[:, c, :], AF.Square)

    def norm_post(self, L, xt, hn, sq, std, rstd, psb):
        for c in range(8):
            self.mm(psb.v, self.ones_bf.v, sq[c].v, start=(c == 0), stop=(c == 7))
        self.act(std.v, psb.v, AF.Ln, bias=self.epsT.v, scale=1.0 / D)
        self.act(rstd.v, std.v, AF.Exp, scale=-0.5)
        for c in range(8):
            self.stt('dve', hn[c].v, xt[:, c, :], self.gmix[:, L * 8 + c:L * 8 + c + 1], rstd.v, ALU.mult, ALU.mult)

    def out_proj(self, wo, og, xt, ot, xout, t0, bias=None, banks=(6, 7), xres=None):
        ps = self.ps
        for oc in range(8):
            pb = ps[banks[oc % 2]]
            if xres is not None:
                xr = xres[0][oc % 2]
                self.dma('sp', xr.v, xres[1][oc * 128:(oc + 1) * 128, t0:t0 + 512])
                xsrc = xr.v
            else:
                xsrc = xt[:, oc, :]
            for j in range(8):
                self.mm(pb.v, wo[:, j, oc * 128:(oc + 1) * 128], og[j], start=(j == 0), stop=(j == 7))
            o = ot[oc % 2]
            if bias is None:
                self.tt('dve', o.v, pb.v, xsrc, ALU.add)
            else:
                self.stt('dve', o.v, pb.v, bias[:, oc:oc + 1], xsrc, ALU.add, ALU.add)
            self.dma('pool', xout[oc * 128:(oc + 1) * 128, t0:t0 + 512], o.v)

    def conv_phase(self, xin, xout):
        L = 3
        ps = self.ps
        with ExitStack() as st:
            w1 = self.load_w(st, 'conv_w_pw1', 8, 2048)
            w2 = self.load_w(st, 'conv_w_pw2', 8, 1024)
            b1 = self.T(st, [128, 16], F32)
            bdw = self.T(st, [128, 8], F32)
            lg = self.T(st, [128, 8], F32)
            lb = self.T(st, [128, 8], F32)
            b2 = self.T(st, [128, 8], F32)
            wdw = self.T(st, [128, 8 * 31], F32)
            identf = self.T(st, [128, 128], F32)
            for t_, n in ((b1, 'conv_b_pw1'), (bdw, 'conv_b_dw'), (lg, 'conv_ln_g'), (lb, 'conv_ln_b'),
                          (b2, 'conv_b_pw2'), (wdw, 'conv_w_dw'), (identf, 'identc')):
                self.dma('sp', t_.v, self.din[n])
            Dg = [self.T(st, [128, 31, 128], BF16, 'dg') for _ in range(8)]
            for c in range(8):
                self.tt('dve', Dg[c].v, identf.v.ub(1, [128, 31, 128]),
                        wdw[:, c * 31:(c + 1) * 31].ub(2, [128, 31, 128]), ALU.mult)
            xt = self.T(st, [128, 8, 512], F32, 'cx')
            hn = [self.T(st, [128, 512], BF16) for _ in range(8)]
            sq = [self.T(st, [128, 512], BF16) for _ in range(8)]
            std = self.T(st, [128, 512], F32)
            rstd = self.T(st, [128, 512], F32)
            u = [self.T(st, [128, 542], BF16, 'cu') for _ in range(8)]
            vv = [self.T(st, [128, 512], F32, 'cv') for _ in range(8)]
            vb = [self.T(st, [128, 512], BF16) for _ in range(8)]
            sig = [self.T(st, [128, 512], F32) for _ in range(2)]
            mean = self.T(st, [128, 512], F32)
            m2 = self.T(st, [128, 512], F32)
            var = self.T(st, [128, 512], F32)
            z = [self.T(st, [128, 512], BF16) for _ in range(8)]
            ot = [self.T(st, [128, 512], F32) for _ in range(2)]
            for c in range(8):
                self.memset('pool', u[c][:, 0:30], 0.0)
            xrs = [self.T(st, [128, 512], F32) for _ in range(2)]
            self.load_x_norm(L, xin, 0, xt, hn, sq, std, rstd, ps[5])
            for ti in range(S // 512):
                t0 = ti * 512
                for oc in range(8):
                    pa, pg = ps[(oc % 2) * 2], ps[(oc % 2) * 2 + 1]
                    for k in range(8):
                        self.mm(pa.v, w1[:, k, oc * 128:(oc + 1) * 128], hn[k].v, start=(k == 0), stop=(k == 7))
                    for k in range(8):
                        self.mm(pg.v, w1[:, k, 1024 + oc * 128:1024 + (oc + 1) * 128], hn[k].v,
                                start=(k == 0), stop=(k == 7))
                    sg = sig[oc % 2]
                    self.act(sg.v, pg.v, AF.Sigmoid, bias=b1[:, 8 + oc:9 + oc])
                    self.stt('dve', u[oc][:, 30:542], pa.v, b1[:, oc:oc + 1], sg.v, ALU.add, ALU.mult)
                for c in range(8):
                    pc = ps[c % 4]
                    for k in range(31):
                        self.mm(pc.v, Dg[c][:, k, :], u[c][:, k:k + 512], start=(k == 0), stop=(k == 30))
                    self.act(vv[c].v, pc.v, AF.Identity, bias=bdw[:, c:c + 1])
                    self.copy('pool', u[c][:, 0:30], u[c][:, 512:542])
                for c in range(8):
                    self.copy('act', vb[c].v, vv[c].v)
                    self.act(sq[c].v, vv[c].v, AF.Square)
                for c in range(8):
                    self.mm(ps[4].v, self.ones_bf.v, vb[c].v, start=(c == 0), stop=(c == 7))
                for c in range(8):
                    self.mm(ps[5].v, self.ones_bf.v, sq[c].v, start=(c == 0), stop=(c == 7))
                self.act(mean.v, ps[4].v, AF.Copy, scale=1.0 / D)
                self.tt('dve', m2.v, mean.v, mean.v, ALU.mult)
                self.stt('dve', var.v, ps[5].v, 1.0 / D, m2.v, ALU.mult, ALU.subtract)
                self.act(std.v, var.v, AF.Ln, bias=self.epsT.v)
                self.act(rstd.v, std.v, AF.Exp, scale=-0.5)
                for c in range(8):
                    eng = 'dve' if c % 2 == 0 else 'pool'
                    self.tt(eng, vv[c].v, vv[c].v, mean.v, ALU.subtract)
                for c in range(8):
                    eng = 'dve' if c % 2 == 0 else 'pool'
                    self.tt(eng, vv[c].v, vv[c].v, rstd.v, ALU.mult)
                for c in range(8):
                    self.act(z[c].v, vv[c].v, AF.Silu, bias=lb[:, c:c + 1], scale=lg[:, c:c + 1])
                if ti + 1 < S // 512:
                    self.norm_pre(L, xin, t0 + 512, xt, sq)
                self.out_proj(w2, [z[c].v for c in range(8)], xt, ot, xout, t0, bias=b2, xres=(xrs, xin))
                if ti + 1 < S // 512:
                    self.norm_post(L, xt, hn, sq, std, rstd, ps[5])

    nbn = 4
    nbo = 0

    def nb(self):
        self._nb = (getattr(self, '_nb', -1) + 1) % self.nbn
        return self.ps[self.nbo + self._nb]

    def head_finalize(self, st_tiles, O, win, hn, gain, gate_func, og, n_in_head):
        sq, std, rstd, rs, tmpo = st_tiles
        ps = self.ps
        for h in range(4):
            for vc in range(2):
                self.act(sq[h * 2 + vc].v, O[h][:, vc, :], AF.Square)
            for vc in range(2):
                self.mm(ps[5].v, self.ones_bf.v, sq[h * 2 + vc].v, start=(vc == 0), stop=(vc == 1))
            self.act(std.v, ps[5].v, AF.Ln, bias=self.epsT.v, scale=1.0 / 256)
            self.act(rstd.v, std.v, AF.Exp, scale=-0.5)
            for vc in range(2):
                j = h * 2 + vc
                pr = self.nb()
                for k in range(8):
                    self.mm(pr.v, win[:, k, 2048 + j * 128:2048 + (j + 1) * 128], hn[k].v, start=(k == 0), stop=(k == 7))
                self.act(rs[j % 2].v, pr.v, gate_func)
                self.stt('dve', tmpo[j % 2].v, O[h][:, vc, :], gain[:, j:j + 1], rstd.v, ALU.mult, ALU.mult)
                self.tt('pool', og[j].v, tmpo[j % 2].v, rs[j % 2].v, ALU.mult)

    def gla_phase(self, xin, xout):
        L = 2
        ps = self.ps
        with ExitStack() as st:
            win = self.load_w(st, 'gla_w_in', 8, 3072)
            wo = self.load_w(st, 'gla_w_o', 8, 1024)
            wa1 = self.load_w(st, 'gla_w_a1', 8, 16)
            wa2f = self.T(st, [17, 512], F32)
            wa2 = self.T(st, [17, 512], BF16)
            triN = self.T(st, [128, 128], F32)
            triU = self.T(st, [128, 128], F32)
            mask = self.T(st, [128, 128], F32)
            hgn = self.T(st, [128, 8], F32)
            onesc = self.T(st, [128, 1], F32)
            self.memset('pool', onesc.v, 1.0)
            for t_, n in ((wa2f, 'gla_w_a2b'), (triN, 'triN'), (triU, 'triU'), (mask, 'bcmask'), (hgn, 'gla_head_norm')):
                self.dma('sp', t_.v, self.din[n])
            self.copy('dve', wa2.v, wa2f.v)
            xt = self.T(st, [128, 8, 512], F32, 'gx')
            hn = [self.T(st, [128, 512], BF16) for _ in range(8)]
            sq = [self.T(st, [128, 512], BF16) for _ in range(8)]
            std = self.T(st, [128, 512], F32)
            rstd = self.T(st, [128, 512], F32)
            g1a = self.T(st, [17, 512], BF16)
            self.memset('pool', g1a.v, 1.0)
            lsp = [self.T(st, [128, 512], F32) for _ in range(4)]
            ez = self.T(st, [128, 512], F32)
            ep = [self.T(st, [128, 512], F32) for _ in range(2)]
            em = [self.T(st, [128, 512], F32) for _ in range(2)]
            eb = [self.T(st, [128, 8], F32) for _ in range(4)]
            qt = [self.T(st, [128, 512], BF16) for _ in range(4)]
            kt = [self.T(st, [128, 512], BF16) for _ in range(4)]
            vtm = [self.T(st, [128, 1024], BF16) for _ in range(4)]
            kd = [self.T(st, [128, 512], BF16) for _ in range(4)]
            erev = [self.T(st, [128, 512], F32) for _ in range(2)]
            attm = [self.T(st, [128, 4, 128], BF16) for _ in range(2)]
            Sst = [self.T(st, [128, 256], F32) for _ in range(4)]
            Sb = [self.T(st, [128, 256], BF16) for _ in range(4)]
            for h in range(4):
                self.memset('pool', Sst[h].v, 0.0)
                self.memset('pool', Sb[h].v, 0.0)
            O = [self.T(st, [128, 2, 512], F32) for _ in range(4)]
            rs = [self.T(st, [128, 512], F32) for _ in range(2)]
            tmpo = [self.T(st, [128, 512], F32) for _ in range(2)]
            og = [self.T(st, [128, 512], BF16) for _ in range(8)]
            ot = [self.T(st, [128, 512], F32) for _ in range(2)]
            poh = [Tile(ps[6 + i // 2].ap[:, (i % 2) * 256:(i % 2) * 256 + 256]) for i in range(4)]
            sc = 128 ** -0.5
            xrs = [self.T(st, [128, 512], F32) for _ in range(2)]
            self.load_x_norm(L, xin, 0, xt, hn, sq, std, rstd, ps[5])
            for ti in range(S // 512):
                t0 = ti * 512
                pb = self.nb()
                for k in range(8):
                    self.mm(pb[0:16, :], wa1[:, k, :], hn[k].v, start=(k == 0), stop=(k == 7))
                self.copy('act', g1a[0:16, :], pb[0:16, :])
                for b in range(4):
                    pz = self.nb()
                    self.mm(pz.v, g1a[0:17, b * 128:(b + 1) * 128], wa2.v)
                    self.act(ez.v, pz.v, AF.Exp, scale=-1.0)
                    self.act(lsp[b].v, ez.v, AF.Ln, bias=onesc.v)
                for h in range(4):
                    pc = self.nb()
                    for b in range(4):
                        self.mm(pc[:, b * 128:(b + 1) * 128], lsp[b][:, h * 128:(h + 1) * 128], triN.v)
                    e_p, e_m = ep[h % 2], em[h % 2]
                    self.act(e_p.v, pc.v, AF.Exp)
                    self.act(e_m.v, pc.v, AF.Exp, scale=-1.0)
                    self.copy('pool', eb[h].v, e_p.v.re("p (c s) -> p c s", s=64)[:, :, 63])
                    pq = self.nb()
                    for k in range(8):
                        self.mm(pq.v, win[:, k, h * 128:(h + 1) * 128], hn[k].v, start=(k == 0), stop=(k == 7))
                    self.stt('dve', qt[h].v, pq.v, sc, e_p.v, ALU.mult, ALU.mult)
                    pk = self.nb()
                    for k in range(8):
                        self.mm(pk.v, win[:, k, 512 + h * 128:512 + (h + 1) * 128], hn[k].v, start=(k == 0), stop=(k == 7))
                    self.tt('dve', kt[h].v, pk.v, e_m.v, ALU.mult)
                for b in range(4):
                    bc = slice(b * 128, (b + 1) * 128)
                    for half in range(2):
                        pv = self.nb()
                        for k in range(8):
                            self.mm(pv.v, hn[k][:, bc], win[:, k, 1024 + half * 512:1024 + (half + 1) * 512],
                                    start=(k == 0), stop=(k == 7))
                        self.copy('act' if half else 'dve', vtm[b][:, half * 512:(half + 1) * 512], pv.v)
                    pk2 = self.nb()
                    for k in range(8):
                        self.mm(pk2.v, hn[k][:, bc], win[:, k, 512:1024], start=(k == 0), stop=(k == 7))
                    pr = self.nb()
                    self.mm(pr.v, triU.v, lsp[b].v)
                    er = erev[b % 2]
                    self.act(er.v, pr.v, AF.Exp)
                    self.tt('dve', kd[b].v, pk2.v, er.v, ALU.mult)
                for b in range(4):
                    bc = slice(b * 128, (b + 1) * 128)
                    pa = ps[4]
                    am = attm[b % 2]
                    for h in range(4):
                        self.mm(pa[:, h * 128:(h + 1) * 128], kt[h][:, bc], qt[h][:, bc])
                    self.tt('dve', am.v, pa.v.re("p (h t) -> p h t", h=4), mask.v.ub(1, [128, 4, 128]), ALU.mult)
                    for h in range(4):
                        po = poh[h]
                        for vc in range(2):
                            self.mm(po[:, vc * 128:(vc + 1) * 128], vtm[b][:, h * 256 + vc * 128:h * 256 + (vc + 1) * 128],
                                    am[:, h, :], start=(vc == 0 and h % 2 == 0), stop=False)
                    for X in range(2):
                        rows = slice(X * 64, (X + 1) * 64)
                        cols = slice(b * 128 + X * 64, b * 128 + (X + 1) * 64)
                        cl = b * 2 + X
                        for h in range(4):
                            po = poh[h]
                            for vc in range(2):
                                self.mm(po[:, vc * 128 + X * 64:vc * 128 + (X + 1) * 64], Sb[h][:, vc * 128:(vc + 1) * 128],
                                        qt[h][:, cols], start=False, stop=True)
                        pus = []
                        for h in range(4):
                            pu = self.nb()
                            pus.append(pu)
                            self.mm(pu[:, 0:256], kd[b][rows, h * 128:(h + 1) * 128], vtm[b][rows, h * 256:(h + 1) * 256])
                        for h in range(4):
                            self.stt('dve', Sst[h].v, Sst[h].v, eb[h][:, cl:cl + 1], pus[h][:, 0:256], ALU.mult, ALU.add)
                        for h in range(4):
                            self.copy('act', Sb[h].v, Sst[h].v)
                    for h in range(4):
                        self.copy('act', O[h][:, :, bc], poh[h].v.re("p (v t) -> p v t", v=2))
                self.head_finalize((sq, std, rstd, rs, tmpo), O, win, hn, hgn, AF.Silu, og, 256)
                if ti + 1 < S // 512:
                    self.norm_pre(L, xin, t0 + 512, xt, sq)
                self.out_proj(wo, [og[j].v for j in range(8)], xt, ot, xout, t0, banks=(4, 5), xres=(xrs, xin))
                if ti + 1 < S // 512:
                    self.norm_post(L, xt, hn, sq, std, rstd, ps[5])


    def mlstm_phase(self, xin, xout):
        L = 1
        ps = self.ps
        nc = self.nc
        gi_s = self.dram_scr("ml_gi", (4, S), F32)
        gf_s = self.dram_scr("ml_gf", (4, S), F32)
        em_s = self.dram_scr("ml_em", (4, S), F32)
        wa_s = self.dram_scr("ml_wa", (4, S), F32)
        wc_s = self.dram_scr("ml_wc", (4, 128), F32)
        with ExitStack() as st:
            wif = self.load_w(st, 'mlstm_w_if', 8, 8)
            bif = self.T(st, [4, 16], F32)
            bi15 = self.T(st, [4, 16], F32)
            onesc = self.T(st, [128, 1], F32)
            self.memset('pool', onesc.v, 1.0)
            self.dma('sp', bif.v, self.din['mlstm_b_if'])
            self.ts('dve', bi15.v, bif.v, 1.0 / 15.0, None, ALU.mult)
            xts = [self.T(st, [128, 8, 512], F32) for _ in range(2)]
            hn = [self.T(st, [128, 512], BF16) for _ in range(8)]
            sq = [self.T(st, [128, 512], BF16) for _ in range(8)]
            std = self.T(st, [128, 512], F32)
            rstd = self.T(st, [128, 512], F32)
            t1 = [self.T(st, [4, 512], F32) for _ in range(2)]
            t2 = [self.T(st, [4, 512], F32) for _ in range(2)]
            t3 = [self.T(st, [4, 512], F32) for _ in range(2)]
            li = [self.T(st, [4, 512], F32) for _ in range(2)]
            lf = [self.T(st, [4, 512], F32) for _ in range(2)]
            self.load_x_norm(L, xin, 0, xts[0], hn, sq, std, rstd, ps[5])
            for ti in range(S // 512):
                t0 = ti * 512
                xt = xts[ti % 2]
                if ti + 1 < S // 512:
                    self.norm_pre(L, xin, t0 + 512, xts[(ti + 1) % 2], sq)
                pgi, pgf = self.nb(), self.nb()
                for k in range(8):
                    self.mm(pgi[0:4, :], wif[:, k, 0:4], hn[k].v, start=(k == 0), stop=(k == 7))
                for k in range(8):
                    self.mm(pgf[0:4, :], wif[:, k, 4:8], hn[k].v, start=(k == 0), stop=(k == 7))
                a1, a2, a3, l_i, l_f = t1[ti % 2], t2[ti % 2], t3[ti % 2], li[ti % 2], lf[ti % 2]
                self.act(a1.v, pgi[0:4, :], AF.Tanh, bias=bi15[:, 0:1], scale=1.0 / 15.0)
                self.ts('dve', l_i.v, a1.v, 15.0, None, ALU.mult)
                self.act(a2.v, pgf[0:4, :], AF.Tanh, bias=bi15[:, 1:2], scale=1.0 / 15.0)
                self.act(a3.v, a2.v, AF.Exp, scale=-15.0)
                self.act(a2.v, a3.v, AF.Ln, bias=onesc[0:4, :])
                self.ts('dve', l_f.v, a2.v, -1.0, None, ALU.mult)
                self.dma('sp', gi_s[:, t0:t0 + 512], l_i.v)
                self.dma('sp', gf_s[:, t0:t0 + 512], l_f.v)
                if ti + 1 < S // 512:
                    self.norm_post(L, xts[(ti + 1) % 2], hn, sq, std, rstd, ps[5])
        self.p.barrier()
        with ExitStack() as st:
            def t_(shape, dt=F32):
                return self.T(st, shape, dt)
            Li, Lf, onesr, Floc, Fg, a_, Aloc, Ap, tmpA, wa, emx, Ab = [t_([128, 256]) for _ in range(12)]
            lmat, ident, mneg, rb, rowv = [t_([128, 128]) for _ in range(5)]
            Gs, Apre = t_([128, 1]), t_([128, 1])
            Aend, Astart, wc = t_([128, 4]), t_([128, 4]), t_([128, 4])
            self.dma('sp', Li.v, gi_s.rearrange("h (s t) -> (h s) t", t=256))
            self.dma('sp', Lf.v, gf_s.rearrange("h (s t) -> (h s) t", t=256))
            self.dma('sp', lmat.v, self.din['lmat'])
            self.dma('sp', ident.v, self.din['ident'])
            self.dma('sp', mneg.v, self.din['mneg'])
            self.memset('pool', onesr.v, 1.0)
            self.scan(Floc.v, onesr.v, Lf.v, 0.0, ALU.mult, ALU.add)
            self.copy('dve', rb.v, Floc[:, 255:256].bc([128, 128]))
            pg = self.nb()
            self.mm(pg[:, 0:128], lmat.v, rb.v)
            self.copy('act', Gs.v, pg[:, 0:1])
            self.ts('dve', Fg.v, Floc.v, Gs[:, 0:1], None, ALU.add)
            self.tt('dve', a_.v, Li.v, Fg.v, ALU.subtract)
            self.ts('dve', Aloc.v, a_.v, 0.0, None, ALU.max)
            src, dst = Aloc, Ab
            d = 1
            while d < 256:
                self.tt('dve', dst[:, d:256], src[:, d:256], src[:, 0:256 - d], ALU.max)
                self.copy('dve', dst[:, 0:d], src[:, 0:d])
                src, dst = dst, src
                d *= 2
            Aloc = src
            self.copy('dve', rb.v, Aloc[:, 255:256].bc([128, 128]))
            pr = self.nb()
            self.mm(pr[:, 0:128], rb.v, ident.v)
            self.tt('dve', rowv.v, pr[:, 0:128], mneg.v, ALU.add)
            self.p.op('dve', (lambda o_, i_: (lambda e: e.tensor_reduce(o_, i_, mybir.AxisListType.X, ALU.max)))(Apre.ap, rowv.ap),
                      [rowv], [Apre])
            self.ts('dve', Apre.v, Apre.v, 0.0, None, ALU.max)
            self.ts('dve', Ap.v, Aloc.v, Apre[:, 0:1], None, ALU.max)
            self.copy('dve', Aend.v, Ap.v.re("p (j t) -> p j t", t=64)[:, :, 63])
            self.copy('dve', Astart[:, 0:1], Apre.v)
            self.copy('dve', Astart[:, 1:4], Aend[:, 0:3])
            self.tt('dve', wc.v, Astart.v, Aend.v, ALU.subtract)
            self.act(wc.v, wc.v, AF.Exp)
            self.dma('sp', wc_s.rearrange("h (s j) -> (h s) j", j=4), wc.v)
            self.tt('dve', tmpA.v.re("p (j t) -> p j t", t=64), a_.v.re("p (j t) -> p j t", t=64),
                    Aend.v.ub(2, [128, 4, 64]), ALU.subtract)
            self.act(wa.v, tmpA.v, AF.Exp)
            self.dma('pool', wa_s.rearrange("h (s t) -> (h s) t", t=256), wa.v)
            self.tt('dve', tmpA.v.re("p (j t) -> p j t", t=64), Fg.v.re("p (j t) -> p j t", t=64),
                    Aend.v.ub(2, [128, 4, 64]), ALU.add)
            self.act(emx.v, tmpA.v, AF.Exp)
            self.dma('pool', em_s.rearrange("h (s t) -> (h s) t", t=256), emx.v)
        self.p.barrier()
        with ExitStack() as st:
            win = self.load_w(st, 'mlstm_w_in', 8, 3072)
            wo = self.load_w(st, 'mlstm_w_o', 8, 1024)
            mask = self.T(st, [128, 128], F32)
            hgn = self.T(st, [128, 8], F32)
            sel4 = self.T(st, [4, 512], F32)
            ident = self.T(st, [128, 128], F32)
            for t_, n in ((mask, 'bcmask'), (hgn, 'mlstm_head_norm'), (sel4, 'sel4'), (ident, 'ident')):
                self.dma('sp', t_.v, self.din[n])
            xt = self.T(st, [128, 8, 512], F32)
            hn = [self.T(st, [128, 512], BF16) for _ in range(8)]
            sq = [self.T(st, [128, 512], BF16) for _ in range(8)]
            std = self.T(st, [128, 512], F32)
            rstd = self.T(st, [128, 512], F32)
            emT = self.T(st, [4, 512], F32)
            waT = self.T(st, [4, 512], F32)
            wcT = self.T(st, [4, 128], F32)
            watm = self.T(st, [128, 16], F32)
            wcb = self.T(st, [128, 4, 128], F32)
            embc = [self.T(st, [128, 512], F32) for _ in range(2)]
            qs = [self.T(st, [128, 512], BF16) for _ in range(4)]
            kt = [self.T(st, [128, 512], BF16) for _ in range(4)]
            vaug = [self.T(st, [128, 4, 384], BF16) for _ in range(4)]
            for b in range(4):
                self.memset('pool', vaug[b].v, 1.0)
            kw = [self.T(st, [128, 4, 128], BF16) for _ in range(4)]
            qkw = [self.T(st, [128, 4, 128], BF16) for _ in range(2)]
            Sst = [self.T(st, [128, 384], F32) for _ in range(4)]
            Sb = [self.T(st, [128, 384], BF16) for _ in range(4)]
            for h in range(4):
                self.memset('pool', Sst[h].v, 0.0)
            dn = [self.T(st, [128, 128], F32) for _ in range(2)]
            rdn = [self.T(st, [128, 128], F32) for _ in range(2)]
            O = [self.T(st, [128, 2, 512], F32) for _ in range(4)]
            rs = [self.T(st, [128, 512], F32) for _ in range(2)]
            tmpo = [self.T(st, [128, 512], F32) for _ in range(2)]
            og = [self.T(st, [128, 512], BF16) for _ in range(8)]
            ot = [self.T(st, [128, 512], F32) for _ in range(2)]
            sc = 128 ** -0.5
            self.dma('sp', wcT.v, wc_s)
            for h in range(4):
                pwc = self.nb()
                self.mm(pwc[:, 0:128], sel4[0:4, h * 128:(h + 1) * 128], wcT.v)
                self.copy('act', wcb[:, h, :], pwc[:, 0:128])
            xrs = [self.T(st, [128, 512], F32) for _ in range(2)]
            self.load_x_norm(L, xin, 0, xt, hn, sq, std, rstd, ps[5])
            for ti in range(S // 512):
                t0 = ti * 512
                self.dma('sp', emT.v, em_s[:, t0:t0 + 512])
                self.dma('sp', waT.v, wa_s[:, t0:t0 + 512])
                pw = self.nb()
                for b in range(4):
                    self.mm(pw[:, b * 128:(b + 1) * 128], waT[0:4, b * 128:(b + 1) * 128], ident[0:4, 0:128])
                self.ts('dve', watm.v.re("p (b h) -> p b h", h=4), pw.v.re("p (b n) -> p b n", n=128)[:, :, 0:4], sc, None, ALU.mult)
                for h in range(4):
                    pe_ = self.nb()
                    self.mm(pe_.v, sel4[0:4, h * 128:(h + 1) * 128], emT.v)
                    eb_ = embc[h % 2]
                    self.copy('act', eb_.v, pe_.v)
                    pq = self.nb()
                    for k in range(8):
                        self.mm(pq.v, win[:, k, h * 128:(h + 1) * 128], hn[k].v, start=(k == 0), stop=(k == 7))
                    self.tt('dve', qs[h].v, pq.v, eb_.v, ALU.mult)
                    pk = self.nb()
                    for k in range(8):
                        self.mm(pk.v, win[:, k, 512 + h * 128:512 + (h + 1) * 128], hn[k].v, start=(k == 0), stop=(k == 7))
                    self.copy('act', kt[h].v, pk.v)
                for b in range(4):
                    bc = slice(b * 128, (b + 1) * 128)
                    for half in range(2):
                        pv = self.nb()
                        for k in range(8):
                            self.mm(pv.v, hn[k][:, bc], win[:, k, 1024 + half * 512:1024 + (half + 1) * 512],
                                    start=(k == 0), stop=(k == 7))
                        self.copy('act' if half else 'dve', vaug[b][:, 2 * half:2 * half + 2, 0:256],
                                  pv.v.re("p (h v) -> p h v", h=2))
                    pk2 = self.nb()
                    for k in range(8):
                        self.mm(pk2.v, hn[k][:, bc], win[:, k, 512:1024], start=(k == 0), stop=(k == 7))
                    self.tt('dve', kw[b].v, pk2.v.re("p (h d) -> p h d", h=4),
                            watm[:, b * 4:(b + 1) * 4].ub(2, [128, 4, 128]), ALU.mult)
                for b in range(4):
                    bc = slice(b * 128, (b + 1) * 128)
                    pa = self.nb()
                    qk_ = qkw[b % 2]
                    for h in range(4):
                        self.mm(pa[:, h * 128:(h + 1) * 128], kt[h][:, bc], qs[h][:, bc])
                    for h in range(4):
                        self.stt('dve', qk_[:, h, :], pa[:, h * 128:(h + 1) * 128], watm[:, b * 4 + h:b * 4 + h + 1],
                                 mask.v, ALU.mult, ALU.mult)
                    for h in range(4):
                        po = ps[4 + h]
                        for j in range(3):
                            self.mm(po[:, j * 128:(j + 1) * 128], vaug[b][:, h, j * 128:(j + 1) * 128], qk_[:, h, :],
                                    start=(j == 0), stop=False)
                    for X in range(2):
                        rows = slice(X * 64, (X + 1) * 64)
                        cols = slice(b * 128 + X * 64, b * 128 + (X + 1) * 64)
                        cl = b * 2 + X
                        for h in range(4):
                            self.act(Sb[h].v, Sst[h].v, AF.Copy, scale=wcb[:, h, ti * 8 + cl:ti * 8 + cl + 1])
                        for h in range(4):
                            po = ps[4 + h]
                            for j in range(3):
                                self.mm(po[:, j * 128 + X * 64:j * 128 + (X + 1) * 64], Sb[h][:, j * 128:(j + 1) * 128],
                                        qs[h][:, cols], start=False, stop=True)
                        pus = []
                        for h in range(4):
                            pu = self.nb()
                            pus.append(pu)
                            self.mm(pu[:, 0:384], kw[b][rows, h, :], vaug[b][rows, h, :])
                        for h in range(4):
                            self.stt('dve', Sst[h].v, Sst[h].v, wcb[:, h, ti * 8 + cl:ti * 8 + cl + 1], pus[h][:, 0:384],
                                     ALU.mult, ALU.add)
                    for h in range(4):
                        po = ps[4 + h]
                        d_, r_ = dn[h % 2], rdn[h % 2]
                        self.act(d_.v, po[:, 256:384], AF.Abs)
                        self.ts('dve', d_.v, d_.v, 1.0, None, ALU.max)
                        self.act(r_.v, d_.v, AF.Ln)
                        self.act(r_.v, r_.v, AF.Exp, scale=-1.0)
                        self.tt('dve', O[h][:, :, bc], po[:, 0:256].re("p (v t) -> p v t", v=2),
                                r_.v.ub(1, [128, 2, 128]), ALU.mult)
                self.head_finalize((sq, std, rstd, rs, tmpo), O, win, hn, hgn, AF.Sigmoid, og, 256)
                if ti + 1 < S // 512:
                    self.norm_pre(L, xin, t0 + 512, xt, sq)
                self.out_proj(wo, [og[j].v for j in range(8)], xt, ot, xout, t0, banks=(4, 5), xres=(xrs, xin))
                if ti + 1 < S // 512:
                    self.norm_post(L, xt, hn, sq, std, rstd, ps[5])


    def normrope(self, tl, x_pre, cos_t, sin_t, g, Rg, bias_ap, scale, out):
        sq96, xb, std96, rstd96, t1, t2 = tl
        self.act(sq96.v, x_pre.v, AF.Square)
        pss = self.nb()
        self.mm(pss[0:96, :], self.ones_bf[0:96, 0:96], sq96.v)
        self.act(std96.v, pss[0:96, :], AF.Ln, bias=bias_ap, scale=scale)
        self.act(rstd96.v, std96.v, AF.Exp, scale=-0.5)
        self.copy('act', xb.v, x_pre.v)
        prot = self.nb()
        self.mm(prot[0:96, :], Rg.v, xb.v)
        self.stt('dve', t1.v, x_pre.v, g[:, 0:1], cos_t.v, ALU.mult, ALU.mult)
        self.tt('dve', t2.v, prot[0:96, :], sin_t.v, ALU.mult)
        self.tt('pool', t1.v, t1.v, t2.v, ALU.add)
        self.tt('pool', out, t1.v, rstd96.v, ALU.mult)

    def mla_phase(self, xin, xout):
        L = 0
        ps = self.ps
        kr_s = self.dram_scr("mla_kr", (32, S), F32)
        o_s = self.dram_scr("mla_o", (D, S), BF16)
        cosd, sind = self.din['rope_cos'], self.din['rope_sin']
        with ExitStack() as st0:
            cqn = [self.T(st0, [128, S], BF16, 'cqn') for _ in range(3)]
            ckvn = [self.T(st0, [128, S], BF16, 'ckvn') for _ in range(2)]
            with ExitStack() as st:
                wdq = self.load_w(st, 'mla_w_dq', 8, 384)
                wdkv = self.load_w(st, 'mla_w_dkv', 8, 288)
                qn = self.T(st, [128, 3], F32)
                kvn = self.T(st, [128, 2], F32)
                self.dma('sp', qn.v, self.din['mla_q_norm'])
                self.dma('sp', kvn.v, self.din['mla_kv_norm'])
                xts = [self.T(st, [128, 8, 512], F32) for _ in range(2)]
                hn = [self.T(st, [128, 512], BF16) for _ in range(8)]
                sq = [self.T(st, [128, 512], BF16) for _ in range(8)]
                std = self.T(st, [128, 512], F32)
                rstd = self.T(st, [128, 512], F32)
                std2 = self.T(st, [128, 512], F32)
                rstd2 = self.T(st, [128, 512], F32)
                krt = [self.T(st, [32, 512], F32) for _ in range(2)]
                for ti in range(S // 512):
                    t0 = ti * 512
                    tc_ = slice(t0, t0 + 512)
                    xt = xts[ti % 2]
                    self.load_x_norm(L, xin, t0, xt, hn, sq, std, rstd, ps[7])
                    for j in range(3):
                        for k in range(8):
                            self.mm(ps[j].v, wdq[:, k, j * 128:(j + 1) * 128], hn[k].v, start=(k == 0), stop=(k == 7))
                    self.rmsnorm([ps[j].v for j in range(3)], [qn[:, j:j + 1] for j in range(3)],
                                 [cqn[j][:, tc_] for j in range(3)], [sq[j].v for j in range(3)],
                                 ps[3].v, std2.v, rstd2.v, 384, 512)
                    for j in range(2):
                        for k in range(8):
                            self.mm(ps[4 + j].v, wdkv[:, k, j * 128:(j + 1) * 128], hn[k].v, start=(k == 0), stop=(k == 7))
                    self.rmsnorm([ps[4 + j].v for j in range(2)], [kvn[:, j:j + 1] for j in range(2)],
                                 [ckvn[j][:, tc_] for j in range(2)], [sq[3 + j].v for j in range(2)],
                                 ps[6].v, std2.v, rstd2.v, 256, 512)
                    pk = ps[7]
                    for k in range(8):
                        self.mm(pk[0:32, :], wdkv[:, k, 256:288], hn[k].v, start=(k == 0), stop=(k == 7))
                    kr = krt[ti % 2]
                    self.copy('act', kr.v, pk[0:32, :])
                    self.dma('sp', kr_s[:, tc_], kr.v)
            self.p.barrier()
            with ExitStack() as st:
                wuq = self.load_w(st, 'mla_w_uq', 3, 1536)
                wukv = self.load_w(st, 'mla_w_ukv', 2, 2048)
                qg = self.T(st, [96, 16], F32)
                kg = self.T(st, [96, 16], F32)
                Rf = self.T(st, [96, 96], F32)
                Rq = self.T(st, [96, 96], BF16)
                Rk = self.T(st, [96, 96], BF16)
                epsq = self.T(st, [96, 1], F32)
                self.memset('pool', epsq.v, EPS * 96.0)
                self.dma('sp', qg.v, self.din['mla_qg'])
                self.dma('sp', kg.v, self.din['mla_kg'])
                self.dma('sp', Rf.v, self.din['rope_R'])
                self.ts('dve', Rq.v, Rf.v, qg[:, 0:1], None, ALU.mult)
                self.ts('dve', Rk.v, Rf.v, kg[:, 0:1], None, ALU.mult)
                KTs = [self.T(st, [96, S], BF16, 'KT') for _ in range(2)]
                VAs = [self.T(st, [128, 64, 128], BF16, 'VA') for _ in range(2)]
                for v_ in VAs:
                    self.memset('pool', v_.v, 1.0)
                kp = [self.T(st, [96, 512], F32) for _ in range(2)]
                cs = [self.T(st, [96, 512], F32) for _ in range(2)]
                sn = [self.T(st, [96, 512], F32) for _ in range(2)]
                def mk_tl():
                    return dict(sq=self.T(st, [96, 512], BF16), xb=self.T(st, [96, 512], BF16),
                                rs=self.T(st, [96, 512], F32), t1=self.T(st, [96, 512], F32),
                                t2=self.T(st, [96, 512], F32))
                tlq, tlk = mk_tl(), mk_tl()
                Qf = [self.T(st, [96, 512], BF16) for _ in range(2)]
                qpre = self.T(st, [96, 512], F32)
                Pt = [self.T(st, [128, 1024], BF16) for _ in range(3)]
                rsum = [self.T(st, [64, 512], F32) for _ in range(1)]
                oh = [self.T(st, [64, 512], BF16) for _ in range(2)]
                self.nbn = 2
                self.nbo = 4
                cnt = {'ci': 0, 'npt': 0}

                def nr_stages(tl_, x_pre, c_t, s_t, g, Rg, bias_ap, scale, out):
                    hold = {}

                    def s_a():
                        self.tt('dve', tl_['sq'].v, x_pre.v, x_pre.v, ALU.mult)
                        self.copy('dve', tl_['xb'].v, x_pre.v)

                    def s_b():
                        hold['pss'] = self.nb()
                        self.mm(hold['pss'][0:96, :], self.ones_bf[0:96, 0:96], tl_['sq'].v)

                    def s_c():
                        self.act(tl_['rs'].v, hold['pss'][0:96, :], AF.Ln, bias=bias_ap, scale=scale)
                        self.act(tl_['rs'].v, tl_['rs'].v, AF.Exp, scale=-0.5)

                    def s_d():
                        hold['prot'] = self.nb()
                        self.mm(hold['prot'][0:96, :], Rg.v, tl_['xb'].v)

                    def s_e():
                        self.stt('dve', tl_['t1'].v, x_pre.v, g[:, 0:1], c_t.v, ALU.mult, ALU.mult)
                        self.tt('dve', tl_['t2'].v, hold['prot'][0:96, :], s_t.v, ALU.mult)

                    def s_f():
                        self.tt('dve', tl_['t1'].v, tl_['t1'].v, tl_['t2'].v, ALU.add)
                        self.tt('pool', out, tl_['t1'].v, tl_['rs'].v, ALU.mult)
                    return [s_a, s_b, s_c, s_d, s_e, s_f]

                def kgen_stages(h, ti):
                    KT, VA = KTs[h % 2], VAs[h % 2]
                    t0 = ti * 512
                    tc_ = slice(t0, t0 + 512)
                    ci = cnt['ci']
                    cnt['ci'] += 1
                    kpre, c_t, s_t = kp[ci % 2], cs[ci % 2], sn[ci % 2]
                    hold = {}

                    def k1():
                        hold['pk'] = self.nb()
                        for j in range(2):
                            self.mm(hold['pk'][0:64, :], wukv[:, j, h * 128:h * 128 + 64], ckvn[j][:, tc_],
                                    start=(j == 0), stop=(j == 1))
                        self.dma('sp', kpre[64:96, :], kr_s[:, tc_])
                        self.dma('sp', c_t.v, cosd[:, tc_])
                        self.dma('sp', s_t.v, sind[:, tc_])

                    def k2():
                        self.copy('dve', kpre[0:64, :], hold['pk'][0:64, :])

                    def k9():
                        hold['pv'] = self.nb()
                        for b in range(4):
                            for j in range(2):
                                self.mm(hold['pv'][:, b * 64:(b + 1) * 64], ckvn[j][:, t0 + b * 128:t0 + (b + 1) * 128],
                                        wukv[:, j, h * 128 + 64:h * 128 + 128], start=(j == 0), stop=(j == 1))

                    def k10():
                        self.copy('dve', VA[:, ti * 4:(ti + 1) * 4, 0:64], hold['pv'][:, 0:256].re("p (b v) -> p b v", b=4))
                    return [k1, k2] + nr_stages(tlk, kpre, c_t, s_t, kg, Rk, self.epsT[0:96, :], 1.0 / 96.0, KT[:, tc_]) + [k9, k10]

                def qgen_stages(h, qi):
                    qc_ = slice(qi * 512, qi * 512 + 512)
                    ci = cnt['ci']
                    cnt['ci'] += 1
                    c_t, s_t = cs[ci % 2], sn[ci % 2]
                    qf = Qf[(h * 16 + qi) % 2]
                    hold = {}

                    def q1():
                        hold['pq'] = self.nb()
                        for j in range(3):
                            self.mm(hold['pq'][0:96, :], wuq[:, j, h * 96:(h + 1) * 96], cqn[j][:, qc_], start=(j == 0), stop=(j == 2))
                        self.dma('sp', c_t.v, cosd[:, qc_])
                        self.dma('sp', s_t.v, sind[:, qc_])

                    def q2():
                        self.copy('dve', qpre.v, hold['pq'][0:96, :])
                    return [q1, q2] + nr_stages(tlq, qpre, c_t, s_t, qg, Rq, epsq.v, 1.0, qf.v)

                def attn(h, qi, pending):
                    KT, VA = KTs[h % 2], VAs[h % 2]
                    qc_ = slice(qi * 512, qi * 512 + 512)
                    qf = Qf[(h * 16 + qi) % 2]
                    po = ps[6 + qi % 2]
                    nkb = 4 * qi + 4
                    npairs = nkb // 2
                    pend = []
                    for pp in range(npairs):
                        pst = self.ps2[cnt['npt'] % 2]
                        P = Pt[cnt['npt'] % 3]
                        cnt['npt'] += 1
                        c0s = []
                        for hf in range(2):
                            kb = 2 * pp + hf
                            kl = kb - 4 * qi
                            c0s.append(128 * kl if kl > 0 else 0)
                        for hf in range(2):
                            kb = 2 * pp + hf
                            c0 = c0s[hf]
                            self.mm(pst[:, hf * 512 + c0:(hf + 1) * 512], KT[:, kb * 128:(kb + 1) * 128], qf[:, c0:512])
                        if c0s[0] == 0 and c0s[1] == 0:
                            self.act(P.v, pst.v, AF.Exp)
                        else:
                            for hf in range(2):
                                self.act(P[:, hf * 512 + c0s[hf]:(hf + 1) * 512], pst[:, hf * 512 + c0s[hf]:(hf + 1) * 512], AF.Exp)
                        for hf in range(2):
                            kb = 2 * pp + hf
                            kl = kb - 4 * qi
                            if kl >= 0:
                                self.memset('pool', P[64:128, hf * 512 + 128 * kl:hf * 512 + 128 * kl + 64], 0.0)
                        pend.append((pp, P, c0s))
                        if len(pend) > 2:
                            pp_, P_, cs_ = pend.pop(0)
                            for hf in range(2):
                                kb_ = 2 * pp_ + hf
                                self.mm(po[:, cs_[hf]:512], VA[:, kb_, :], P_[:, hf * 512 + cs_[hf]:(hf + 1) * 512],
                                        start=(kb_ == 0), stop=False)
                        left = npairs - pp
                        nst = -(-len(pending) // left)
                        for _ in range(nst):
                            if pending:
                                pending.pop(0)()
                    for (pp_, P_, cs_) in pend:
                        for hf in range(2):
                            kb_ = 2 * pp_ + hf
                            self.mm(po[:, cs_[hf]:512], VA[:, kb_, :], P_[:, hf * 512 + cs_[hf]:(hf + 1) * 512],
                                    start=(kb_ == 0), stop=(kb_ == nkb - 1))
                    while pending:
                        pending.pop(0)()
                    rs_, oh_ = rsum[0], oh[qi % 2]
                    self.act(rs_.v, po[64:128, :], AF.Ln)
                    self.act(rs_.v, rs_.v, AF.Exp, scale=-1.0)
                    self.tt('dve', oh_.v, po[0:64, :], rs_.v, ALU.mult)
                    self.dma('sp', o_s[h * 64:(h + 1) * 64, qc_], oh_.v)

                for ti in range(16):
                    for f_ in kgen_stages(0, ti):
                        f_()
                for f_ in qgen_stages(0, 0):
                    f_()
                for h in range(16):
                    for qi in range(16):
                        pending = []
                        qs_ = ks_ = []
                        if qi + 1 < 16:
                            qs_ = qgen_stages(h, qi + 1)
                        elif h + 1 < 16:
                            qs_ = qgen_stages(h + 1, 0)
                        if h + 1 < 16:
                            ks_ = kgen_stages(h + 1, qi)
                        qs_, ks_ = list(qs_), list(ks_)
                        while qs_ or ks_:
                            if qs_:
                                pending.append(qs_.pop(0))
                            if ks_:
                                pending.append(ks_.pop(0))
                        attn(h, qi, pending)
                self.nbn = 4
                self.nbo = 0
        self.p.barrier()
        with ExitStack() as st:
            wo = self.load_w(st, 'mla_w_o', 8, 1024)
            xts = [self.T(st, [128, 8, 512], F32) for _ in range(2)]
            ots = [self.T(st, [128, 8, 512], BF16) for _ in range(2)]
            ot = [self.T(st, [128, 512], F32) for _ in range(2)]
            xv = xin.rearrange("(c p) t -> p c t", p=128)
            ov = o_s.rearrange("(c p) t -> p c t", p=128)
            for ti in range(S // 512):
                t0 = ti * 512
                xt, o_t = xts[ti % 2], ots[ti % 2]
                self.dma('sp', xt.v, xv[:, :, t0:t0 + 512])
                self.dma('sp', o_t.v, ov[:, :, t0:t0 + 512])
                self.out_proj(wo, [o_t[:, j, :] for j in range(8)], xt, ot, xout, t0)


def _colmajor(v, n):
    return np.ascontiguousarray(np.asarray(v, np.float32).reshape(n // 128, 128).T)


def _host_inputs(inputs, layers):
    m = {}
    m["norm_mix"] = np.ascontiguousarray(
        np.asarray(inputs["norm_mix"], np.float32).reshape(4, 8, 128).transpose(2, 0, 1).reshape(128, 32))
    m["norm_ffn"] = np.ascontiguousarray(
        np.asarray(inputs["norm_ffn"], np.float32).reshape(4, 8, 128).transpose(2, 0, 1).reshape(128, 32))
    for L in layers:
        for n in LAYER_W[L]:
            m[n] = np.ascontiguousarray(np.asarray(inputs[n], np.float32)[0])
        for n, ln in LAYER_V[L]:
            m[n] = _colmajor(inputs[n][0], ln)
        m["ffn_w1_%d" % L] = np.ascontiguousarray(np.asarray(inputs["ffn_w1"], np.float32)[L])
        m["ffn_w2_%d" % L] = np.ascontiguousarray(np.asarray(inputs["ffn_w2"], np.float32)[L])
    idx = np.arange(128)
    same = (idx[:, None] // 64) == (idx[None, :] // 64)
    if 1 in layers or 2 in layers:
        m["bcmask"] = (same & (idx[:, None] <= idx[None, :])).astype(np.float32)
    if 0 in layers:
        m["mla_qg"] = np.ascontiguousarray(np.repeat(np.asarray(inputs["mla_q_gain"], np.float32)[0].reshape(96, 1), 16, axis=1))
        m["mla_kg"] = np.ascontiguousarray(np.repeat(np.asarray(inputs["mla_k_gain"], np.float32)[0].reshape(96, 1), 16, axis=1))
        inv = (10000.0 ** (-np.arange(16, dtype=np.float32) / 16.0)).astype(np.float32)
        ang = np.arange(S, dtype=np.float32)[None, :] * inv[:, None]
        cos = np.ones((96, S), np.float32)
        sin = np.zeros((96, S), np.float32)
        cos[64:80] = np.cos(ang)
        cos[80:96] = np.cos(ang)
        sin[64:80] = np.sin(ang)
        sin[80:96] = np.sin(ang)
        m["rope_cos"] = cos
        m["rope_sin"] = sin
        R = np.zeros((96, 96), np.float32)
        for i in range(16):
            R[80 + i, 64 + i] = -1.0
            R[64 + i, 80 + i] = 1.0
        m["rope_R"] = R
    if 1 in layers:
        b = np.asarray(inputs["mlstm_b_if"], np.float32)[0]
        m["mlstm_b_if"] = np.ascontiguousarray(np.tile(np.stack([b[0:4], b[4:8]], axis=1), (1, 8)))
        sel = np.zeros((4, 512), np.float32)
        for h in range(4):
            sel[h, h * 128:(h + 1) * 128] = 1.0
        m["sel4"] = sel
        m["lmat"] = (((idx[:, None] // 32) == (idx[None, :] // 32)) & (idx[:, None] < idx[None, :])).astype(np.float32)
        m["ident"] = np.eye(128, dtype=np.float32)
        m["mneg"] = np.where(((idx[:, None] // 32) == (idx[None, :] // 32)) & (idx[None, :] < idx[:, None]), 0.0, -1e30).astype(np.float32)
    if 2 in layers:
        m["gla_w_a2b"] = np.ascontiguousarray(np.concatenate(
            [np.asarray(inputs["gla_w_a2"], np.float32)[0], np.asarray(inputs["gla_b_a"], np.float32)[0][None, :]], axis=0))
        m["triN"] = (same & (idx[:, None] <= idx[None, :])).astype(np.float32) * (-1.0 / 16.0)
        m["triU"] = (same & (idx[:, None] > idx[None, :])).astype(np.float32) * (-1.0 / 16.0)
    if 3 in layers:
        w = np.asarray(inputs["conv_w_dw"], np.float32)[0]
        m["conv_w_dw"] = np.ascontiguousarray(w.reshape(31, 8, 128).transpose(2, 1, 0).reshape(128, 8 * 31))
        m["identc"] = np.eye(128, dtype=np.float32)
    return m


_NC_CACHE = {}


def run_layers(layers, xT_list, inputs, skip_ffn=False):
    key = (tuple(layers), skip_ffn)
    if key not in _NC_CACHE:
        _NC_CACHE[key] = KB(list(layers), skip_ffn).build()
    nc = _NC_CACHE[key]
    shared = _host_inputs(inputs, layers)
    in_maps = []
    for xT in xT_list:
        mm = dict(shared)
        mm["xT"] = xT
        in_maps.append(mm)
    res = run_bass_kernel_spmd(nc, in_maps, core_ids=list(range(len(xT_list))))
    return [r["yT"] for r in res.results]


def kernel(**inputs):
    x = np.asarray(inputs["x"], np.float32)
    B = x.shape[0]
    xT = [np.ascontiguousarray(x[b % B].T) for b in range(8)]
    outs = run_layers((0, 1, 2, 3), xT, inputs)
    y = np.stack([np.ascontiguousarray(outs[b].T) for b in range(B)], axis=0)
    return y.astype(np.float32)
```
